# Optimizing a Trainium2 kernel written in Bass

```python
import math
import jax
import jax.numpy as jnp
from jax import lax
import numpy as np

D_MODEL = 1024
BATCH = 8
SEQ = 2048
DEPTH = 2
DEC_BATCH = 32
DEC_SEQ = 4
PAST_LEN = 8192
PAGE_SIZE = 128

MIX_W = D_MODEL
GDN_H = 4
GDN_DK = MIX_W // (2 * GDN_H)
GDN_DV = GDN_DK
GDN_W = GDN_H * GDN_DV
GDN_CONV = 4
GDN_CHUNK = 64
DIFF_H = 4
DIFF_DH = MIX_W // (4 * DIFF_H)
DIFF_DV = 2 * DIFF_DH
DIFF_W = DIFF_H * DIFF_DV
IN_W = 4 * GDN_W + 2 * GDN_H + 3 * DIFF_W
SPLITS = (3 * GDN_W, 4 * GDN_W, 4 * GDN_W + GDN_H, 4 * GDN_W + 2 * GDN_H,
          4 * GDN_W + 2 * GDN_H + DIFF_W, 4 * GDN_W + 2 * GDN_H + 2 * DIFF_W)
ROPE_THETA = 10000.0
ATTN_BLOCK = 128
MEM_LEN = 256
CA_H = 4
CA_DH = D_MODEL // 8
CA_W = CA_H * CA_DH
D_FF = ((8 * D_MODEL // 3 + 127) // 128) * 128
FFN_CONV = 3
EPS = 1e-6

kernel_name = 'hybrid_gdn_diffattn_memxattn_convffn_step'


def rms_norm(x, gain):
    x32 = x.astype(jnp.float32)
    y = x32 * lax.rsqrt(jnp.mean(x32 * x32, axis=-1, keepdims=True) + EPS)
    return (y * gain.astype(jnp.float32)).astype(x.dtype)


def l2_norm(x):
    return x * lax.rsqrt(jnp.sum(x * x, axis=-1, keepdims=True) + EPS)


def rotary(x, pos):
    half = x.shape[-1] // 2
    inv = ROPE_THETA ** (-jnp.arange(half, dtype=jnp.float32) / half)
    ang = pos.astype(jnp.float32)[:, None] * inv[None, :]
    cos = jnp.cos(ang)[None, :, None, :]
    sin = jnp.sin(ang)[None, :, None, :]
    x32 = x.astype(jnp.float32)
    x1, x2 = x32[..., :half], x32[..., half:]
    return jnp.concatenate([x1 * cos - x2 * sin, x2 * cos + x1 * sin], axis=-1).astype(x.dtype)


def causal_dwconv(x, buf, w):
    width, L = w.shape[0], x.shape[1]
    xp = jnp.concatenate([buf.astype(x.dtype), x], axis=1)
    w = w.astype(x.dtype)
    y = xp[:, 0:L] * w[0]
    for j in range(1, width):
        y = y + xp[:, j:j + L] * w[j]
    return y, xp[:, L:]


def gated_delta_rule(q, k, v, g, beta, s0):
    B, L, H, dk = q.shape
    dv = v.shape[-1]
    C = math.gcd(L, GDN_CHUNK)
    N = L // C

    def chunks(t):
        t = t.reshape((B, N, C, H) + t.shape[3:])
        return jnp.moveaxis(t, (1, 3), (0, 2))

    qc = chunks(q) * (dk ** -0.5)
    kc = chunks(k)
    vc = chunks(v)
    bc = chunks(beta)
    gc = jnp.cumsum(chunks(g), axis=-1)
    idx = jnp.arange(C)
    lower = idx[:, None] >= idx[None, :]
    strict = idx[:, None] > idx[None, :]
    decay = jnp.exp(jnp.where(lower, gc[..., :, None] - gc[..., None, :], -jnp.inf))
    kb = kc * bc[..., None]
    m = jnp.where(strict, jnp.einsum('nbhid,nbhjd->nbhij', kb, kc) * decay, 0.0)
    eye = jnp.eye(C, dtype=jnp.float32)
    t = lax.linalg.triangular_solve(eye + m, jnp.broadcast_to(eye, m.shape),
                                    left_side=True, lower=True, unit_diagonal=True)
    u = t @ (vc * bc[..., None])
    w = t @ (kb * jnp.exp(gc)[..., None])
    aqk = jnp.einsum('nbhid,nbhjd->nbhij', qc, kc) * decay
    qg = qc * jnp.exp(gc)[..., None]
    kg = kc * jnp.exp(gc[..., -1:] - gc)[..., None]
    gl = jnp.exp(gc[..., -1])

    def step(S, xs):
        u_i, w_i, aqk_i, qg_i, kg_i, gl_i = xs
        v_new = u_i - w_i @ S
        o = qg_i @ S + aqk_i @ v_new
        S = S * gl_i[..., None, None] + jnp.swapaxes(kg_i, -1, -2) @ v_new
        return S, o

    S, o = lax.scan(step, s0, (u, w, aqk, qg, kg, gl))
    o = jnp.moveaxis(o, (0, 2), (1, 3)).reshape(B, L, H, dv)
    return o, S


def diff_attention(q, k, v, q_pos, k_pos, lam):
    B, Lq, H2, dh = q.shape
    H, dv = v.shape[2], v.shape[3]
    Lk = k.shape[1]
    qb = ATTN_BLOCK if Lq % ATTN_BLOCK == 0 else Lq
    nb = Lq // qb
    q_blocks = jnp.moveaxis(q.reshape(B, nb, qb, H2, dh), 1, 0)
    pos_blocks = q_pos.reshape(nb, qb)
    scale = dh ** -0.5

    def block(args):
        qi, pi = args
        s = jnp.einsum('bqhd,bkhd->bhqk', qi, k).astype(jnp.float32) * scale
        s = jnp.where(k_pos[None, None, None, :] <= pi[None, None, :, None], s, -jnp.inf)
        p = jax.nn.softmax(s, axis=-1).reshape(B, H, 2, qb, Lk)
        a = p[:, :, 0] - lam * p[:, :, 1]
        return jnp.einsum('bhqk,bkhe->bqhe', a.astype(v.dtype), v)

    out = lax.map(block, (q_blocks, pos_blocks))
    return jnp.moveaxis(out, 0, 1).reshape(B, Lq, H, dv)


def memory_attention(q, mk, mv):
    s = jnp.einsum('bqhd,bkhd->bhqk', q, mk).astype(jnp.float32) * (q.shape[-1] ** -0.5)
    p = jax.nn.softmax(s, axis=-1)
    return jnp.einsum('bhqk,bkhd->bqhd', p.astype(mv.dtype), mv)


def memory_kv(mem, lp):
    B, M, _ = mem.shape
    mn = rms_norm(mem, lp['norm_mem'])
    mk = rms_norm((mn @ lp['w_ck']).reshape(B, M, CA_H, CA_DH), lp['knorm_cross'])
    mv = (mn @ lp['w_cv']).reshape(B, M, CA_H, CA_DH)
    return mk, mv


def trunk_layer(x, lp, layer, pos, conv_buf, s0, past_k, past_v, mem_k, mem_v, ffn_buf):
    B, L, _ = x.shape
    f32 = jnp.float32
    h = rms_norm(x, lp['norm_mix'])
    z = h @ lp['w_in']
    qkv_raw, gate, b_raw, a_raw, dq, dk, dv = jnp.split(z, SPLITS, axis=-1)
    qkv, conv_new = causal_dwconv(qkv_raw, conv_buf, lp['conv_qkv'])
    qkv = jax.nn.silu(qkv.astype(f32))
    gq = l2_norm(qkv[..., :GDN_W].reshape(B, L, GDN_H, GDN_DK))
    gk = l2_norm(qkv[..., GDN_W:2 * GDN_W].reshape(B, L, GDN_H, GDN_DK))
    gv = qkv[..., 2 * GDN_W:].reshape(B, L, GDN_H, GDN_DV)
    beta = jax.nn.sigmoid(b_raw.astype(f32))
    g = -jnp.exp(lp['a_log'].astype(f32)) * jax.nn.softplus(a_raw.astype(f32) + lp['dt_bias'].astype(f32))
    o_gdn, s_new = gated_delta_rule(gq, gk, gv, g, beta, s0.astype(f32))
    o_gdn = rms_norm(o_gdn, lp['gdn_norm']) * jax.nn.silu(gate.astype(f32).reshape(B, L, GDN_H, GDN_DV))
    o_gdn = o_gdn.reshape(B, L, GDN_W).astype(x.dtype)
    qd = rotary(rms_norm(dq.reshape(B, L, 2 * DIFF_H, DIFF_DH), lp['qnorm_diff']), pos)
    kd = rotary(rms_norm(dk.reshape(B, L, 2 * DIFF_H, DIFF_DH), lp['knorm_diff']), pos)
    vd = dv.reshape(B, L, DIFF_H, DIFF_DV)
    if past_k is None:
        keys, vals, k_pos = kd, vd, pos
    else:
        keys = jnp.concatenate([past_k.astype(x.dtype), kd], axis=1)
        vals = jnp.concatenate([past_v.astype(x.dtype), vd], axis=1)
        k_pos = jnp.concatenate([jnp.arange(past_k.shape[1], dtype=pos.dtype), pos])
    lam_init = 0.8 - 0.6 * math.exp(-0.3 * layer)
    lam = (jnp.exp(jnp.sum(lp['lam_q1'].astype(f32) * lp['lam_k1'].astype(f32)))
           - jnp.exp(jnp.sum(lp['lam_q2'].astype(f32) * lp['lam_k2'].astype(f32))) + lam_init)
    o_diff = diff_attention(qd, keys, vals, pos, k_pos, lam)
    o_diff = (rms_norm(o_diff, lp['diff_norm']) * (1.0 - lam_init)).reshape(B, L, DIFF_W)
    x = x + jnp.concatenate([o_gdn, o_diff.astype(x.dtype)], axis=-1) @ lp['w_out']
    hc = rms_norm(x, lp['norm_cross'])
    qc = rms_norm((hc @ lp['w_cq']).reshape(B, L, CA_H, CA_DH), lp['qnorm_cross'])
    oc = memory_attention(qc, mem_k.astype(x.dtype), mem_v.astype(x.dtype))
    x = x + oc.reshape(B, L, CA_W) @ lp['w_co']
    hf = rms_norm(x, lp['norm_ffn'])
    gt, ffn_new = causal_dwconv(hf @ lp['w_gate'], ffn_buf, lp['conv_ffn'])
    x = x + (jax.nn.silu(gt) * (hf @ lp['w_up'])) @ lp['w_down']
    return x, kd, vd, s_new, conv_new, ffn_new


def setup_inputs(seed: int = 0) -> dict:
    key = jax.random.key(seed)
    ks = iter(jax.random.split(key, 64))
    f32 = jnp.float32

    def nrm(shape, scale=1.0):
        return jax.random.normal(next(ks), shape, f32) * scale

    def gain(n):
        return 1.0 + 0.02 * nrm((DEPTH, n))

    n_pages = PAST_LEN // PAGE_SIZE
    n_used = DEC_BATCH * n_pages
    n_pool = n_used + max(1, n_used // 4)
    page_table = jax.random.permutation(next(ks), n_pool)[:n_used].astype(jnp.int32).reshape(DEC_BATCH, n_pages)
    dt = jnp.exp(jax.random.uniform(next(ks), (DEPTH, GDN_H), f32, math.log(1e-3), math.log(1e-1)))
    dt_bias = dt + jnp.log(-jnp.expm1(-dt))
    a_log = jnp.log(jax.random.uniform(next(ks), (DEPTH, GDN_H), f32, 1.0, 16.0))
    return {
        'x_prompt': nrm((BATCH, SEQ, D_MODEL)),
        'x_sample': nrm((DEC_BATCH, DEC_SEQ, D_MODEL)),
        'mem_prompt': nrm((BATCH, MEM_LEN, D_MODEL)),
        'cache_k': nrm((DEPTH, n_pool, PAGE_SIZE, 2 * DIFF_H, DIFF_DH)),
        'cache_v': nrm((DEPTH, n_pool, PAGE_SIZE, DIFF_H, DIFF_DV)),
        'page_table': page_table,
        'state_gdn': nrm((DEPTH, DEC_BATCH, GDN_H, GDN_DK, GDN_DV), 0.3),
        'state_gdn_conv': nrm((DEPTH, DEC_BATCH, GDN_CONV - 1, 3 * GDN_W)),
        'cache_mem_k': nrm((DEPTH, DEC_BATCH, MEM_LEN, CA_H, CA_DH)),
        'cache_mem_v': nrm((DEPTH, DEC_BATCH, MEM_LEN, CA_H, CA_DH)),
        'state_ffn_conv': nrm((DEPTH, DEC_BATCH, FFN_CONV - 1, D_FF)),
        'norm_mix': gain(D_MODEL),
        'w_in': nrm((DEPTH, D_MODEL, IN_W), D_MODEL ** -0.5),
        'conv_qkv': nrm((DEPTH, GDN_CONV, 3 * GDN_W), GDN_CONV ** -0.5),
        'a_log': a_log,
        'dt_bias': dt_bias,
        'gdn_norm': gain(GDN_DV),
        'qnorm_diff': gain(DIFF_DH),
        'knorm_diff': gain(DIFF_DH),
        'lam_q1': nrm((DEPTH, DIFF_DH), 0.1),
        'lam_k1': nrm((DEPTH, DIFF_DH), 0.1),
        'lam_q2': nrm((DEPTH, DIFF_DH), 0.1),
        'lam_k2': nrm((DEPTH, DIFF_DH), 0.1),
        'diff_norm': gain(DIFF_DV),
        'w_out': nrm((DEPTH, MIX_W, D_MODEL), MIX_W ** -0.5),
        'norm_cross': gain(D_MODEL),
        'norm_mem': gain(D_MODEL),
        'w_cq': nrm((DEPTH, D_MODEL, CA_W), D_MODEL ** -0.5),
        'w_ck': nrm((DEPTH, D_MODEL, CA_W), D_MODEL ** -0.5),
        'w_cv': nrm((DEPTH, D_MODEL, CA_W), D_MODEL ** -0.5),
        'qnorm_cross': gain(CA_DH),
        'knorm_cross': gain(CA_DH),
        'w_co': nrm((DEPTH, CA_W, D_MODEL), CA_W ** -0.5),
        'norm_ffn': gain(D_MODEL),
        'w_gate': nrm((DEPTH, D_MODEL, D_FF), D_MODEL ** -0.5),
        'w_up': nrm((DEPTH, D_MODEL, D_FF), D_MODEL ** -0.5),
        'conv_ffn': nrm((DEPTH, FFN_CONV, D_FF), FFN_CONV ** -0.5),
        'w_down': nrm((DEPTH, D_FF, D_MODEL), D_FF ** -0.5),
    }


def reference(x_prompt, x_sample, mem_prompt, cache_k, cache_v, page_table, state_gdn, state_gdn_conv,
              cache_mem_k, cache_mem_v, state_ffn_conv, norm_mix, w_in, conv_qkv, a_log, dt_bias, gdn_norm,
              qnorm_diff, knorm_diff, lam_q1, lam_k1, lam_q2, lam_k2, diff_norm, w_out, norm_cross, norm_mem,
              w_cq, w_ck, w_cv, qnorm_cross, knorm_cross, w_co, norm_ffn, w_gate, w_up, conv_ffn, w_down):
    params = {
        'norm_mix': norm_mix, 'w_in': w_in, 'conv_qkv': conv_qkv, 'a_log': a_log, 'dt_bias': dt_bias,
        'gdn_norm': gdn_norm, 'qnorm_diff': qnorm_diff, 'knorm_diff': knorm_diff, 'lam_q1': lam_q1,
        'lam_k1': lam_k1, 'lam_q2': lam_q2, 'lam_k2': lam_k2, 'diff_norm': diff_norm, 'w_out': w_out,
        'norm_cross': norm_cross, 'norm_mem': norm_mem, 'w_cq': w_cq, 'w_ck': w_ck, 'w_cv': w_cv,
        'qnorm_cross': qnorm_cross, 'knorm_cross': knorm_cross, 'w_co': w_co, 'norm_ffn': norm_ffn,
        'w_gate': w_gate, 'w_up': w_up, 'conv_ffn': conv_ffn, 'w_down': w_down,
    }
    Bp, Bs = x_prompt.shape[0], x_sample.shape[0]
    past_len = page_table.shape[1] * PAGE_SIZE
    p_pos = jnp.arange(x_prompt.shape[1], dtype=jnp.int32)
    s_pos = past_len + jnp.arange(x_sample.shape[1], dtype=jnp.int32)
    xp, xs = x_prompt, x_sample
    p_out, s_out = [], []
    for l in range(DEPTH):
        lp = {name: arr[l] for name, arr in params.items()}
        mk_p, mv_p = memory_kv(mem_prompt, lp)
        xp, kp, vp, sp, cp, fp = trunk_layer(
            xp, lp, l, p_pos,
            jnp.zeros((Bp, GDN_CONV - 1, 3 * GDN_W), xp.dtype),
            jnp.zeros((Bp, GDN_H, GDN_DK, GDN_DV), jnp.float32),
            None, None, mk_p, mv_p,
            jnp.zeros((Bp, FFN_CONV - 1, D_FF), xp.dtype))
        p_out.append((kp, vp, sp, cp, mk_p, mv_p, fp))
        past_k = cache_k[l][page_table].reshape(Bs, past_len, 2 * DIFF_H, DIFF_DH)
        past_v = cache_v[l][page_table].reshape(Bs, past_len, DIFF_H, DIFF_DV)
        xs, ks_, vs_, ss, cs, fs = trunk_layer(
            xs, lp, l, s_pos, state_gdn_conv[l], state_gdn[l], past_k, past_v,
            cache_mem_k[l], cache_mem_v[l], state_ffn_conv[l])
        s_out.append((ks_, vs_, ss, cs, fs))
    pk, pv, pg, pc, pmk, pmv, pf = [jnp.stack(t, axis=0) for t in zip(*p_out)]
    sk, sv, sg, sc, sf = [jnp.stack(t, axis=0) for t in zip(*s_out)]
    return (xp, xs, pk, pv, pg, pc, pmk, pmv, pf, sk, sv, sg, sc, sf)
```

```python
import math
import numpy as np
import concourse.bass as bass
import concourse.mybir as mybir
from concourse.bass_utils import run_bass_kernel_spmd

F32 = mybir.dt.float32
BF16 = mybir.dt.bfloat16
I32 = mybir.dt.int32
AF = mybir.ActivationFunctionType
ALU = mybir.AluOpType

NCORES = 8
D = 1024
KC = 8
SEQ = 2048
TSEG = 512
NSEG = 4
TT = 4
NS = 4
SL = 4
TS = NS * SL
DEPTH = 2
NPAGES = 64
NPOOL = 2560
DFF = 2816
FC = 22
INW = 3592
EPS = 1e-6
BIG = 1.0e30
C_Q, C_K, C_V, C_G, C_BA, C_DQ, C_DK, C_DV = 0, 512, 1024, 1536, 2048, 2056, 2568, 3080


class Tl:
    def __init__(self, t, n=1, excl=False):
        self.t = t
        self.n = n
        self.excl = excl
        self.lw = [None] * n
        self.rd = [[] for _ in range(n)]

    def __getitem__(self, k):
        return self.t[k]


class Al:
    def __init__(self, t, parents):
        self.t = t
        self.parents = parents
        self.n = 1

    def __getitem__(self, k):
        return self.t[k]


class Sched:
    ENG = ("pe", "act", "dve", "pool", "sp")

    def __init__(self, nc, ndma=40):
        self.nc = nc
        self.eng = {"pe": nc.tensor, "act": nc.scalar, "dve": nc.vector, "pool": nc.gpsimd, "sp": nc.sync}
        self.sem = {}
        self.cnt = {e: 0 for e in self.ENG}
        self.known = {e: {} for e in self.ENG}
        self.prog = {e: [] for e in self.ENG}
        self.dsem = []
        self.dcnt = []
        self.dlast = []
        self.dnext = 0
        self.ndma = ndma
        self.outdeps = []
        self.ninstr = 0
        self.dry = False

    def open(self, stack):
        for e in self.ENG:
            self.sem[e] = stack.enter_context(self.nc.semaphore("s_" + e))
        for i in range(self.ndma):
            self.dsem.append(stack.enter_context(self.nc.semaphore("d%d" % i)))
            self.dcnt.append(0)
            self.dlast.append(None)

    @staticmethod
    def _regs(lst):
        out = []
        for r in lst:
            if isinstance(r, Al):
                for p in r.parents:
                    out.extend((p, i) for i in range(p.n))
            elif isinstance(r, Tl):
                out.extend((r, i) for i in range(r.n))
            else:
                t, i = r
                if isinstance(t, Al):
                    for p in t.parents:
                        out.extend((p, i2) for i2 in range(p.n))
                elif isinstance(i, (list, tuple, range)):
                    out.extend((t, j) for j in i)
                else:
                    out.append((t, i))
        return out

    def _deps(self, reads, writes):
        deps = []
        for t, i in reads:
            if t.lw[i] is not None:
                deps.append(t.lw[i])
        for t, i in writes:
            if t.lw[i] is not None:
                deps.append(t.lw[i])
            deps.extend(t.rd[i])
        return deps

    def _waits(self, e, deps):
        need = {}
        for s, v in deps:
            if self.known[e].get(id(s), (None, 0))[1] < v:
                if need.get(id(s), (None, 0))[1] < v:
                    need[id(s)] = (s, v)
        for k, (s, v) in need.items():
            self.known[e][k] = (s, v)
        return list(need.values())

    def _mark(self, reads, writes, dep):
        for t, i in writes:
            t.lw[i] = dep
            t.rd[i] = []
        for t, i in reads:
            t.rd[i].append(dep)

    def op(self, e, fn, reads=(), writes=()):
        if self.dry:
            return
        reads = self._regs(reads)
        writes = self._regs(writes)
        writes = writes + [r for r in reads if r[0].excl]
        reads = [r for r in reads if not r[0].excl]
        waits = self._waits(e, self._deps(reads, writes))
        self.cnt[e] += 1
        c = self.cnt[e]
        sem = self.sem[e]
        engobj = self.eng[e]

        def thunk():
            for s, v in waits:
                engobj.wait_ge(s, v)
            ins = fn(engobj)
            ins.then_inc(sem, 1)
        thunk()
        self._mark(reads, writes, (sem, c))
        self.ninstr += 1

    def dma(self, q, out_ap, in_ap, reads=(), writes=(), is_output=False, indirect=None):
        if self.dry:
            return
        reads = self._regs(reads)
        writes = self._regs(writes)
        deps = self._deps(reads, writes)
        k = self.dnext
        self.dnext = (self.dnext + 1) % self.ndma
        if self.dlast[k] is not None:
            deps.append(self.dlast[k])
        waits = self._waits(q, deps)
        self.dcnt[k] += 16
        s = self.dsem[k]
        v = self.dcnt[k]
        engobj = self.eng[q]

        def thunk():
            for ws, wv in waits:
                engobj.wait_ge(ws, wv)
            if indirect is None:
                ins = engobj.dma_start(out=out_ap, in_=in_ap)
            else:
                ins = engobj.indirect_dma_start(out=out_ap, out_offset=None, in_=in_ap,
                                                in_offset=bass.IndirectOffsetOnAxis(indirect, 0))
            ins.then_inc(s, 16)
        thunk()
        dep = (s, v)
        self.dlast[k] = dep
        self._mark(reads, writes, dep)
        if is_output:
            self.outdeps.append(dep)
        self.ninstr += 1

    def finish(self):
        alld = list(self.outdeps) + [d for d in self.dlast if d is not None] + [(self.sem[e], self.cnt[e]) for e in self.ENG if self.cnt[e] > 0]
        waits = self._waits("sp", alld)
        engobj = self.eng["sp"]

        def thunk():
            for s, v in waits:
                engobj.wait_ge(s, v)
        thunk()

    def emit(self, block):
        progs = self.prog

        @block.tensor
        def _(e):
            for th in progs["pe"]:
                th()

        @block.scalar
        def _(e):
            for th in progs["act"]:
                th()

        @block.vector
        def _(e):
            for th in progs["dve"]:
                th()

        @block.gpsimd
        def _(e):
            for th in progs["pool"]:
                th()

        @block.sync
        def _(e):
            for th in progs["sp"]:
                th()


DBG = {"segs": NSEG, "layers": DEPTH, "phase": 99, "small_cache": False, "sub": 99}


class _Stop(Exception):
    pass


def build_program(stage=99):
    from contextlib import ExitStack
    nc = bass.Bass("TRN2", target_bir_lowering=False)
    S = Sched(nc)

    def din(name, shape, dt=F32):
        return nc.dram_tensor(name, list(shape), dt, kind="ExternalInput").ap()

    def dout(name, shape, dt=F32):
        return nc.dram_tensor(name, list(shape), dt, kind="ExternalOutput").ap()

    xT_in = din("xT_in", [D, SEQ])
    xsT_in = din("xsT_in", [D, TS])
    memT_in = din("memT_in", [D, 256])
    CR = 128 if DBG["small_cache"] else DEPTH * NPOOL * 128
    cache_k = din("cache_k", [CR, 512])
    cache_v = din("cache_v", [CR, 512])
    ptab = din("ptab", [1, NS * NPAGES], I32)
    st_gdn = din("st_gdn", [DEPTH, NS, 4, 128, 128])
    st_gconvT = din("st_gconvT", [DEPTH, 1536, NS, 3])
    cmkT = din("cmkT", [DEPTH, NS, 4, 128, 256])
    cmv = din("cmv", [DEPTH, NS, 256, 512])
    st_fconvT = din("st_fconvT", [DEPTH, DFF, NS, 2])
    w_in = din("w_in", [DEPTH, D, INW])
    w_out = din("w_out", [DEPTH, D, D])
    w_cq = din("w_cq", [DEPTH, D, 512])
    w_ck = din("w_ck", [DEPTH, D, 512])
    w_cv = din("w_cv", [DEPTH, D, 512])
    w_co = din("w_co", [DEPTH, 512, D])
    w_gate = din("w_gate", [DEPTH, D, DFF])
    w_up = din("w_up", [DEPTH, D, DFF])
    w_down = din("w_down", [DEPTH, DFF, D])
    NPS = 8 * 4 + 12 * 4 + 22 * 3 + 4 + 4 + 7
    par_in = din("par_in", [128, DEPTH * NPS])
    lam_in = din("lam_in", [1, DEPTH * 4 * 64])
    NCON = 128 * 11
    con_in = din("con_in", [128, NCON])
    cs_in = din("cs_in", [128, 2, SEQ + SL])
    smask_in = din("smask_in", [128, 32])

    yT_o = dout("yT_o", [D, SEQ])
    ysT_o = dout("ysT_o", [D, TS])
    pkT_o = dout("pkT_o", [DEPTH, 512, SEQ])
    pv_o = dout("pv_o", [DEPTH, SEQ, 512])
    pg_o = dout("pg_o", [DEPTH, 4, 128, 128])
    pgcT_o = dout("pgcT_o", [DEPTH, 1536, 3])
    pmkT_o = dout("pmkT_o", [DEPTH, 512, 256])
    pmv_o = dout("pmv_o", [DEPTH, 256, 512])
    pfcT_o = dout("pfcT_o", [DEPTH, DFF, 2])
    skT_o = dout("skT_o", [DEPTH, 512, TS])
    sv_o = dout("sv_o", [DEPTH, NS, SL, 512])
    sg_o = dout("sg_o", [DEPTH, NS, 4, 128, 128])
    sgcT_o = dout("sgcT_o", [DEPTH, 1536, NS, 3])
    sfcT_o = dout("sfcT_o", [DEPTH, DFF, NS, 2])

    es = ExitStack()
    with es:
        S.open(es)
        _uid = [0]

        def sb(shape, dt=F32, n=1, name=None):
            _uid[0] += 1
            return Tl(nc.alloc_sbuf_tensor(name or ("t%d" % _uid[0]), list(shape), dt), n)

        TM = TSEG + TS
        xT = sb([128, KC, TM], F32, n=KC, name="xT")
        hT = sb([128, KC, TM], BF16, n=KC, name="hT")
        NSLOT = 3
        wsl = [sb([128, 8, 512], BF16, name="wsl%d" % i) for i in range(NSLOT)]
        par = sb([128, DEPTH * NPS], F32, name="par")
        con = sb([128, NCON], F32, name="con")
        conb = sb([128, NCON], BF16, name="conb")
        lam_t = sb([1, DEPTH * 4 * 64], F32, name="lam_t")
        lamv = sb([128, 2 * DEPTH], F32, name="lamv")
        smask = sb([128, 32], F32, name="smask")
        idx_t = sb([128, NS * NPAGES], I32, name="idx_t")
        kd_st = [sb([128, 4, (NSEG - 1) * TSEG], BF16, name="kdst%d" % l) for l in range(DEPTH)]
        v_st = [sb([128, (NSEG - 1) * TT, 512], BF16, name="vst%d" % l) for l in range(DEPTH)]
        S32 = [sb([128, 4, 128], F32, name="S32_%d" % l) for l in range(DEPTH)]
        gtail = [sb([128, 12, 3], F32, name="gtail%d" % l) for l in range(DEPTH)]
        ftail = [sb([128, FC, 2], F32, name="ftail%d" % l) for l in range(DEPTH)]
        PS = [Tl(nc.alloc_psum_tensor("ps%d" % i, [128, 512], F32), excl=True) for i in range(8)]

        def cf(k, rows=128, cols=128):
            return con[0:rows, k * 128:k * 128 + cols]

        def cb(k, rows=128, cols=128):
            return conb[0:rows, k * 128:k * 128 + cols]
        C_ID, C_ONE, C_UTRI, C_MPOS, C_STRICT, C_TRI, C_ROT, C_BLK, C_MISC, C_O128, C_O1024 = range(11)

        S.dma("sp", con[:, :], con_in[:, :], writes=[con])
        S.dma("sp", par[:, :], par_in[:, :], writes=[par])
        S.dma("sp", lam_t[:, :], lam_in[:, :], writes=[lam_t])
        S.dma("sp", smask[:, :], smask_in[:, :], writes=[smask])
        S.op("dve", lambda e: e.tensor_copy(out=conb[:, :], in_=con[:, :]), reads=[con], writes=[conb])

        def pcol(l, off, n=1):
            return par[:, l * NPS + off:l * NPS + off + n]
        P_NMIX, P_NCROSS, P_NMEM, P_NFFN = 0, 8, 16, 24
        P_CQKV = 32
        P_CFFN = 32 + 48
        P_ALOG = P_CFFN + 66
        P_DTB = P_ALOG + 4
        P_GDNN, P_QND, P_KND, P_DIFFN, P_QNC, P_KNC = [P_DTB + 4 + i for i in range(6)]
        P_SP = P_DTB + 4 + 6

        negA = sb([128, DEPTH * 4], F32, name="negA")
        for l in range(DEPTH):
            S.op("act", lambda e, l=l: e.activation(out=negA[:, 4 * l:4 * l + 4], in_=pcol(l, P_ALOG, 4), func=AF.Exp),
                 reads=[par], writes=[negA])
        S.op("dve", lambda e: e.tensor_scalar(out=negA[:, :], in0=negA[:, :], scalar1=-1.0, scalar2=None, op0=ALU.mult),
             reads=[negA], writes=[negA])
        lprod = sb([1, DEPTH * 2 * 64], F32, name="lprod")
        lsum = sb([1, DEPTH * 2], F32, name="lsum")
        lam1 = sb([1, DEPTH], F32, name="lam1")
        for l in range(DEPTH):
            for j in range(2):
                o = (l * 4 + 2 * j) * 64
                S.op("dve", lambda e, o=o, l=l, j=j: e.tensor_tensor(
                    out=lprod[0:1, (l * 2 + j) * 64:(l * 2 + j + 1) * 64], in0=lam_t[0:1, o:o + 64],
                    in1=lam_t[0:1, o + 64:o + 128], op=ALU.mult), reads=[lam_t], writes=[lprod])
                S.op("dve", lambda e, l=l, j=j: e.reduce_sum(
                    out=lsum[0:1, l * 2 + j:l * 2 + j + 1], in_=lprod[0:1, (l * 2 + j) * 64:(l * 2 + j + 1) * 64],
                    axis=mybir.AxisListType.X), reads=[lprod], writes=[lsum])
        S.op("act", lambda e: e.activation(out=lsum[0:1, :], in_=lsum[0:1, :], func=AF.Exp), reads=[lsum], writes=[lsum])
        for l in range(DEPTH):
            lam_init = 0.8 - 0.6 * math.exp(-0.3 * l)
            S.op("dve", lambda e, l=l, li=lam_init: e.scalar_tensor_tensor(
                out=lam1[0:1, l:l + 1], in0=lsum[0:1, 2 * l + 1:2 * l + 2], scalar=-li, in1=lsum[0:1, 2 * l:2 * l + 1],
                op0=ALU.add, op1=ALU.subtract), reads=[lsum], writes=[lam1])
        S.op("pe", lambda e: e.matmul(PS[7][0:128, 0:DEPTH], lhsT=con[0:1, 128:256], rhs=lam1[0:1, 0:DEPTH], start=True, stop=True),
             reads=[con, lam1], writes=[PS[7]])
        S.op("dve", lambda e: e.tensor_copy(out=lamv[:, 0:DEPTH], in_=PS[7][0:128, 0:DEPTH]), reads=[PS[7]], writes=[lamv])

        idx_f = sb([128, NS * NPAGES], F32, name="idx_f")
        idx_l = [sb([128, NS * NPAGES], I32, name="idxl%d" % l) for l in range(DEPTH)]
        S.dma("sp", idx_t[:, :], ptab[0:1, :].partition_broadcast(128), writes=[idx_t])
        S.op("dve", lambda e: e.tensor_copy(out=idx_f[:, :], in_=idx_t[:, :]), reads=[idx_t], writes=[idx_f])
        for l in range(DEPTH):
            S.op("dve", lambda e, l=l: e.tensor_scalar(
                out=idx_f[:, :] if False else idx_l[l][:, :], in0=idx_f[:, :], scalar1=128.0, scalar2=con[:, C_MISC * 128 + l:C_MISC * 128 + l + 1],
                op0=ALU.mult, op1=ALU.add), reads=[idx_f, con], writes=[idx_l[l]])

        AX = mybir.AxisListType.X
        wrot = [0]

        WD = {"w_in": w_in, "w_out": w_out, "w_cq": w_cq, "w_ck": w_ck, "w_cv": w_cv, "w_co": w_co,
              "w_gate": w_gate, "w_up": w_up, "w_down": w_down}
        wreq = []
        wst = {"pos": 0, "issued": {}, "l": 0}

        def w_issue(i):
            key, k0, nk, c0, ncol, _ = wreq[i]
            src2d = WD[key][wst["l"]]
            sl = wsl[wrot[0] % NSLOT]
            wrot[0] += 1
            S.dma("pool", sl[:, 0:nk, 0:ncol],
                  src2d[k0 * 128:(k0 + nk) * 128, c0:c0 + ncol].rearrange("(kc p) n -> p kc n", p=128),
                  writes=[sl])
            wst["issued"][i] = sl

        def wload(key, k0, nk, c0, ncol, nopf=False):
            if S.dry:
                wreq.append((key, k0, nk, c0, ncol, nopf))
                return wsl[0]
            i = wst["pos"]
            wst["pos"] += 1
            assert wreq[i][:5] == (key, k0, nk, c0, ncol), (wreq[i], key, k0, nk, c0, ncol)
            if i not in wst["issued"]:
                w_issue(i)
            sl = wst["issued"].pop(i)
            if not nopf and i + 1 < len(wreq):
                w_issue(i + 1)
            return sl

        def blocks_of(T):
            b = []
            t = 0
            while t < T:
                n = min(512, T - t)
                b.append((t, n))
                t += n
            return b

        def barrier():
            if S.dry:
                return
            deps = [(S.sem[e], S.cnt[e]) for e in S.ENG if S.cnt[e] > 0]
            deps += [d for d in S.dlast if d is not None]
            for e in S.ENG:
                waits = S._waits(e, deps)
                eo = S.eng[e]

                def th(waits=waits, eo=eo):
                    for s_, v_ in waits:
                        eo.wait_ge(s_, v_)
                th()

        scr = [sb([128, 512], F32, name="scr%d" % i) for i in range(6)]
        scrb = [sb([128, 512], BF16, name="scrb%d" % i) for i in range(4)]
        srot = [0]
        sbrot = [0]

        def nscr():
            srot[0] += 1
            return scr[srot[0] % len(scr)]

        def nscrb():
            sbrot[0] += 1
            return scrb[sbrot[0] % len(scrb)]
        psrot = [0]

        def nps(lo=0, hi=8):
            psrot[0] += 1
            return PS[lo + psrot[0] % (hi - lo)]

        def ACT(fn, reads, writes):
            S.op("act", fn, reads=reads, writes=writes)

        def DVE(fn, reads, writes):
            S.op("dve", fn, reads=reads, writes=writes)

        def POOL(fn, reads, writes):
            S.op("pool", fn, reads=reads, writes=writes)

        def PE(fn, reads, writes):
            S.op("pe", fn, reads=reads, writes=writes)

        def stats(srcs, n, ones_blk, psb):
            m = len(srcs)
            for i, (a, reg) in enumerate(srcs):
                sq = nscrb()
                ACT(lambda e, sq=sq, a=a: e.activation(out=sq[:, 0:n], in_=a, func=AF.Square), [reg], [sq])
                PE(lambda e, sq=sq, i=i: e.matmul(psb[:, 0:n], lhsT=cb(ones_blk), rhs=sq[:, 0:n], start=(i == 0), stop=(i == m - 1)),
                   [sq, conb], [psb])
            ta = nscr()
            tr = nscr()
            ACT(lambda e: e.activation(out=ta[:, 0:n], in_=psb[:, 0:n], func=AF.Sqrt, bias=con[:, C_MISC * 128 + 8:C_MISC * 128 + 9], scale=1.0),
                [psb, con], [ta])
            DVE(lambda e: e.reciprocal(out=tr[:, 0:n], in_=ta[:, 0:n]), [ta], [tr])
            return tr

        def rmsnorm_x(src, l, gcol, T, dst, nreg=True):
            for (t0, n) in blocks_of(T):
                tr = stats([(src[:, kc, t0:t0 + n], (src, kc)) for kc in range(KC)], n, C_O1024, nps(4, 6))
                for kc in range(KC):
                    DVE(lambda e, kc=kc, t0=t0, n=n, tr=tr: e.scalar_tensor_tensor(
                        out=dst[:, kc, t0:t0 + n], in0=src[:, kc, t0:t0 + n], scalar=pcol(l, gcol + kc),
                        in1=tr[:, 0:n], op0=ALU.mult, op1=ALU.mult),
                        [(src, kc), tr, par], [(dst, kc)])

        pjrot = [0]

        def proj_fm(slots, ccs, rhs_fn, rhs_regs, blks, handler, after=None, delay=2):
            nktot = sum(nk for _, nk in slots)
            pend = []

            def run(item):
                cc, t0, n, ps, last = item
                handler(cc, t0, n, ps)
                if last and after is not None:
                    after(cc)
            for cc in ccs:
                for bi_, (t0, n) in enumerate(blks):
                    pjrot[0] += 1
                    ps = PS[pjrot[0] % 4]

                    def mm(e, ps=ps, cc=cc, t0=t0, n=n):
                        ins = None
                        kk = 0
                        for sl, nk in slots:
                            for k in range(nk):
                                ins = e.matmul(ps[:, 0:n], lhsT=sl[:, k, cc * 128:(cc + 1) * 128], rhs=rhs_fn(kk, t0, n),
                                               start=(kk == 0), stop=(kk == nktot - 1))
                                kk += 1
                        return ins
                    PE(mm, [sl for sl, _ in slots] + rhs_regs, [ps])
                    pend.append((cc, t0, n, ps, bi_ == len(blks) - 1))
                    if len(pend) > delay:
                        run(pend.pop(0))
            while pend:
                run(pend.pop(0))

        BE = 4 * TM
        arena = nc.alloc_sbuf_tensor("arena", [128, 7 * BE], BF16)

        def bview(o):
            return arena[:, o:o + BE].rearrange("p (h t) -> p h t", h=4)
        B0 = Tl(bview(0), 4)
        B1 = Tl(bview(BE), 4)
        B2 = Tl(bview(2 * BE), 4)
        B3 = Tl(bview(3 * BE), 4)
        OF = Tl(arena[:, 4 * BE:6 * BE].bitcast(F32).rearrange("p (h t) -> p h t", h=4), 4)
        OD = Tl(bview(6 * BE), 4)
        OG = B0
        actT = Al(arena[:, 0:FC * TM].rearrange("p (f t) -> p f t", f=FC), [B0, B1, B2, B3, OF, OD])
        cst = Al(arena[:, 3 * BE:4 * BE].bitcast(F32).rearrange("p (c t) -> p c t", c=2), [B3])
        vloc = Al(arena[:, 2 * BE:2 * BE + 2048].rearrange("p (a f) -> p a f", a=4), [B2])
        memx = Al(arena[:, 0:4096].bitcast(F32).rearrange("p (k n) -> p k n", k=8), [B0, B1])
        mnT = Al(arena[:, 4 * BE:4 * BE + 2048].rearrange("p (k n) -> p k n", k=8), [OF])
        mkT = Al(arena[:, 4 * BE + 2048:4 * BE + 3072].rearrange("p (k n) -> p k n", k=4), [OF])
        mv = Al(arena[:, 4 * BE + 3072:4 * BE + 4096].rearrange("p (k n) -> p k n", k=2), [OF])
        vs_tok = sb([128, NS, 512], BF16, name="vs_tok")
        mkTs = sb([128, 4, 256], BF16, name="mkTs")
        mvs = sb([128, 2, 512], BF16, name="mvs")
        raw = [sb([128, 3 + TSEG], F32, name="raw%d" % i) for i in range(1)]
        sraw = [sb([128, NS, 8], F32, name="sraw%d" % i) for i in range(1)]
        acc = [sb([128, TM], F32, name="acc%d" % i) for i in range(1)]
        betaT = sb([128, TT + 1, 4], F32, name="betaT")
        gT = sb([128, TT + 1, 4], F32, name="gT")
        betaS = sb([128, NS, 4], F32, name="betaS")
        gS = sb([128, NS, 4], F32, name="gS")
        Ssm = sb([128, 4, 128], F32, name="Ssm")
        Sbf = sb([128, 4, 128], BF16, name="Sbf")
        kpage = [sb([128, 512], BF16, name="kpage%d" % i) for i in range(2)]
        vpage = [sb([128, 512], BF16, name="vpage%d" % i) for i in range(2)]
        KTp = [sb([128, 4, 128], BF16, name="KTp%d" % i) for i in range(2)]
        gA = [sb([128, 4, 128], F32, name="gA%d" % i) for i in range(6)]
        gB = [sb([128, 4, 128], BF16, name="gB%d" % i) for i in range(10)]
        gsm = [sb([128, 16], F32, name="gsm%d" % i) for i in range(8)]

        ISQ = 128.0 ** -0.5

        def gdn_step(C, I, qa, ka, va, regs_in, beta_ap, g_ap, Sst, oa, o_regs, q3=None, S3=None, o3=None, beta2=None, g2=None):
            IC = I * C
            gsc = gsm[0]; negb = gsm[1]; gcs = gsm[2]; eg = gsm[3]; bg = gsm[4]; egl = gsm[5]; glt = gsm[6]
            GU, TMPD, Dm, N_, NT, TT = gA[0:6]
            P2, PT2 = GU, TMPD
            AQ, AQT, QG, KBG, KG, VB, NW, VN, DS, TTb = gB[0:10]

            def v3(t, rows=C, w=C):
                return t[0:rows, 0:I, 0:w]

            def pv(ps, rows=C, w=C):
                return ps[0:rows, 0:I * w].rearrange("p (i c) -> p i c", i=I)
            if g2 is not None:
                DVE(lambda e: e.tensor_copy(out=gsc[0:C, 0:I], in_=g2), regs_in, [gsc])
                DVE(lambda e: e.tensor_scalar(out=negb[0:C, 0:I], in0=beta2, scalar1=-1.0, scalar2=None, op0=ALU.mult), regs_in, [negb])
            else:
                for it in range(I):
                    DVE(lambda e, it=it: e.tensor_copy(out=gsc[0:C, it:it + 1], in_=g_ap(it)), regs_in, [gsc])
                    DVE(lambda e, it=it: e.tensor_scalar(out=negb[0:C, it:it + 1], in0=beta_ap(it), scalar1=-1.0, scalar2=None, op0=ALU.mult),
                        regs_in, [negb])
            p_gc, p_gcb, p_kk, p_qk = PS[7], PS[0], PS[1], PS[2]
            PE(lambda e: e.matmul(p_gc[0:C, 0:I], lhsT=cf(C_UTRI, C, C), rhs=gsc[0:C, 0:I], start=True, stop=True), [con, gsc], [p_gc])
            DVE(lambda e: e.tensor_copy(out=gcs[0:C, 0:I], in_=p_gc[0:C, 0:I]), [p_gc], [gcs])
            for it in range(I):
                DVE(lambda e, it=it: e.tensor_scalar(out=GU[0:C, it, 0:C], in0=cf(C_UTRI, C, C), scalar1=gsc[0:C, it:it + 1], scalar2=None, op0=ALU.mult),
                    [con, gsc], [GU])
            for it in range(I):
                PE(lambda e, it=it: e.matmul(p_gcb[:, it * C:(it + 1) * C], lhsT=cf(C_ONE, C, 128), rhs=GU[0:C, it, 0:C], start=True, stop=True),
                   [con, GU], [p_gcb])
            for it in range(I):
                DVE(lambda e, it=it: e.scalar_tensor_tensor(out=TMPD[0:C, it, 0:C], in0=p_gcb[0:C, it * C:(it + 1) * C], scalar=gcs[0:C, it:it + 1],
                                                            in1=cf(C_MPOS, C, C), op0=ALU.subtract, op1=ALU.add), [p_gcb, gcs, con], [TMPD])
            ACT(lambda e: e.activation(out=v3(Dm), in_=v3(TMPD), func=AF.Exp, scale=-1.0), [TMPD], [Dm])
            for it in range(I):
                POOL(lambda e, it=it: e.tensor_tensor(out=DS[0:C, it, 0:C], in0=Dm[0:C, it, 0:C], in1=cf(C_STRICT, C, C), op=ALU.mult), [Dm, con], [DS])
            for it in range(I):
                PE(lambda e, it=it: e.matmul(p_kk[0:C, it * C:(it + 1) * C], lhsT=ka(it), rhs=ka(it), start=True, stop=True), regs_in, [p_kk])
                PE(lambda e, it=it: e.matmul(p_qk[0:C, it * C:(it + 1) * C], lhsT=qa(it), rhs=ka(it), start=True, stop=True), regs_in, [p_qk])
            for it in range(I):
                DVE(lambda e, it=it: e.scalar_tensor_tensor(out=N_[0:C, it, 0:C], in0=p_kk[0:C, it * C:(it + 1) * C], scalar=negb[0:C, it:it + 1],
                                                            in1=DS[0:C, it, 0:C], op0=ALU.mult, op1=ALU.mult), [p_kk, negb, DS], [N_])
            DVE(lambda e: e.tensor_tensor(out=v3(AQ), in0=pv(p_qk), in1=v3(Dm), op=ALU.mult), [p_qk, Dm], [AQ])
            p_nt, p_aqt = PS[3], PS[4]
            for it in range(I):
                PE(lambda e, it=it: e.matmul(p_nt[0:C, it * C:(it + 1) * C], lhsT=N_[0:C, it, 0:C], rhs=cf(C_ID, C, C), start=True, stop=True), [N_, con], [p_nt])
                PE(lambda e, it=it: e.matmul(p_aqt[0:C, it * C:(it + 1) * C], lhsT=AQ[0:C, it, 0:C], rhs=cb(C_ID, C, C), start=True, stop=True), [AQ, conb], [p_aqt])
            ACT(lambda e: e.activation(out=v3(NT), in_=pv(p_nt), func=AF.Copy), [p_nt], [NT])
            ACT(lambda e: e.activation(out=v3(AQT), in_=pv(p_aqt), func=AF.Copy), [p_aqt], [AQT])
            for it in range(I):
                DVE(lambda e, it=it: e.tensor_tensor(out=TT[0:C, it, 0:C], in0=p_nt[0:C, it * C:(it + 1) * C], in1=cf(C_ID, C, C), op=ALU.add), [p_nt, con], [TT])
            nlev = int(round(math.log2(C)))
            Pk, Ptk = N_, NT
            Pn, Ptn = P2, PT2
            for k in range(nlev):
                if k >= 1:
                    p_d = PS[5]
                    for it in range(I):
                        PE(lambda e, it=it, Pk=Pk: e.matmul(p_d[0:C, it * C:(it + 1) * C], lhsT=Pk[0:C, it, 0:C], rhs=TT[0:C, it, 0:C], start=True, stop=True),
                           [Pk, TT], [p_d])
                    DVE(lambda e: e.tensor_tensor(out=v3(TT), in0=pv(p_d), in1=v3(TT), op=ALU.add), [p_d, TT], [TT])
                if k <= nlev - 2:
                    p_a, p_b = PS[6], PS[7]
                    for it in range(I):
                        PE(lambda e, it=it, Pk=Pk, Ptk=Ptk: e.matmul(p_a[0:C, it * C:(it + 1) * C], lhsT=Ptk[0:C, it, 0:C], rhs=Pk[0:C, it, 0:C], start=True, stop=True),
                           [Pk, Ptk], [p_a])
                        PE(lambda e, it=it, Pk=Pk, Ptk=Ptk: e.matmul(p_b[0:C, it * C:(it + 1) * C], lhsT=Pk[0:C, it, 0:C], rhs=Ptk[0:C, it, 0:C], start=True, stop=True),
                           [Pk, Ptk], [p_b])
                    ACT(lambda e, Pn=Pn: e.activation(out=v3(Pn), in_=pv(p_a), func=AF.Copy), [p_a], [Pn])
                    DVE(lambda e, Ptn=Ptn: e.tensor_copy(out=v3(Ptn), in_=pv(p_b)), [p_b], [Ptn])
                    Pk, Ptk, Pn, Ptn = Pn, Ptn, Pk, Ptk
                    if Pn is N_:
                        Pn, Ptn = N_, NT
            ACT(lambda e: e.activation(out=v3(TTb), in_=v3(TT), func=AF.Copy), [TT], [TTb])
            ACT(lambda e: e.activation(out=eg[0:C, 0:I], in_=gcs[0:C, 0:I], func=AF.Exp), [gcs], [eg])
            if beta2 is not None:
                DVE(lambda e: e.tensor_tensor(out=bg[0:C, 0:I], in0=eg[0:C, 0:I], in1=beta2, op=ALU.mult), [eg] + regs_in, [bg])
            else:
                for it in range(I):
                    DVE(lambda e, it=it: e.tensor_tensor(out=bg[0:C, it:it + 1], in0=eg[0:C, it:it + 1], in1=beta_ap(it), op=ALU.mult), [eg] + regs_in, [bg])
            DVE(lambda e: e.tensor_tensor(out=egl[0:C, 0:I], in0=pv(p_gcb)[:, :, C - 1], in1=gcs[0:C, 0:I], op=ALU.subtract), [p_gcb, gcs], [egl])
            ACT(lambda e: e.activation(out=glt[:, 0:I], in_=pv(p_gcb, 128)[:, :, C - 1], func=AF.Exp), [p_gcb], [glt])
            ACT(lambda e: e.activation(out=egl[0:C, 0:I], in_=egl[0:C, 0:I], func=AF.Exp), [egl], [egl])
            EGB = GU
            ACT(lambda e: e.activation(out=v3(EGB, 128), in_=pv(p_gcb, 128), func=AF.Exp), [p_gcb], [EGB])
            if q3 is not None:
                DVE(lambda e: e.tensor_tensor(out=v3(QG, 128), in0=q3, in1=v3(EGB, 128), op=ALU.mult), regs_in + [EGB], [QG])
            else:
                for it in range(I):
                    DVE(lambda e, it=it: e.tensor_tensor(out=QG[:, it, 0:C], in0=qa(it), in1=EGB[:, it, 0:C], op=ALU.mult), regs_in + [EGB], [QG])
            p_kt, p_vt = PS[1], PS[2]
            for it in range(I):
                PE(lambda e, it=it: e.matmul(p_kt[0:C, it * 128:(it + 1) * 128], lhsT=ka(it), rhs=cb(C_ID), start=True, stop=True), regs_in + [conb], [p_kt])
                PE(lambda e, it=it: e.matmul(p_vt[0:C, it * 128:(it + 1) * 128], lhsT=va(it), rhs=cb(C_ID), start=True, stop=True), regs_in + [conb], [p_vt])
            for it in range(I):
                DVE(lambda e, it=it: e.tensor_scalar(out=KBG[0:C, it, :], in0=p_kt[0:C, it * 128:(it + 1) * 128], scalar1=bg[0:C, it:it + 1], scalar2=None, op0=ALU.mult), [p_kt, bg], [KBG])
                DVE(lambda e, it=it: e.tensor_scalar(out=KG[0:C, it, :], in0=p_kt[0:C, it * 128:(it + 1) * 128], scalar1=egl[0:C, it:it + 1], scalar2=None, op0=ALU.mult), [p_kt, egl], [KG])
                DVE(lambda e, it=it: e.tensor_scalar(out=VB[0:C, it, :], in0=p_vt[0:C, it * 128:(it + 1) * 128], scalar1=beta_ap(it), scalar2=None, op0=ALU.mult), [p_vt] + regs_in, [VB])
            p_w = PS[3]
            for it in range(I):
                PE(lambda e, it=it: e.matmul(p_w[:, it * C:(it + 1) * C], lhsT=KBG[0:C, it, :], rhs=TTb[0:C, it, 0:C], start=True, stop=True), [KBG, TTb], [p_w])
            ACT(lambda e: e.activation(out=v3(NW, 128), in_=pv(p_w, 128), func=AF.Copy, scale=-1.0), [p_w], [NW])
            if S3 is not None:
                ACT(lambda e: e.activation(out=Sbf[:, 0:I, :], in_=S3, func=AF.Copy), o_regs, [Sbf])
            else:
                for it in range(I):
                    ACT(lambda e, it=it: e.activation(out=Sbf[:, it, :], in_=Sst(it), func=AF.Copy), o_regs, [Sbf])
            p_vn, p_o, p_s = PS[4], PS[5], PS[6]
            for it in range(I):
                def f(e, it=it):
                    e.matmul(p_vn[0:C, it * 128:(it + 1) * 128], lhsT=TTb[0:C, it, 0:C], rhs=VB[0:C, it, :], start=True, stop=False)
                    return e.matmul(p_vn[0:C, it * 128:(it + 1) * 128], lhsT=NW[:, it, 0:C], rhs=Sbf[:, it, :], start=False, stop=True)
                PE(f, [TTb, VB, NW, Sbf], [p_vn])
            ACT(lambda e: e.activation(out=VN[0:C, 0:I, :], in_=pv(p_vn, C, 128), func=AF.Copy), [p_vn], [VN])
            for it in range(I):
                def f2(e, it=it):
                    e.matmul(p_o[:, it * C:(it + 1) * C], lhsT=Sbf[:, it, :], rhs=QG[:, it, 0:C], start=True, stop=False)
                    return e.matmul(p_o[:, it * C:(it + 1) * C], lhsT=VN[0:C, it, :], rhs=AQT[0:C, it, 0:C], start=False, stop=True)
                PE(f2, [Sbf, QG, VN, AQT], [p_o])
                PE(lambda e, it=it: e.matmul(p_s[:, it * 128:(it + 1) * 128], lhsT=KG[0:C, it, :], rhs=VN[0:C, it, :], start=True, stop=True), [KG, VN], [p_s])
            if o3 is not None:
                ACT(lambda e: e.activation(out=o3, in_=pv(p_o, 128), func=AF.Copy), [p_o], o_regs[1:])
            for it in range(I):
                if o3 is None:
                    DVE(lambda e, it=it: e.tensor_copy(out=oa(it), in_=p_o[:, it * C:(it + 1) * C]), [p_o], o_regs[1:])
                DVE(lambda e, it=it: e.scalar_tensor_tensor(out=Sst(it), in0=Sst(it), scalar=glt[:, it:it + 1], in1=p_s[:, it * 128:(it + 1) * 128],
                                                            op0=ALU.mult, op1=ALU.add), [glt, p_s], o_regs[0:1])

        def attn_combine(Ops, Lps, ncols, dst_ap, dst_reg, lcol=None):
            r0 = nscr()
            DVE(lambda e: e.reciprocal(out=r0[:, 0:ncols], in_=Lps[0][:, 0:ncols]), [Lps[0]], [r0])
            t0_ = nscr()
            DVE(lambda e: e.tensor_tensor(out=t0_[:, 0:ncols], in0=Ops[0][:, 0:ncols], in1=r0[:, 0:ncols], op=ALU.mult), [Ops[0], r0], [t0_])
            if lcol is None:
                ACT(lambda e: e.activation(out=dst_ap, in_=t0_[:, 0:ncols], func=AF.Copy), [t0_], [dst_reg])
                return
            r1 = nscr()
            DVE(lambda e: e.reciprocal(out=r1[:, 0:ncols], in_=Lps[1][:, 0:ncols]), [Lps[1]], [r1])
            t1_ = nscr()
            DVE(lambda e: e.tensor_tensor(out=t1_[:, 0:ncols], in0=Ops[1][:, 0:ncols], in1=r1[:, 0:ncols], op=ALU.mult), [Ops[1], r1], [t1_])
            DVE(lambda e: e.scalar_tensor_tensor(out=dst_ap, in0=t1_[:, 0:ncols], scalar=lcol, in1=t0_[:, 0:ncols], op0=ALU.mult, op1=ALU.add),
                [t1_, t0_, lamv], [dst_reg])

        def headnorm(src, dst, l, gcol, T, extra=None, post_scale=1.0):
            for h in range(4):
                for (t0, n) in blocks_of(T):
                    tr = stats([(src[:, h, t0:t0 + n], (src, h))], n, C_O128, nps(4, 6))
                    if extra is None:
                        DVE(lambda e, h=h, t0=t0, n=n, tr=tr: e.scalar_tensor_tensor(out=dst[:, h, t0:t0 + n], in0=src[:, h, t0:t0 + n], scalar=pcol(l, gcol),
                                                                                    in1=tr[:, 0:n], op0=ALU.mult, op1=ALU.mult), [(src, h), tr, par], [(dst, h)])
                        if post_scale != 1.0:
                            DVE(lambda e, h=h, t0=t0, n=n: e.tensor_scalar(out=dst[:, h, t0:t0 + n], in0=dst[:, h, t0:t0 + n], scalar1=post_scale, scalar2=None, op0=ALU.mult),
                                [(dst, h)], [(dst, h)])
                    else:
                        tq = nscr()
                        DVE(lambda e, h=h, t0=t0, n=n, tr=tr, tq=tq: e.scalar_tensor_tensor(out=tq[:, 0:n], in0=src[:, h, t0:t0 + n], scalar=pcol(l, gcol),
                                                                                           in1=tr[:, 0:n], op0=ALU.mult, op1=ALU.mult), [(src, h), tr, par], [tq])
                        DVE(lambda e, h=h, t0=t0, n=n, tq=tq: e.tensor_tensor(out=dst[:, h, t0:t0 + n], in0=tq[:, 0:n], in1=extra[:, h, t0:t0 + n], op=ALU.mult),
                            [tq, (extra, h)], [(dst, h)])

        def layer(seg, l):
            has_s = (seg == NSEG - 1)
            T = TSEG + TS if has_s else TSEG
            TP = TSEG
            a0 = seg * TSEG
            blks = blocks_of(T)
            win = w_in[l]
            S.dma("sp", memx[:, :, :], memT_in.rearrange("(kc p) n -> p kc n", p=128), writes=[memx]) if False else None
            lam_init = 0.8 - 0.6 * math.exp(-0.3 * l)
            hreg = [hT]
            wst["pos"] = 0
            wst["issued"] = {}
            wst["l"] = l

            def hrhs(k, t0, n):
                return hT[:, k, t0:t0 + n]
            def chk(p):
                if DBG["phase"] < p:
                    raise _Stop()
            rmsnorm_x(xT, l, P_NMIX, T, hT)
            chk(2)
            S.dma("sp", cst[:, :, 0:TP], cs_in[:, :, a0:a0 + TP], writes=[cst])
            if has_s:
                for s in range(NS):
                    S.dma("sp", cst[:, :, TP + 4 * s:TP + 4 * s + 4], cs_in[:, :, SEQ:SEQ + SL], writes=[cst])
            if DBG["sub"] < 1:
                raise _Stop()
            kcur = B1 if has_s else kd_st[l]
            vcur = vloc if has_s else v_st[l]
            kdo = 0 if has_s else a0
            vto = 0 if has_s else seg * TT
            for (c0, dst, gcol, isk) in ((C_DQ, B0, P_QND, False), (C_DK, kcur, P_KND, True)):
                sl = wload("w_in", 0, 8, c0, 512)
                if DBG["sub"] < 2:
                    raise _Stop()

                def hqk(cc, t0, n, ps, dst=dst, gcol=gcol, isk=isk):
                    if DBG["sub"] < 3:
                        return
                    X = nscr()
                    ACT(lambda e: e.activation(out=X[:, 0:n], in_=ps[:, 0:n], func=AF.Copy), [ps], [X])
                    tr = stats([(X[:, 0:n], X)], n, C_BLK, nps(4, 6))
                    Y = nscr()
                    DVE(lambda e: e.scalar_tensor_tensor(out=Y[:, 0:n], in0=X[:, 0:n], scalar=pcol(l, gcol), in1=tr[:, 0:n], op0=ALU.mult, op1=ALU.mult),
                        [X, tr, par], [Y])
                    pr = nps(6, 8)
                    PE(lambda e: e.matmul(pr[:, 0:n], lhsT=cf(C_ROT), rhs=Y[:, 0:n], start=True, stop=True), [con, Y], [pr])
                    Z = nscr()
                    DVE(lambda e: e.tensor_tensor(out=Z[:, 0:n], in0=Y[:, 0:n], in1=cst[:, 0, t0:t0 + n], op=ALU.mult), [Y, cst], [Z])
                    Z2 = nscr()
                    DVE(lambda e: e.tensor_tensor(out=Z2[:, 0:n], in0=pr[:, 0:n], in1=cst[:, 1, t0:t0 + n], op=ALU.mult), [pr, cst], [Z2])
                    if not isk:
                        DVE(lambda e: e.tensor_tensor(out=dst[:, cc, t0:t0 + n], in0=Z[:, 0:n], in1=Z2[:, 0:n], op=ALU.add), [Z, Z2], [(dst, cc)])
                    else:
                        KF = nscr()
                        DVE(lambda e: e.tensor_tensor(out=KF[:, 0:n], in0=Z[:, 0:n], in1=Z2[:, 0:n], op=ALU.add), [Z, Z2], [KF])
                        ACT(lambda e: e.activation(out=dst[:, cc, kdo + t0:kdo + t0 + n], in_=KF[:, 0:n], func=AF.Copy), [KF], [(dst, cc)] if dst.n > 1 else [dst])
                        if t0 < TP:
                            S.dma("sp", pkT_o[l, cc * 128:(cc + 1) * 128, a0 + t0:a0 + t0 + n], KF[:, 0:n], reads=[KF], is_output=True)
                        else:
                            S.dma("sp", skT_o[l, cc * 128:(cc + 1) * 128, :], KF[:, 0:n], reads=[KF], is_output=True)
                proj_fm([(sl, 8)], range(4), hrhs, hreg, blks, hqk)
            if DBG["sub"] < 4:
                raise _Stop()
            sl = wload("w_in", 0, 8, C_DV, 512)
            for tt in range(TT):
                ps = nps(0, 4)

                def mmv(e, ps=ps, tt=tt):
                    ins = None
                    for k in range(8):
                        ins = e.matmul(ps[:, 0:512], lhsT=hT[:, k, tt * 128:(tt + 1) * 128], rhs=sl[:, k, 0:512], start=(k == 0), stop=(k == 7))
                    return ins
                PE(mmv, [sl, hT], [ps])
                VF = nscr()
                ACT(lambda e, ps=ps, VF=VF: e.activation(out=VF[:, :], in_=ps[:, :], func=AF.Copy), [ps], [VF])
                if DBG["sub"] >= 5:
                    DVE(lambda e, VF=VF, tt=tt: e.tensor_copy(out=vcur[:, vto + tt, :], in_=VF[:, :]), [VF], [vcur])
                if DBG["sub"] >= 6:
                    S.dma("sp", pv_o[l, a0 + tt * 128:a0 + (tt + 1) * 128, :], VF[:, :], reads=[VF], is_output=True)
            if has_s:
                for s in range(NS):
                    ps = nps(0, 4)

                    def mmvs(e, ps=ps, s=s):
                        ins = None
                        for k in range(8):
                            ins = e.matmul(ps[0:SL, 0:512], lhsT=hT[:, k, TP + 4 * s:TP + 4 * s + 4], rhs=sl[:, k, 0:512], start=(k == 0), stop=(k == 7))
                        return ins
                    PE(mmvs, [sl, hT], [ps])
                    VF = nscr()
                    ACT(lambda e, ps=ps, VF=VF: e.activation(out=VF[0:SL, :], in_=ps[0:SL, :], func=AF.Copy), [ps], [VF])
                    DVE(lambda e, VF=VF, s=s: e.tensor_copy(out=vs_tok[0:SL, s, :], in_=VF[0:SL, :]), [VF], [vs_tok])
                    S.dma("sp", sv_o[l, s, :, :], VF[0:SL, :], reads=[VF], is_output=True)
            chk(3)
            neglam = lamv[:, l:l + 1]
            for qb in range(1):
                q0t = seg * TT
                for h in range(4):
                    Ops = [PS[2], PS[3]]
                    Lps = [PS[4], PS[5]]
                    for c in range(2):
                        nkt = q0t + 4
                        for kt in range(nkt):
                            off = max(0, kt - q0t) * 128
                            ncol = 512 - off
                            qc0 = qb * 512 + off
                            if kt < seg * TT or not has_s:
                                ksrc, kreg, ko = kd_st[l], kd_st[l], kt * 128
                                vsrc, vreg, vt = v_st[l], v_st[l], kt
                            else:
                                ksrc, kreg, ko = B1, B1, (kt - seg * TT) * 128
                                vsrc, vreg, vt = vloc, vloc, kt - seg * TT
                            sp_ = nps(0, 2)
                            PE(lambda e, sp_=sp_, ksrc=ksrc, ko=ko, c=c, h=h, qc0=qc0, ncol=ncol: e.matmul(
                                sp_[:, 0:ncol], lhsT=ksrc[c * 64:(c + 1) * 64, h, ko:ko + 128], rhs=B0[c * 64:(c + 1) * 64, h, qc0:qc0 + ncol], start=True, stop=True),
                               [kreg, (B0, h)], [sp_])
                            PT = nscrb()
                            ACT(lambda e, sp_=sp_, PT=PT, ncol=ncol: e.activation(out=PT[:, 0:ncol], in_=sp_[:, 0:ncol], func=AF.Exp, scale=0.125), [sp_], [PT])
                            if kt >= q0t:
                                POOL(lambda e, PT=PT: e.tensor_tensor(out=PT[:, 0:128], in0=PT[:, 0:128], in1=cb(C_TRI), op=ALU.mult), [PT, conb], [PT])
                            PE(lambda e, PT=PT, vsrc=vsrc, vt=vt, h=h, c=c, off=off, ncol=ncol, kt=kt, nkt=nkt: e.matmul(
                                Ops[c][:, off:off + ncol], lhsT=vsrc[:, vt, h * 128:(h + 1) * 128], rhs=PT[:, 0:ncol], start=(kt == 0), stop=(kt == nkt - 1)),
                               [vreg, PT], [Ops[c]])
                            PE(lambda e, PT=PT, c=c, off=off, ncol=ncol, kt=kt, nkt=nkt: e.matmul(
                                Lps[c][:, off:off + ncol], lhsT=cb(C_ONE), rhs=PT[:, 0:ncol], start=(kt == 0), stop=(kt == nkt - 1)),
                               [conb, PT], [Lps[c]])
                    attn_combine(Ops, Lps, 512, OF[:, h, qb * 512:(qb + 1) * 512], (OF, h), lcol=neglam)
            chk(4)
            if has_s:
                for s in range(NS):
                    Os, Ls = PS[6], PS[7]
                    sc0 = TP + 4 * s

                    def gatherK(j):
                        ic = s * NPAGES + j
                        S.dma("pool", kpage[j % 2][:, :], cache_k[:, :], writes=[kpage[j % 2]], reads=[idx_l[l]], indirect=idx_l[l][:, ic:ic + 1])

                    def gatherV(j):
                        ic = s * NPAGES + j
                        S.dma("pool", vpage[j % 2][:, :], cache_v[:, :], writes=[vpage[j % 2]], reads=[idx_l[l]], indirect=idx_l[l][:, ic:ic + 1])

                    def stageT(j):
                        kp = kpage[j % 2]
                        pk_ = PS[2 + j % 2]
                        for h in range(4):
                            PE(lambda e, h=h: e.matmul(pk_[:, h * 128:(h + 1) * 128], lhsT=kp[:, h * 128:(h + 1) * 128], rhs=cb(C_ID), start=True, stop=True),
                               [kp, conb], [pk_])
                        KT = KTp[j % 2]
                        DVE(lambda e: e.tensor_copy(out=KT[:, :, :], in_=pk_[:, :].rearrange("p (h t) -> p h t", h=4)), [pk_], [KT])

                    PTs = {}

                    def stageS(j):
                        sp_ = PS[j % 2]
                        nk_ = 128 if j < NPAGES else SL
                        for h in range(4):
                            for c in range(2):
                                col = (h * 2 + c) * 4
                                if j < NPAGES:
                                    KT = KTp[j % 2]
                                    PE(lambda e, h=h, c=c, col=col: e.matmul(sp_[:, col:col + 4], lhsT=KT[c * 64:(c + 1) * 64, h, :],
                                                                             rhs=B0[c * 64:(c + 1) * 64, h, sc0:sc0 + 4], start=True, stop=True), [KT, (B0, h)], [sp_])
                                else:
                                    PE(lambda e, h=h, c=c, col=col: e.matmul(sp_[0:SL, col:col + 4], lhsT=B1[c * 64:(c + 1) * 64, h, sc0:sc0 + 4],
                                                                             rhs=B0[c * 64:(c + 1) * 64, h, sc0:sc0 + 4], start=True, stop=True), [(B1, h), (B0, h)], [sp_])
                        PT = scrb[j % 4]
                        PTs[j] = PT
                        ACT(lambda e: e.activation(out=PT[0:nk_, 0:32], in_=sp_[0:nk_, 0:32], func=AF.Exp, scale=0.125), [sp_], [PT])
                        if j == NPAGES:
                            DVE(lambda e: e.tensor_tensor(out=PT[0:SL, 0:32], in0=PT[0:SL, 0:32], in1=smask[0:SL, 0:32], op=ALU.mult), [PT, smask], [PT])

                    def stagePV(j):
                        PT = PTs.pop(j)
                        nk_ = 128 if j < NPAGES else SL
                        vp = vpage[j % 2]
                        for h in range(4):
                            for c in range(2):
                                col = (h * 2 + c) * 4
                                if j < NPAGES:
                                    PE(lambda e, h=h, col=col: e.matmul(Os[:, col:col + 4], lhsT=vp[:, h * 128:(h + 1) * 128], rhs=PT[:, col:col + 4],
                                                                        start=(j == 0 and col == 0), stop=False, skip_group_check=True), [vp, PT], [Os])
                                else:
                                    PE(lambda e, h=h, col=col: e.matmul(Os[:, col:col + 4], lhsT=vs_tok[0:SL, s, h * 128:(h + 1) * 128], rhs=PT[0:SL, col:col + 4],
                                                                        start=False, stop=True, skip_group_check=True), [vs_tok, PT], [Os])
                        PE(lambda e: e.matmul(Ls[:, 0:32], lhsT=cb(C_ONE, nk_, 128), rhs=PT[0:nk_, 0:32], start=(j == 0), stop=(j == NPAGES)),
                           [conb, PT], [Ls])

                    gatherK(0)
                    gatherK(1)
                    gatherV(0)
                    stageT(0)
                    for j in range(0, NPAGES + 2):
                        if j + 1 < NPAGES:
                            stageT(j + 1)
                        if j <= NPAGES:
                            stageS(j)
                        if 0 <= j - 1 <= NPAGES:
                            stagePV(j - 1)
                        if j + 2 < NPAGES:
                            gatherK(j + 2)
                        if j + 1 < NPAGES:
                            gatherV(j + 1)
                    r = nscr()
                    DVE(lambda e, r=r: e.reciprocal(out=r[:, 0:32], in_=Ls[:, 0:32]), [Ls], [r])
                    t_ = nscr()
                    DVE(lambda e, r=r, t_=t_: e.tensor_tensor(out=t_[:, 0:32], in0=Os[:, 0:32], in1=r[:, 0:32], op=ALU.mult), [Os, r], [t_])
                    for h in range(4):
                        DVE(lambda e, h=h, t_=t_: e.scalar_tensor_tensor(out=OF[:, h, sc0:sc0 + 4], in0=t_[:, (2 * h + 1) * 4:(2 * h + 2) * 4], scalar=neglam,
                                                                        in1=t_[:, 2 * h * 4:(2 * h + 1) * 4], op0=ALU.mult, op1=ALU.add), [t_, lamv], [(OF, h)])
            chk(5)
            headnorm(OF, OD, l, P_DIFFN, T, post_scale=(1.0 - lam_init))
            chk(6)
            for bi, (c0, dst) in enumerate(((C_Q, B0), (C_K, B1), (C_V, B2))):
                sl = wload("w_in", 0, 8, c0, 512)
                st = {}

                def hraw(cc, t0, n, ps, st=st, bi=bi):
                    rw = raw[0]
                    sr = sraw[0]
                    ch = bi * 4 + cc
                    if t0 == 0:
                        if seg == 0:
                            DVE(lambda e: e.memset(rw[:, 0:3], 0.0), [], [rw])
                        else:
                            DVE(lambda e: e.tensor_copy(out=rw[:, 0:3], in_=gtail[l][:, ch, :]), [gtail[l]], [rw])
                    if t0 < TP:
                        ACT(lambda e: e.activation(out=rw[:, 3 + t0:3 + t0 + n], in_=ps[:, 0:n], func=AF.Copy), [ps], [rw])
                    else:
                        S.dma("sp", sr[:, :, 0:3], st_gconvT[l, ch * 128:(ch + 1) * 128, :, :], writes=[sr])
                        ACT(lambda e: e.activation(out=sr[:, :, 3:7], in_=ps[:, 0:TS].rearrange("p (s j) -> p s j", s=NS), func=AF.Copy), [ps], [sr])

                def aft(cc, bi=bi, dst=dst):
                    ch = bi * 4 + cc
                    rw = raw[0]
                    sr = sraw[0]
                    ac = acc[0]
                    wc = P_CQKV + ch * 4
                    ACT(lambda e: e.activation(out=ac[:, 0:TP], in_=rw[:, 0:TP], func=AF.Copy, scale=pcol(l, wc)), [rw, par], [ac])
                    for j in range(1, 4):
                        DVE(lambda e, j=j: e.scalar_tensor_tensor(out=ac[:, 0:TP], in0=rw[:, j:j + TP], scalar=pcol(l, wc + j), in1=ac[:, 0:TP], op0=ALU.mult, op1=ALU.add),
                            [rw, par, ac], [ac])
                    if has_s:
                        av = ac[:, TP:TP + TS].rearrange("p (s j) -> p s j", s=NS)
                        ACT(lambda e: e.activation(out=av, in_=sr[:, :, 0:4], func=AF.Copy, scale=pcol(l, wc)), [sr, par], [ac])
                        for j in range(1, 4):
                            DVE(lambda e, j=j: e.scalar_tensor_tensor(out=av, in0=sr[:, :, j:j + 4], scalar=pcol(l, wc + j), in1=av, op0=ALU.mult, op1=ALU.add),
                                [sr, par, ac], [ac])
                        S.dma("sp", sgcT_o[l, ch * 128:(ch + 1) * 128, :, :], sr[:, :, 4:7], reads=[sr], is_output=True)
                        S.dma("sp", pgcT_o[l, ch * 128:(ch + 1) * 128, :], rw[:, TP:TP + 3], reads=[rw], is_output=True)
                    else:
                        DVE(lambda e: e.tensor_copy(out=gtail[l][:, ch, :], in_=rw[:, TP:TP + 3]), [rw], [gtail[l]])
                    if bi == 2:
                        ACT(lambda e: e.activation(out=dst[:, cc, 0:T], in_=ac[:, 0:T], func=AF.Silu), [ac], [(dst, cc)])
                    else:
                        ACT(lambda e: e.activation(out=ac[:, 0:T], in_=ac[:, 0:T], func=AF.Silu), [ac], [ac])
                        for (t0, n) in blks:
                            tr = stats([(ac[:, t0:t0 + n], ac)], n, C_ONE, nps(4, 6))
                            DVE(lambda e, t0=t0, n=n, tr=tr: e.scalar_tensor_tensor(out=dst[:, cc, t0:t0 + n], in0=ac[:, t0:t0 + n], scalar=(ISQ if bi == 0 else 1.0),
                                                                                  in1=tr[:, 0:n], op0=ALU.mult, op1=ALU.mult), [ac, tr], [(dst, cc)])
                proj_fm([(sl, 8)], range(4), hrhs, hreg, blks, hraw, after=aft)
            sl = wload("w_in", 0, 8, C_G, 512)
            proj_fm([(sl, 8)], range(4), hrhs, hreg, blks,
                    lambda cc, t0, n, ps: ACT(lambda e: e.activation(out=B3[:, cc, t0:t0 + n], in_=ps[:, 0:n], func=AF.Silu), [ps], [(B3, cc)]))
            sl = wload("w_in", 0, 8, C_BA, 8)

            def ba_post(ps, m, bdst, gdst):
                ACT(lambda e: e.activation(out=bdst, in_=ps[0:m, 0:4], func=AF.Sigmoid), [ps], [betaT, betaS])
                xs, tt_, ee, ln_ = gsm[0], gsm[1], gsm[2], gsm[3]
                DVE(lambda e: e.tensor_tensor(out=xs[0:m, 0:4], in0=ps[0:m, 4:8], in1=pcol(l, P_DTB, 4)[0:m, :], op=ALU.add), [ps, par], [xs])
                ACT(lambda e: e.activation(out=tt_[0:m, 0:4], in_=xs[0:m, 0:4], func=AF.Abs), [xs], [tt_])
                ACT(lambda e: e.activation(out=ee[0:m, 0:4], in_=tt_[0:m, 0:4], func=AF.Exp, scale=-1.0), [tt_], [ee])
                ACT(lambda e: e.activation(out=ln_[0:m, 0:4], in_=ee[0:m, 0:4], func=AF.Ln, bias=1.0), [ee], [ln_])
                DVE(lambda e: e.scalar_tensor_tensor(out=xs[0:m, 0:4], in0=xs[0:m, 0:4], scalar=0.0, in1=ln_[0:m, 0:4], op0=ALU.max, op1=ALU.add), [xs, ln_], [xs])
                DVE(lambda e: e.tensor_tensor(out=gdst, in0=xs[0:m, 0:4], in1=negA[0:m, 4 * l:4 * l + 4], op=ALU.mult), [xs, negA], [gT, gS])
            for tt in range(TT):
                ps = nps(0, 4)

                def mmb(e, ps=ps, tt=tt):
                    ins = None
                    for k in range(8):
                        ins = e.matmul(ps[:, 0:8], lhsT=hT[:, k, tt * 128:(tt + 1) * 128], rhs=sl[:, k, 0:8], start=(k == 0), stop=(k == 7))
                    return ins
                PE(mmb, [sl, hT], [ps])
                ba_post(ps, 128, betaT[:, tt, :], gT[:, tt, :])
            if has_s:
                for s in range(NS):
                    ps = nps(0, 4)

                    def mmbs(e, ps=ps, s=s):
                        ins = None
                        for k in range(8):
                            ins = e.matmul(ps[0:SL, 0:8], lhsT=hT[:, k, TP + 4 * s:TP + 4 * s + 4], rhs=sl[:, k, 0:8], start=(k == 0), stop=(k == 7))
                        return ins
                    PE(mmbs, [sl, hT], [ps])
                    ba_post(ps, SL, betaS[0:SL, s, :], gS[0:SL, s, :])
            chk(7)
            if seg == 0:
                DVE(lambda e: e.memset(S32[l][:, :, :], 0.0), [], [S32[l]])
            for ci in range(TT):
                cs_ = slice(ci * 128, (ci + 1) * 128)
                gdn_step(128, 4, lambda it: B0[:, it, cs_], lambda it: B1[:, it, cs_], lambda it: B2[:, it, cs_], [B0, B1, B2, betaT, gT],
                         lambda it: betaT[:, ci, it:it + 1], lambda it: gT[:, ci, it:it + 1],
                         lambda it: S32[l][:, it, :], lambda it: OF[:, it, cs_], [S32[l], OF],
                         q3=B0[:, 0:4, cs_], S3=S32[l][:, 0:4, :], o3=OF[:, 0:4, cs_], beta2=betaT[:, ci, 0:4], g2=gT[:, ci, 0:4])
            if has_s:
                for h in range(4):
                    S.dma("sp", pg_o[l, h, :, :], S32[l][:, h, :], reads=[S32[l]], is_output=True)
                for h in range(4):
                    for s in range(NS):
                        S.dma("sp", Ssm[:, s, :], st_gdn[l, s, h, :, :], writes=[Ssm])
                    gdn_step(SL, NS, lambda it, h=h: B0[:, h, TP + 4 * it:TP + 4 * it + 4], lambda it, h=h: B1[:, h, TP + 4 * it:TP + 4 * it + 4],
                             lambda it, h=h: B2[:, h, TP + 4 * it:TP + 4 * it + 4], [B0, B1, B2, betaS, gS],
                             lambda it, h=h: betaS[0:SL, it, h:h + 1], lambda it, h=h: gS[0:SL, it, h:h + 1],
                             lambda it: Ssm[:, it, :], lambda it, h=h: OF[:, h, TP + 4 * it:TP + 4 * it + 4], [Ssm, OF])
                    for s in range(NS):
                        S.dma("sp", sg_o[l, s, h, :, :], Ssm[:, s, :], reads=[Ssm], is_output=True)
            headnorm(OF, OG, l, P_GDNN, T, extra=B3)
            chk(8)
            for cbk in range(2):
                sl = wload("w_out", 0, 8, cbk * 512, 512)

                def orhs(k, t0, n):
                    return OG[:, k, t0:t0 + n] if k < 4 else OD[:, k - 4, t0:t0 + n]

                def hres(cc, t0, n, ps, cbk=cbk):
                    kc = cbk * 4 + cc
                    DVE(lambda e: e.tensor_tensor(out=xT[:, kc, t0:t0 + n], in0=xT[:, kc, t0:t0 + n], in1=ps[:, 0:n], op=ALU.add), [(xT, kc), ps], [(xT, kc)])
                proj_fm([(sl, 8)], range(4), orhs, [OG, OD], blks, hres)
            chk(9)
            rmsnorm_x(xT, l, P_NCROSS, T, hT)
            sl = wload("w_cq", 0, 8, 0, 512)

            def hq(cc, t0, n, ps):
                X = nscr()
                ACT(lambda e: e.activation(out=X[:, 0:n], in_=ps[:, 0:n], func=AF.Copy), [ps], [X])
                tr = stats([(X[:, 0:n], X)], n, C_O128, nps(4, 6))
                DVE(lambda e: e.scalar_tensor_tensor(out=B2[:, cc, t0:t0 + n], in0=X[:, 0:n], scalar=pcol(l, P_QNC), in1=tr[:, 0:n], op0=ALU.mult, op1=ALU.mult),
                    [X, tr, par], [(B2, cc)])
            proj_fm([(sl, 8)], range(4), hrhs, hreg, blks, hq)
            S.dma("sp", memx[:, :, :], memT_in.rearrange("(kc p) n -> p kc n", p=128), writes=[memx])
            rmsnorm_x(memx, l, P_NMEM, 256, mnT)
            sl = wload("w_ck", 0, 8, 0, 512)

            def hmk(cc, t0, n, ps):
                X = nscr()
                ACT(lambda e: e.activation(out=X[:, 0:n], in_=ps[:, 0:n], func=AF.Copy), [ps], [X])
                tr = stats([(X[:, 0:n], X)], n, C_O128, nps(4, 6))
                KF = nscr()
                DVE(lambda e: e.scalar_tensor_tensor(out=KF[:, 0:n], in0=X[:, 0:n], scalar=pcol(l, P_KNC), in1=tr[:, 0:n], op0=ALU.mult, op1=ALU.mult),
                    [X, tr, par], [KF])
                ACT(lambda e: e.activation(out=mkT[:, cc, 0:n], in_=KF[:, 0:n], func=AF.Copy), [KF], [mkT])
                if seg == 0:
                    S.dma("sp", pmkT_o[l, cc * 128:(cc + 1) * 128, :], KF[:, 0:n], reads=[KF], is_output=True)
            proj_fm([(sl, 8)], range(4), lambda k, t0, n: mnT[:, k, t0:t0 + n], [mnT], [(0, 256)], hmk)
            sl = wload("w_cv", 0, 8, 0, 512)
            for mt in range(2):
                ps = nps(0, 4)

                def mmm(e, ps=ps, mt=mt):
                    ins = None
                    for k in range(8):
                        ins = e.matmul(ps[:, 0:512], lhsT=mnT[:, k, mt * 128:(mt + 1) * 128], rhs=sl[:, k, 0:512], start=(k == 0), stop=(k == 7))
                    return ins
                PE(mmm, [sl, mnT], [ps])
                VF = nscr()
                ACT(lambda e, ps=ps, VF=VF: e.activation(out=VF[:, :], in_=ps[:, :], func=AF.Copy), [ps], [VF])
                DVE(lambda e, VF=VF, mt=mt: e.tensor_copy(out=mv[:, mt, :], in_=VF[:, :]), [VF], [mv])
                if seg == 0:
                    S.dma("sp", pmv_o[l, mt * 128:(mt + 1) * 128, :], VF[:, :], reads=[VF], is_output=True)
            qlist = [(0, 512, mkT, mv, None)]
            if has_s:
                qlist += [(TP + 4 * s, 4, mkTs, mvs, s) for s in range(NS)]
            for (q0, nq, mk_, mv_, ss) in qlist:
                if ss is not None:
                    for h in range(4):
                        S.dma("pool", mkTs[:, h, :], cmkT[l, ss, h, :, :], writes=[mkTs])
                    S.dma("pool", mvs[:, :, :], cmv[l, ss].rearrange("(m p) f -> p m f", p=128), writes=[mvs])
                for h in range(4):
                    Ops = [PS[2]]
                    Lps = [PS[4]]
                    for mt in range(2):
                        sp_ = nps(0, 2)
                        PE(lambda e, sp_=sp_, mt=mt, h=h, mk_=mk_, q0=q0, nq=nq: e.matmul(sp_[:, 0:nq], lhsT=mk_[:, h, mt * 128:(mt + 1) * 128], rhs=B2[:, h, q0:q0 + nq], start=True, stop=True),
                           [mk_, (B2, h)], [sp_])
                        PT = nscrb()
                        ACT(lambda e, sp_=sp_, PT=PT, nq=nq: e.activation(out=PT[:, 0:nq], in_=sp_[:, 0:nq], func=AF.Exp, scale=ISQ), [sp_], [PT])
                        PE(lambda e, PT=PT, mt=mt, h=h, mv_=mv_, nq=nq: e.matmul(Ops[0][:, 0:nq], lhsT=mv_[:, mt, h * 128:(h + 1) * 128], rhs=PT[:, 0:nq], start=(mt == 0), stop=(mt == 1)),
                           [mv_, PT], [Ops[0]])
                        PE(lambda e, PT=PT, mt=mt, nq=nq: e.matmul(Lps[0][:, 0:nq], lhsT=cb(C_ONE), rhs=PT[:, 0:nq], start=(mt == 0), stop=(mt == 1)), [conb, PT], [Lps[0]])
                    attn_combine(Ops, Lps, nq, B3[:, h, q0:q0 + nq], (B3, h))
            for cbk in range(2):
                sl = wload("w_co", 0, 4, cbk * 512, 512)

                def hres2(cc, t0, n, ps, cbk=cbk):
                    kc = cbk * 4 + cc
                    DVE(lambda e: e.tensor_tensor(out=xT[:, kc, t0:t0 + n], in0=xT[:, kc, t0:t0 + n], in1=ps[:, 0:n], op=ALU.add), [(xT, kc), ps], [(xT, kc)])
                proj_fm([(sl, 4)], range(4), lambda k, t0, n: B3[:, k, t0:t0 + n], [B3], blks, hres2)
            chk(10)
            rmsnorm_x(xT, l, P_NFFN, T, hT)
            barrier()
            for fb in range(6):
                ncol = 512 if fb < 5 else 256
                ncc = ncol // 128
                slg = wload("w_gate", 0, 8, fb * 512, ncol)
                slu = wload("w_up", 0, 8, fb * 512, ncol)

                def hgr(cc, t0, n, ps, fb=fb):
                    fc = fb * 4 + cc
                    rw = raw[0]
                    sr = sraw[0]
                    if t0 == 0:
                        if seg == 0:
                            DVE(lambda e: e.memset(rw[:, 0:2], 0.0), [], [rw])
                        else:
                            DVE(lambda e: e.tensor_copy(out=rw[:, 0:2], in_=ftail[l][:, fc, :]), [ftail[l]], [rw])
                    if t0 < TP:
                        ACT(lambda e: e.activation(out=rw[:, 2 + t0:2 + t0 + n], in_=ps[:, 0:n], func=AF.Copy), [ps], [rw])
                    else:
                        S.dma("sp", sr[:, :, 0:2], st_fconvT[l, fc * 128:(fc + 1) * 128, :, :], writes=[sr])
                        ACT(lambda e: e.activation(out=sr[:, :, 2:6], in_=ps[:, 0:TS].rearrange("p (s j) -> p s j", s=NS), func=AF.Copy), [ps], [sr])

                def aftg(cc, fb=fb):
                    fc = fb * 4 + cc
                    rw = raw[0]
                    sr = sraw[0]
                    ac = acc[0]
                    wc = P_CFFN + fc * 3
                    ACT(lambda e: e.activation(out=ac[:, 0:TP], in_=rw[:, 0:TP], func=AF.Copy, scale=pcol(l, wc)), [rw, par], [ac])
                    for j in range(1, 3):
                        DVE(lambda e, j=j: e.scalar_tensor_tensor(out=ac[:, 0:TP], in0=rw[:, j:j + TP], scalar=pcol(l, wc + j), in1=ac[:, 0:TP], op0=ALU.mult, op1=ALU.add),
                            [rw, par, ac], [ac])
                    if has_s:
                        av = ac[:, TP:TP + TS].rearrange("p (s j) -> p s j", s=NS)
                        ACT(lambda e: e.activation(out=av, in_=sr[:, :, 0:4], func=AF.Copy, scale=pcol(l, wc)), [sr, par], [ac])
                        for j in range(1, 3):
                            DVE(lambda e, j=j: e.scalar_tensor_tensor(out=av, in0=sr[:, :, j:j + 4], scalar=pcol(l, wc + j), in1=av, op0=ALU.mult, op1=ALU.add),
                                [sr, par, ac], [ac])
                        S.dma("sp", sfcT_o[l, fc * 128:(fc + 1) * 128, :, :], sr[:, :, 4:6], reads=[sr], is_output=True)
                        S.dma("sp", pfcT_o[l, fc * 128:(fc + 1) * 128, :], rw[:, TP:TP + 2], reads=[rw], is_output=True)
                    else:
                        DVE(lambda e: e.tensor_copy(out=ftail[l][:, fc, :], in_=rw[:, TP:TP + 2]), [rw], [ftail[l]])
                    ACT(lambda e: e.activation(out=ac[:, 0:T], in_=ac[:, 0:T], func=AF.Silu), [ac], [ac])

                    def hup(cc2, t0, n, ps, fc=fc, ac=ac):
                        DVE(lambda e: e.tensor_tensor(out=actT[:, fc, t0:t0 + n], in0=ac[:, t0:t0 + n], in1=ps[:, 0:n], op=ALU.mult), [ac, ps], [(actT, fc)])
                    proj_fm([(slu, 8)], [cc], hrhs, hreg, blks, hup)
                proj_fm([(slg, 8)], range(ncc), hrhs, hreg, blks, hgr, after=aftg, delay=0)
            for cbk in range(2):
                sls = [(wload("w_down", 0, 8, cbk * 512, 512), 8), (wload("w_down", 8, 8, cbk * 512, 512), 8), (wload("w_down", 16, 6, cbk * 512, 512, nopf=True), 6)]

                def hres3(cc, t0, n, ps, cbk=cbk):
                    kc = cbk * 4 + cc
                    DVE(lambda e: e.tensor_tensor(out=xT[:, kc, t0:t0 + n], in0=xT[:, kc, t0:t0 + n], in1=ps[:, 0:n], op=ALU.add), [(xT, kc), ps], [(xT, kc)])
                proj_fm(sls, range(4), lambda k, t0, n: actT[:, k, t0:t0 + n], [actT], blks, hres3)
            barrier()


        S.dry = True
        try:
            layer(0, 0)
        except _Stop:
            pass
        S.dry = False
        for seg in range(NSEG):
            TP = TSEG
            a0 = seg * TSEG
            for kc in range(KC):
                S.dma("sp", xT[:, kc, 0:TP], xT_in[kc * 128:(kc + 1) * 128, a0:a0 + TP], writes=[(xT, kc)])
                if seg == NSEG - 1:
                    S.dma("sp", xT[:, kc, TP:TP + TS], xsT_in[kc * 128:(kc + 1) * 128, :], writes=[(xT, kc)])
            for l in range(DBG["layers"]):
                if stage >= 1 and seg < DBG["segs"]:
                    try:
                        layer(seg, l)
                    except _Stop:
                        pass
            for kc in range(KC):
                S.dma("sp", yT_o[kc * 128:(kc + 1) * 128, a0:a0 + TP], xT[:, kc, 0:TP], reads=[(xT, kc)], is_output=True)
                if seg == NSEG - 1:
                    S.dma("sp", ysT_o[kc * 128:(kc + 1) * 128, :], xT[:, kc, TP:TP + TS], reads=[(xT, kc)], is_output=True)

        S.finish()
    return nc, S


def _consts():
    con = np.zeros((128, 11, 128), np.float32)
    p = np.arange(128)
    con[:, 0] = np.eye(128)
    con[:, 1] = 1.0
    con[:, 2] = (p[:, None] <= p[None, :])
    con[:, 3] = np.where(p[:, None] < p[None, :], BIG, 0.0)
    con[:, 4] = (p[None, :] < p[:, None])
    con[:, 5] = (p[:, None] <= p[None, :])
    R = np.zeros((128, 128), np.float32)
    for q in range(128):
        if q % 64 < 32:
            R[q + 32, q] = -1.0
        else:
            R[q - 32, q] = 1.0
    con[:, 6] = R
    con[:, 7] = (p[:, None] // 64 == p[None, :] // 64) / 64.0
    con[:, 8, 0] = p
    con[:, 8, 1] = p + NPOOL * 128
    con[:, 8, 8] = EPS
    con[:, 9] = 1.0 / 128
    con[:, 10] = 1.0 / 1024
    half = 32
    inv = 10000.0 ** (-np.arange(half, dtype=np.float32) / half)
    pos = np.concatenate([np.arange(SEQ), 8192 + np.arange(SL)]).astype(np.float32)
    ang = pos[None, :] * inv[p % 32][:, None]
    cs = np.stack([np.cos(ang), np.sin(ang)], axis=1).astype(np.float32)
    sm = np.zeros((128, 32), np.float32)
    for j in range(4):
        for m in range(8):
            for q in range(4):
                sm[j, m * 4 + q] = 1.0 if j <= q else 0.0
    return con.reshape(128, 11 * 128), cs, sm


_CACHE = {}


def kernel(**inp):
    f = lambda a: np.ascontiguousarray(np.asarray(a, dtype=np.float32))
    if "nc" not in _CACHE:
        _CACHE["nc"] = build_program()[0]
    nc = _CACHE["nc"]
    con, cs, sm = _consts()
    NPS = 161
    par = np.zeros((128, DEPTH, NPS), np.float32)
    for l in range(DEPTH):
        for o, nm in ((0, "norm_mix"), (8, "norm_cross"), (16, "norm_mem"), (24, "norm_ffn")):
            par[:, l, o:o + 8] = np.asarray(inp[nm][l]).reshape(8, 128).T
        par[:, l, 32:80] = np.asarray(inp["conv_qkv"][l]).reshape(4, 12, 128).transpose(2, 1, 0).reshape(128, 48)
        par[:, l, 80:146] = np.asarray(inp["conv_ffn"][l]).reshape(3, 22, 128).transpose(2, 1, 0).reshape(128, 66)
        par[:, l, 146:150] = np.asarray(inp["a_log"][l])[None, :]
        par[:, l, 150:154] = np.asarray(inp["dt_bias"][l])[None, :]
        par[:, l, 154] = np.asarray(inp["gdn_norm"][l])
        par[:, l, 155] = np.tile(np.asarray(inp["qnorm_diff"][l]), 2)
        par[:, l, 156] = np.tile(np.asarray(inp["knorm_diff"][l]), 2)
        par[:, l, 157] = np.asarray(inp["diff_norm"][l])
        par[:, l, 158] = np.asarray(inp["qnorm_cross"][l])
        par[:, l, 159] = np.asarray(inp["knorm_cross"][l])
    par = par.reshape(128, DEPTH * NPS)
    lam = np.stack([np.stack([np.asarray(inp[n][l]) for n in ("lam_q1", "lam_k1", "lam_q2", "lam_k2")]) for l in range(DEPTH)]).astype(np.float32).reshape(1, -1)
    ck = f(inp["cache_k"]).reshape(DEPTH * NPOOL * 128, 512)
    cv = f(inp["cache_v"]).reshape(DEPTH * NPOOL * 128, 512)
    if DBG["small_cache"]:
        ck, cv = ck[:128], cv[:128]
    shared = {k: f(inp[k]) for k in ("w_in", "w_out", "w_cq", "w_ck", "w_cv", "w_co", "w_gate", "w_up", "w_down")}
    xp, xs, mem = f(inp["x_prompt"]), f(inp["x_sample"]), f(inp["mem_prompt"])
    pt = np.asarray(inp["page_table"]).astype(np.int32)
    sg, sgc = f(inp["state_gdn"]), f(inp["state_gdn_conv"])
    cmk, cmv_, sfc = f(inp["cache_mem_k"]), f(inp["cache_mem_v"]), f(inp["state_ffn_conv"])
    in_maps = []
    for c in range(NCORES):
        sl = slice(NS * c, NS * c + NS)
        m = dict(shared)
        m.update({
            "xT_in": np.ascontiguousarray(xp[c].T), "xsT_in": np.ascontiguousarray(xs[sl].reshape(TS, D).T),
            "memT_in": np.ascontiguousarray(mem[c].T), "cache_k": ck, "cache_v": cv,
            "ptab": np.ascontiguousarray(pt[sl].reshape(1, NS * NPAGES)),
            "st_gdn": np.ascontiguousarray(sg[:, sl]),
            "st_gconvT": np.ascontiguousarray(sgc[:, sl].transpose(0, 3, 1, 2)),
            "cmkT": np.ascontiguousarray(cmk[:, sl].transpose(0, 1, 3, 4, 2)),
            "cmv": np.ascontiguousarray(cmv_[:, sl].reshape(DEPTH, NS, 256, 512)),
            "st_fconvT": np.ascontiguousarray(sfc[:, sl].transpose(0, 3, 1, 2)),
            "par_in": par, "lam_in": lam, "con_in": con, "cs_in": cs, "smask_in": sm,
        })
        in_maps.append(m)
    res = run_bass_kernel_spmd(nc, in_maps, core_ids=list(range(NCORES))).results
    B, SB = NCORES, NCORES * NS
    y_p = np.stack([res[c]["yT_o"].T for c in range(B)])
    y_s = np.concatenate([res[c]["ysT_o"].T.reshape(NS, SL, D) for c in range(B)])
    p_k = np.stack([np.stack([res[c]["pkT_o"][l].T.reshape(SEQ, 8, 64) for c in range(B)]) for l in range(DEPTH)])
    p_v = np.stack([np.stack([res[c]["pv_o"][l].reshape(SEQ, 4, 128) for c in range(B)]) for l in range(DEPTH)])
    p_g = np.stack([np.stack([res[c]["pg_o"][l] for c in range(B)]) for l in range(DEPTH)])
    p_gc = np.stack([np.stack([res[c]["pgcT_o"][l].T for c in range(B)]) for l in range(DEPTH)])
    p_mk = np.stack([np.stack([res[c]["pmkT_o"][l].T.reshape(256, 4, 128) for c in range(B)]) for l in range(DEPTH)])
    p_mv = np.stack([np.stack([res[c]["pmv_o"][l].reshape(256, 4, 128) for c in range(B)]) for l in range(DEPTH)])
    p_fc = np.stack([np.stack([res[c]["pfcT_o"][l].T for c in range(B)]) for l in range(DEPTH)])
    s_k = np.stack([np.concatenate([res[c]["skT_o"][l].T.reshape(NS, SL, 8, 64) for c in range(B)]) for l in range(DEPTH)])
    s_v = np.stack([np.concatenate([res[c]["sv_o"][l].reshape(NS, SL, 4, 128) for c in range(B)]) for l in range(DEPTH)])
    s_g = np.stack([np.concatenate([res[c]["sg_o"][l] for c in range(B)]) for l in range(DEPTH)])
    s_gc = np.stack([np.concatenate([res[c]["sgcT_o"][l].transpose(1, 2, 0) for c in range(B)]) for l in range(DEPTH)])
    s_fc = np.stack([np.concatenate([res[c]["sfcT_o"][l].transpose(1, 2, 0) for c in range(B)]) for l in range(DEPTH)])
    outs = (y_p, y_s, p_k, p_v, p_g, p_gc, p_mk, p_mv, p_fc, s_k, s_v, s_g, s_gc, s_fc)
    return tuple(np.ascontiguousarray(o, dtype=np.float32) for o in outs)
```

```python
import math
import numpy as np
import concourse.bass as bass
import concourse.mybir as mybir
from concourse.bass_utils import run_bass_kernel_spmd

F32 = mybir.dt.float32
BF16 = mybir.dt.bfloat16
I32 = mybir.dt.int32
AF = mybir.ActivationFunctionType
ALU = mybir.AluOpType

NCORES = 8
D = 1024
KC = 8
SEQ = 2048
TSEG = 512
NSEG = 4
TT = 4
NS = 4
SL = 4
TS = NS * SL
DEPTH = 2
NPAGES = 64
NPOOL = 2560
DFF = 2816
FC = 22
INW = 3592
EPS = 1e-6
BIG = 1.0e30
C_Q, C_K, C_V, C_G, C_BA, C_DQ, C_DK, C_DV = 0, 512, 1024, 1536, 2048, 2056, 2568, 3080


class Tl:
    def __init__(self, t, n=1, excl=False):
        self.t = t
        self.n = n
        self.excl = excl
        self.lw = [None] * n
        self.rd = [[] for _ in range(n)]

    def __getitem__(self, k):
        return self.t[k]


class Al:
    def __init__(self, t, parents):
        self.t = t
        self.parents = parents
        self.n = 1

    def __getitem__(self, k):
        return self.t[k]


class Sched:
    ENG = ("pe", "act", "dve", "pool", "sp")

    def __init__(self, nc, ndma=40):
        self.nc = nc
        self.eng = {"pe": nc.tensor, "act": nc.scalar, "dve": nc.vector, "pool": nc.gpsimd, "sp": nc.sync}
        self.sem = {}
        self.cnt = {e: 0 for e in self.ENG}
        self.known = {e: {} for e in self.ENG}
        self.prog = {e: [] for e in self.ENG}
        self.dsem = []
        self.dcnt = []
        self.dlast = []
        self.dnext = 0
        self.ndma = ndma
        self.outdeps = []
        self.ninstr = 0
        self.dry = False

    def open(self, stack):
        for e in self.ENG:
            self.sem[e] = stack.enter_context(self.nc.semaphore("s_" + e))
        for i in range(self.ndma):
            self.dsem.append(stack.enter_context(self.nc.semaphore("d%d" % i)))
            self.dcnt.append(0)
            self.dlast.append(None)

    @staticmethod
    def _regs(lst):
        out = []
        for r in lst:
            if isinstance(r, Al):
                for p in r.parents:
                    out.extend((p, i) for i in range(p.n))
            elif isinstance(r, Tl):
                out.extend((r, i) for i in range(r.n))
            else:
                t, i = r
                if isinstance(t, Al):
                    for p in t.parents:
                        out.extend((p, i2) for i2 in range(p.n))
                elif isinstance(i, (list, tuple, range)):
                    out.extend((t, j) for j in i)
                else:
                    out.append((t, i))
        return out

    def _deps(self, reads, writes):
        deps = []
        for t, i in reads:
            if t.lw[i] is not None:
                deps.append(t.lw[i])
        for t, i in writes:
            if t.lw[i] is not None:
                deps.append(t.lw[i])
            deps.extend(t.rd[i])
        return deps

    def _waits(self, e, deps):
        need = {}
        for s, v in deps:
            if self.known[e].get(id(s), (None, 0))[1] < v:
                if need.get(id(s), (None, 0))[1] < v:
                    need[id(s)] = (s, v)
        for k, (s, v) in need.items():
            self.known[e][k] = (s, v)
        return list(need.values())

    def _mark(self, reads, writes, dep):
        for t, i in writes:
            t.lw[i] = dep
            t.rd[i] = []
        for t, i in reads:
            t.rd[i].append(dep)

    def op(self, e, fn, reads=(), writes=()):
        if self.dry:
            return
        reads = self._regs(reads)
        writes = self._regs(writes)
        writes = writes + [r for r in reads if r[0].excl]
        reads = [r for r in reads if not r[0].excl]
        waits = self._waits(e, self._deps(reads, writes))
        self.cnt[e] += 1
        c = self.cnt[e]
        sem = self.sem[e]
        engobj = self.eng[e]

        def thunk():
            for s, v in waits:
                engobj.wait_ge(s, v)
            ins = fn(engobj)
            ins.then_inc(sem, 1)
        thunk()
        self._mark(reads, writes, (sem, c))
        self.ninstr += 1

    def dma(self, q, out_ap, in_ap, reads=(), writes=(), is_output=False, indirect=None):
        if self.dry:
            return
        reads = self._regs(reads)
        writes = self._regs(writes)
        deps = self._deps(reads, writes)
        k = self.dnext
        self.dnext = (self.dnext + 1) % self.ndma
        if self.dlast[k] is not None:
            deps.append(self.dlast[k])
        waits = self._waits(q, deps)
        self.dcnt[k] += 16
        s = self.dsem[k]
        v = self.dcnt[k]
        engobj = self.eng[q]

        def thunk():
            for ws, wv in waits:
                engobj.wait_ge(ws, wv)
            if indirect is None:
                ins = engobj.dma_start(out=out_ap, in_=in_ap)
            else:
                ins = engobj.indirect_dma_start(out=out_ap, out_offset=None, in_=in_ap,
                                                in_offset=bass.IndirectOffsetOnAxis(indirect, 0))
            ins.then_inc(s, 16)
        thunk()
        dep = (s, v)
        self.dlast[k] = dep
        self._mark(reads, writes, dep)
        if is_output:
            self.outdeps.append(dep)
        self.ninstr += 1

    def finish(self):
        alld = list(self.outdeps) + [d for d in self.dlast if d is not None] + [(self.sem[e], self.cnt[e]) for e in self.ENG if self.cnt[e] > 0]
        waits = self._waits("sp", alld)
        engobj = self.eng["sp"]

        def thunk():
            for s, v in waits:
                engobj.wait_ge(s, v)
        thunk()

    def emit(self, block):
        progs = self.prog

        @block.tensor
        def _(e):
            for th in progs["pe"]:
                th()

        @block.scalar
        def _(e):
            for th in progs["act"]:
                th()

        @block.vector
        def _(e):
            for th in progs["dve"]:
                th()

        @block.gpsimd
        def _(e):
            for th in progs["pool"]:
                th()

        @block.sync
        def _(e):
            for th in progs["sp"]:
                th()


DBG = {"segs": NSEG, "layers": DEPTH, "phase": 99, "small_cache": False, "sub": 99}


class _Stop(Exception):
    pass


def build_program(stage=99):
    from contextlib import ExitStack
    nc = bass.Bass("TRN2", target_bir_lowering=False)
    S = Sched(nc)

    def din(name, shape, dt=F32):
        return nc.dram_tensor(name, list(shape), dt, kind="ExternalInput").ap()

    def dout(name, shape, dt=F32):
        return nc.dram_tensor(name, list(shape), dt, kind="ExternalOutput").ap()

    xT_in = din("xT_in", [D, SEQ])
    xsT_in = din("xsT_in", [D, TS])
    memT_in = din("memT_in", [D, 256])
    CR = 128 if DBG["small_cache"] else DEPTH * NPOOL * 128
    cache_k = din("cache_k", [CR, 512])
    cache_v = din("cache_v", [CR, 512])
    ptab = din("ptab", [1, NS * NPAGES], I32)
    st_gdn = din("st_gdn", [DEPTH, NS, 4, 128, 128])
    st_gconvT = din("st_gconvT", [DEPTH, 1536, NS, 3])
    cmkT = din("cmkT", [DEPTH, NS, 4, 128, 256])
    cmv = din("cmv", [DEPTH, NS, 256, 512])
    st_fconvT = din("st_fconvT", [DEPTH, DFF, NS, 2])
    w_in = din("w_in", [DEPTH, D, INW])
    w_out = din("w_out", [DEPTH, D, D])
    w_cq = din("w_cq", [DEPTH, D, 512])
    w_ck = din("w_ck", [DEPTH, D, 512])
    w_cv = din("w_cv", [DEPTH, D, 512])
    w_co = din("w_co", [DEPTH, 512, D])
    w_gate = din("w_gate", [DEPTH, D, DFF])
    w_up = din("w_up", [DEPTH, D, DFF])
    w_down = din("w_down", [DEPTH, DFF, D])
    NPS = 8 * 4 + 12 * 4 + 22 * 3 + 4 + 4 + 7
    par_in = din("par_in", [128, DEPTH * NPS])
    lam_in = din("lam_in", [1, DEPTH * 4 * 64])
    NCON = 128 * 11
    con_in = din("con_in", [128, NCON])
    cs_in = din("cs_in", [128, 2, SEQ + SL])
    smask_in = din("smask_in", [128, 32])

    yT_o = dout("yT_o", [D, SEQ])
    ysT_o = dout("ysT_o", [D, TS])
    pkT_o = dout("pkT_o", [DEPTH, 512, SEQ])
    pv_o = dout("pv_o", [DEPTH, SEQ, 512])
    pg_o = dout("pg_o", [DEPTH, 4, 128, 128])
    pgcT_o = dout("pgcT_o", [DEPTH, 1536, 3])
    pmkT_o = dout("pmkT_o", [DEPTH, 512, 256])
    pmv_o = dout("pmv_o", [DEPTH, 256, 512])
    pfcT_o = dout("pfcT_o", [DEPTH, DFF, 2])
    skT_o = dout("skT_o", [DEPTH, 512, TS])
    sv_o = dout("sv_o", [DEPTH, NS, SL, 512])
    sg_o = dout("sg_o", [DEPTH, NS, 4, 128, 128])
    sgcT_o = dout("sgcT_o", [DEPTH, 1536, NS, 3])
    sfcT_o = dout("sfcT_o", [DEPTH, DFF, NS, 2])

    es = ExitStack()
    with es:
        S.open(es)
        _uid = [0]

        def sb(shape, dt=F32, n=1, name=None):
            _uid[0] += 1
            return Tl(nc.alloc_sbuf_tensor(name or ("t%d" % _uid[0]), list(shape), dt), n)

        TM = TSEG + TS
        xT = sb([128, KC, TM], F32, n=KC, name="xT")
        hT = sb([128, KC, TM], BF16, n=KC, name="hT")
        NSLOT = 3
        wsl = [sb([128, 8, 512], BF16, name="wsl%d" % i) for i in range(NSLOT)]
        par = sb([128, DEPTH * NPS], F32, name="par")
        con = sb([128, NCON], F32, name="con")
        conb = sb([128, NCON], BF16, name="conb")
        lam_t = sb([1, DEPTH * 4 * 64], F32, name="lam_t")
        lamv = sb([128, 2 * DEPTH], F32, name="lamv")
        smask = sb([128, 32], F32, name="smask")
        idx_t = sb([128, NS * NPAGES], I32, name="idx_t")
        kd_st = [sb([128, 4, (NSEG - 1) * TSEG], BF16, name="kdst%d" % l) for l in range(DEPTH)]
        v_st = [sb([128, (NSEG - 1) * TT, 512], BF16, name="vst%d" % l) for l in range(DEPTH)]
        S32 = [sb([128, 4, 128], F32, name="S32_%d" % l) for l in range(DEPTH)]
        gtail = [sb([128, 12, 3], F32, name="gtail%d" % l) for l in range(DEPTH)]
        ftail = [sb([128, FC, 2], F32, name="ftail%d" % l) for l in range(DEPTH)]
        PS = [Tl(nc.alloc_psum_tensor("ps%d" % i, [128, 512], F32), excl=True) for i in range(8)]

        def cf(k, rows=128, cols=128):
            return con[0:rows, k * 128:k * 128 + cols]

        def cb(k, rows=128, cols=128):
            return conb[0:rows, k * 128:k * 128 + cols]
        C_ID, C_ONE, C_UTRI, C_MPOS, C_STRICT, C_TRI, C_ROT, C_BLK, C_MISC, C_O128, C_O1024 = range(11)

        S.dma("sp", con[:, :], con_in[:, :], writes=[con])
        S.dma("sp", par[:, :], par_in[:, :], writes=[par])
        S.dma("sp", lam_t[:, :], lam_in[:, :], writes=[lam_t])
        S.dma("sp", smask[:, :], smask_in[:, :], writes=[smask])
        S.op("dve", lambda e: e.tensor_copy(out=conb[:, :], in_=con[:, :]), reads=[con], writes=[conb])

        def pcol(l, off, n=1):
            return par[:, l * NPS + off:l * NPS + off + n]
        P_NMIX, P_NCROSS, P_NMEM, P_NFFN = 0, 8, 16, 24
        P_CQKV = 32
        P_CFFN = 32 + 48
        P_ALOG = P_CFFN + 66
        P_DTB = P_ALOG + 4
        P_GDNN, P_QND, P_KND, P_DIFFN, P_QNC, P_KNC = [P_DTB + 4 + i for i in range(6)]
        P_SP = P_DTB + 4 + 6

        negA = sb([128, DEPTH * 4], F32, name="negA")
        for l in range(DEPTH):
            S.op("act", lambda e, l=l: e.activation(out=negA[:, 4 * l:4 * l + 4], in_=pcol(l, P_ALOG, 4), func=AF.Exp),
                 reads=[par], writes=[negA])
        S.op("dve", lambda e: e.tensor_scalar(out=negA[:, :], in0=negA[:, :], scalar1=-1.0, scalar2=None, op0=ALU.mult),
             reads=[negA], writes=[negA])
        lprod = sb([1, DEPTH * 2 * 64], F32, name="lprod")
        lsum = sb([1, DEPTH * 2], F32, name="lsum")
        lam1 = sb([1, DEPTH], F32, name="lam1")
        for l in range(DEPTH):
            for j in range(2):
                o = (l * 4 + 2 * j) * 64
                S.op("dve", lambda e, o=o, l=l, j=j: e.tensor_tensor(
                    out=lprod[0:1, (l * 2 + j) * 64:(l * 2 + j + 1) * 64], in0=lam_t[0:1, o:o + 64],
                    in1=lam_t[0:1, o + 64:o + 128], op=ALU.mult), reads=[lam_t], writes=[lprod])
                S.op("dve", lambda e, l=l, j=j: e.reduce_sum(
                    out=lsum[0:1, l * 2 + j:l * 2 + j + 1], in_=lprod[0:1, (l * 2 + j) * 64:(l * 2 + j + 1) * 64],
                    axis=mybir.AxisListType.X), reads=[lprod], writes=[lsum])
        S.op("act", lambda e: e.activation(out=lsum[0:1, :], in_=lsum[0:1, :], func=AF.Exp), reads=[lsum], writes=[lsum])
        for l in range(DEPTH):
            lam_init = 0.8 - 0.6 * math.exp(-0.3 * l)
            S.op("dve", lambda e, l=l, li=lam_init: e.scalar_tensor_tensor(
                out=lam1[0:1, l:l + 1], in0=lsum[0:1, 2 * l + 1:2 * l + 2], scalar=-li, in1=lsum[0:1, 2 * l:2 * l + 1],
                op0=ALU.add, op1=ALU.subtract), reads=[lsum], writes=[lam1])
        S.op("pe", lambda e: e.matmul(PS[7][0:128, 0:DEPTH], lhsT=con[0:1, 128:256], rhs=lam1[0:1, 0:DEPTH], start=True, stop=True),
             reads=[con, lam1], writes=[PS[7]])
        S.op("dve", lambda e: e.tensor_copy(out=lamv[:, 0:DEPTH], in_=PS[7][0:128, 0:DEPTH]), reads=[PS[7]], writes=[lamv])

        idx_f = sb([128, NS * NPAGES], F32, name="idx_f")
        idx_l = [sb([128, NS * NPAGES], I32, name="idxl%d" % l) for l in range(DEPTH)]
        S.dma("sp", idx_t[:, :], ptab[0:1, :].partition_broadcast(128), writes=[idx_t])
        S.op("dve", lambda e: e.tensor_copy(out=idx_f[:, :], in_=idx_t[:, :]), reads=[idx_t], writes=[idx_f])
        for l in range(DEPTH):
            S.op("dve", lambda e, l=l: e.tensor_scalar(
                out=idx_f[:, :] if False else idx_l[l][:, :], in0=idx_f[:, :], scalar1=128.0, scalar2=con[:, C_MISC * 128 + l:C_MISC * 128 + l + 1],
                op0=ALU.mult, op1=ALU.add), reads=[idx_f, con], writes=[idx_l[l]])

        AX = mybir.AxisListType.X
        wrot = [0]

        WD = {"w_in": w_in, "w_out": w_out, "w_cq": w_cq, "w_ck": w_ck, "w_cv": w_cv, "w_co": w_co,
              "w_gate": w_gate, "w_up": w_up, "w_down": w_down}
        wreq = []
        wst = {"pos": 0, "issued": {}, "l": 0}

        def w_issue(i):
            key, k0, nk, c0, ncol, _ = wreq[i]
            src2d = WD[key][wst["l"]]
            sl = wsl[wrot[0] % NSLOT]
            wrot[0] += 1
            S.dma("pool", sl[:, 0:nk, 0:ncol],
                  src2d[k0 * 128:(k0 + nk) * 128, c0:c0 + ncol].rearrange("(kc p) n -> p kc n", p=128),
                  writes=[sl])
            wst["issued"][i] = sl

        def wload(key, k0, nk, c0, ncol, nopf=False):
            if S.dry:
                wreq.append((key, k0, nk, c0, ncol, nopf))
                return wsl[0]
            i = wst["pos"]
            wst["pos"] += 1
            assert wreq[i][:5] == (key, k0, nk, c0, ncol), (wreq[i], key, k0, nk, c0, ncol)
            if i not in wst["issued"]:
                w_issue(i)
            sl = wst["issued"].pop(i)
            if not nopf and i + 1 < len(wreq):
                w_issue(i + 1)
            return sl

        def blocks_of(T):
            b = []
            t = 0
            while t < T:
                n = min(512, T - t)
                b.append((t, n))
                t += n
            return b

        def barrier():
            if S.dry:
                return
            deps = [(S.sem[e], S.cnt[e]) for e in S.ENG if S.cnt[e] > 0]
            deps += [d for d in S.dlast if d is not None]
            for e in S.ENG:
                waits = S._waits(e, deps)
                eo = S.eng[e]

                def th(waits=waits, eo=eo):
                    for s_, v_ in waits:
                        eo.wait_ge(s_, v_)
                th()

        scr = [sb([128, 512], F32, name="scr%d" % i) for i in range(6)]
        scrb = [sb([128, 512], BF16, name="scrb%d" % i) for i in range(4)]
        srot = [0]
        sbrot = [0]

        def nscr():
            srot[0] += 1
            return scr[srot[0] % len(scr)]

        def nscrb():
            sbrot[0] += 1
            return scrb[sbrot[0] % len(scrb)]
        psrot = [0]

        def nps(lo=0, hi=8):
            psrot[0] += 1
            return PS[lo + psrot[0] % (hi - lo)]

        def ACT(fn, reads, writes):
            S.op("act", fn, reads=reads, writes=writes)

        def DVE(fn, reads, writes):
            S.op("dve", fn, reads=reads, writes=writes)

        def POOL(fn, reads, writes):
            S.op("pool", fn, reads=reads, writes=writes)

        def PE(fn, reads, writes):
            S.op("pe", fn, reads=reads, writes=writes)

        def stats(srcs, n, ones_blk, psb):
            m = len(srcs)
            for i, (a, reg) in enumerate(srcs):
                sq = nscrb()
                ACT(lambda e, sq=sq, a=a: e.activation(out=sq[:, 0:n], in_=a, func=AF.Square), [reg], [sq])
                PE(lambda e, sq=sq, i=i: e.matmul(psb[:, 0:n], lhsT=cb(ones_blk), rhs=sq[:, 0:n], start=(i == 0), stop=(i == m - 1)),
                   [sq, conb], [psb])
            ta = nscr()
            tr = nscr()
            ACT(lambda e: e.activation(out=ta[:, 0:n], in_=psb[:, 0:n], func=AF.Sqrt, bias=con[:, C_MISC * 128 + 8:C_MISC * 128 + 9], scale=1.0),
                [psb, con], [ta])
            DVE(lambda e: e.reciprocal(out=tr[:, 0:n], in_=ta[:, 0:n]), [ta], [tr])
            return tr

        def rmsnorm_x(src, l, gcol, T, dst, nreg=True):
            for (t0, n) in blocks_of(T):
                tr = stats([(src[:, kc, t0:t0 + n], (src, kc)) for kc in range(KC)], n, C_O1024, nps(4, 6))
                for kc in range(KC):
                    DVE(lambda e, kc=kc, t0=t0, n=n, tr=tr: e.scalar_tensor_tensor(
                        out=dst[:, kc, t0:t0 + n], in0=src[:, kc, t0:t0 + n], scalar=pcol(l, gcol + kc),
                        in1=tr[:, 0:n], op0=ALU.mult, op1=ALU.mult),
                        [(src, kc), tr, par], [(dst, kc)])

        pjrot = [0]

        def proj_fm(slots, ccs, rhs_fn, rhs_regs, blks, handler, after=None, delay=2):
            nktot = sum(nk for _, nk in slots)
            pend = []

            def run(item):
                cc, t0, n, ps, last = item
                handler(cc, t0, n, ps)
                if last and after is not None:
                    after(cc)
            for cc in ccs:
                for bi_, (t0, n) in enumerate(blks):
                    pjrot[0] += 1
                    ps = PS[pjrot[0] % 4]

                    def mm(e, ps=ps, cc=cc, t0=t0, n=n):
                        ins = None
                        kk = 0
                        for sl, nk in slots:
                            for k in range(nk):
                                ins = e.matmul(ps[:, 0:n], lhsT=sl[:, k, cc * 128:(cc + 1) * 128], rhs=rhs_fn(kk, t0, n),
                                               start=(kk == 0), stop=(kk == nktot - 1))
                                kk += 1
                        return ins
                    PE(mm, [sl for sl, _ in slots] + rhs_regs, [ps])
                    pend.append((cc, t0, n, ps, bi_ == len(blks) - 1))
                    if len(pend) > delay:
                        run(pend.pop(0))
            while pend:
                run(pend.pop(0))

        BE = 4 * TM
        arena = nc.alloc_sbuf_tensor("arena", [128, 7 * BE], BF16)

        def bview(o):
            return arena[:, o:o + BE].rearrange("p (h t) -> p h t", h=4)
        B0 = Tl(bview(0), 4)
        B1 = Tl(bview(BE), 4)
        B2 = Tl(bview(2 * BE), 4)
        B3 = Tl(bview(3 * BE), 4)
        OF = Tl(arena[:, 4 * BE:6 * BE].bitcast(F32).rearrange("p (h t) -> p h t", h=4), 4)
        OD = Tl(bview(6 * BE), 4)
        OG = B0
        actT = Al(arena[:, 0:FC * TM].rearrange("p (f t) -> p f t", f=FC), [B0, B1, B2, B3, OF, OD])
        cst = Al(arena[:, 3 * BE:4 * BE].bitcast(F32).rearrange("p (c t) -> p c t", c=2), [B3])
        vloc = Al(arena[:, 2 * BE:2 * BE + 2048].rearrange("p (a f) -> p a f", a=4), [B2])
        memx = Al(arena[:, 0:4096].bitcast(F32).rearrange("p (k n) -> p k n", k=8), [B0, B1])
        mnT = Al(arena[:, 4 * BE:4 * BE + 2048].rearrange("p (k n) -> p k n", k=8), [OF])
        mkT = Al(arena[:, 4 * BE + 2048:4 * BE + 3072].rearrange("p (k n) -> p k n", k=4), [OF])
        mv = Al(arena[:, 4 * BE + 3072:4 * BE + 4096].rearrange("p (k n) -> p k n", k=2), [OF])
        vs_tok = sb([128, NS, 512], BF16, name="vs_tok")
        mkTs = sb([128, 4, 256], BF16, name="mkTs")
        mvs = sb([128, 2, 512], BF16, name="mvs")
        raw = [sb([128, 3 + TSEG], F32, name="raw%d" % i) for i in range(1)]
        sraw = [sb([128, NS, 8], F32, name="sraw%d" % i) for i in range(1)]
        acc = [sb([128, TM], F32, name="acc%d" % i) for i in range(1)]
        betaT = sb([128, TT + 1, 4], F32, name="betaT")
        gT = sb([128, TT + 1, 4], F32, name="gT")
        betaS = sb([128, NS, 4], F32, name="betaS")
        gS = sb([128, NS, 4], F32, name="gS")
        Ssm = sb([128, 4, 128], F32, name="Ssm")
        Sbf = sb([128, 4, 128], BF16, name="Sbf")
        kpage = [sb([128, 512], BF16, name="kpage%d" % i) for i in range(2)]
        vpage = [sb([128, 512], BF16, name="vpage%d" % i) for i in range(2)]
        KTp = [sb([128, 4, 128], BF16, name="KTp%d" % i) for i in range(2)]
        gA = [sb([128, 4, 128], F32, name="gA%d" % i) for i in range(6)]
        gB = [sb([128, 4, 128], BF16, name="gB%d" % i) for i in range(10)]
        gsm = [sb([128, 16], F32, name="gsm%d" % i) for i in range(8)]
        Qpad = sb([128, 4, 8], BF16, name="Qpad")
        S.op("dve", lambda e: e.memset(Qpad[:, :, :], 0.0), reads=[], writes=[Qpad])

        ISQ = 128.0 ** -0.5

        def gdn_step(C, I, qa, ka, va, regs_in, beta_ap, g_ap, Sst, oa, o_regs, q3=None, S3=None, o3=None, beta2=None, g2=None):
            IC = I * C
            gsc = gsm[0]; negb = gsm[1]; gcs = gsm[2]; eg = gsm[3]; bg = gsm[4]; egl = gsm[5]; glt = gsm[6]
            GU, TMPD, Dm, N_, NT, TT = gA[0:6]
            P2, PT2 = GU, TMPD
            AQ, AQT, QG, KBG, KG, VB, NW, VN, DS, TTb = gB[0:10]

            def v3(t, rows=C, w=C):
                return t[0:rows, 0:I, 0:w]

            def pv(ps, rows=C, w=C):
                return ps[0:rows, 0:I * w].rearrange("p (i c) -> p i c", i=I)
            if g2 is not None:
                DVE(lambda e: e.tensor_copy(out=gsc[0:C, 0:I], in_=g2), regs_in, [gsc])
                DVE(lambda e: e.tensor_scalar(out=negb[0:C, 0:I], in0=beta2, scalar1=-1.0, scalar2=None, op0=ALU.mult), regs_in, [negb])
            else:
                for it in range(I):
                    DVE(lambda e, it=it: e.tensor_copy(out=gsc[0:C, it:it + 1], in_=g_ap(it)), regs_in, [gsc])
                    DVE(lambda e, it=it: e.tensor_scalar(out=negb[0:C, it:it + 1], in0=beta_ap(it), scalar1=-1.0, scalar2=None, op0=ALU.mult),
                        regs_in, [negb])
            p_gc, p_gcb, p_kk, p_qk = PS[7], PS[0], PS[1], PS[2]
            PE(lambda e: e.matmul(p_gc[0:C, 0:I], lhsT=cf(C_UTRI, C, C), rhs=gsc[0:C, 0:I], start=True, stop=True), [con, gsc], [p_gc])
            DVE(lambda e: e.tensor_copy(out=gcs[0:C, 0:I], in_=p_gc[0:C, 0:I]), [p_gc], [gcs])
            for it in range(I):
                DVE(lambda e, it=it: e.tensor_scalar(out=GU[0:C, it, 0:C], in0=cf(C_UTRI, C, C), scalar1=gsc[0:C, it:it + 1], scalar2=None, op0=ALU.mult),
                    [con, gsc], [GU])
            for it in range(I):
                PE(lambda e, it=it: e.matmul(p_gcb[:, it * C:(it + 1) * C], lhsT=cf(C_ONE, C, 128), rhs=GU[0:C, it, 0:C], start=True, stop=True),
                   [con, GU], [p_gcb])
            for it in range(I):
                DVE(lambda e, it=it: e.scalar_tensor_tensor(out=TMPD[0:C, it, 0:C], in0=p_gcb[0:C, it * C:(it + 1) * C], scalar=gcs[0:C, it:it + 1],
                                                            in1=cf(C_MPOS, C, C), op0=ALU.subtract, op1=ALU.add), [p_gcb, gcs, con], [TMPD])
            ACT(lambda e: e.activation(out=v3(Dm), in_=v3(TMPD), func=AF.Exp, scale=-1.0), [TMPD], [Dm])
            for it in range(I):
                POOL(lambda e, it=it: e.tensor_tensor(out=DS[0:C, it, 0:C], in0=Dm[0:C, it, 0:C], in1=cf(C_STRICT, C, C), op=ALU.mult), [Dm, con], [DS])
            for it in range(I):
                PE(lambda e, it=it: e.matmul(p_kk[0:C, it * C:(it + 1) * C], lhsT=ka(it), rhs=ka(it), start=True, stop=True), regs_in, [p_kk])
                PE(lambda e, it=it: e.matmul(p_qk[0:C, it * C:(it + 1) * C], lhsT=qa(it), rhs=ka(it), start=True, stop=True), regs_in, [p_qk])
            for it in range(I):
                DVE(lambda e, it=it: e.scalar_tensor_tensor(out=N_[0:C, it, 0:C], in0=p_kk[0:C, it * C:(it + 1) * C], scalar=negb[0:C, it:it + 1],
                                                            in1=DS[0:C, it, 0:C], op0=ALU.mult, op1=ALU.mult), [p_kk, negb, DS], [N_])
            DVE(lambda e: e.tensor_tensor(out=v3(AQ), in0=pv(p_qk), in1=v3(Dm), op=ALU.mult), [p_qk, Dm], [AQ])
            p_nt, p_aqt = PS[3], PS[4]
            for it in range(I):
                PE(lambda e, it=it: e.matmul(p_nt[0:C, it * C:(it + 1) * C], lhsT=N_[0:C, it, 0:C], rhs=cf(C_ID, C, C), start=True, stop=True), [N_, con], [p_nt])
                PE(lambda e, it=it: e.matmul(p_aqt[0:C, it * C:(it + 1) * C], lhsT=AQ[0:C, it, 0:C], rhs=cb(C_ID, C, C), start=True, stop=True), [AQ, conb], [p_aqt])
            ACT(lambda e: e.activation(out=v3(NT), in_=pv(p_nt), func=AF.Copy), [p_nt], [NT])
            ACT(lambda e: e.activation(out=v3(AQT), in_=pv(p_aqt), func=AF.Copy), [p_aqt], [AQT])
            for it in range(I):
                DVE(lambda e, it=it: e.tensor_tensor(out=TT[0:C, it, 0:C], in0=p_nt[0:C, it * C:(it + 1) * C], in1=cf(C_ID, C, C), op=ALU.add), [p_nt, con], [TT])
            nlev = int(round(math.log2(C)))
            Pk, Ptk = N_, NT
            Pn, Ptn = P2, PT2
            for k in range(nlev):
                if k >= 1:
                    p_d = PS[5]
                    for it in range(I):
                        PE(lambda e, it=it, Pk=Pk: e.matmul(p_d[0:C, it * C:(it + 1) * C], lhsT=Pk[0:C, it, 0:C], rhs=TT[0:C, it, 0:C], start=True, stop=True),
                           [Pk, TT], [p_d])
                    DVE(lambda e: e.tensor_tensor(out=v3(TT), in0=pv(p_d), in1=v3(TT), op=ALU.add), [p_d, TT], [TT])
                if k <= nlev - 2:
                    p_a, p_b = PS[6], PS[7]
                    for it in range(I):
                        PE(lambda e, it=it, Pk=Pk, Ptk=Ptk: e.matmul(p_a[0:C, it * C:(it + 1) * C], lhsT=Ptk[0:C, it, 0:C], rhs=Pk[0:C, it, 0:C], start=True, stop=True),
                           [Pk, Ptk], [p_a])
                        PE(lambda e, it=it, Pk=Pk, Ptk=Ptk: e.matmul(p_b[0:C, it * C:(it + 1) * C], lhsT=Pk[0:C, it, 0:C], rhs=Ptk[0:C, it, 0:C], start=True, stop=True),
                           [Pk, Ptk], [p_b])
                    ACT(lambda e, Pn=Pn: e.activation(out=v3(Pn), in_=pv(p_a), func=AF.Copy), [p_a], [Pn])
                    DVE(lambda e, Ptn=Ptn: e.tensor_copy(out=v3(Ptn), in_=pv(p_b)), [p_b], [Ptn])
                    Pk, Ptk, Pn, Ptn = Pn, Ptn, Pk, Ptk
                    if Pn is N_:
                        Pn, Ptn = N_, NT
            ACT(lambda e: e.activation(out=v3(TTb), in_=v3(TT), func=AF.Copy), [TT], [TTb])
            ACT(lambda e: e.activation(out=eg[0:C, 0:I], in_=gcs[0:C, 0:I], func=AF.Exp), [gcs], [eg])
            if beta2 is not None:
                DVE(lambda e: e.tensor_tensor(out=bg[0:C, 0:I], in0=eg[0:C, 0:I], in1=beta2, op=ALU.mult), [eg] + regs_in, [bg])
            else:
                for it in range(I):
                    DVE(lambda e, it=it: e.tensor_tensor(out=bg[0:C, it:it + 1], in0=eg[0:C, it:it + 1], in1=beta_ap(it), op=ALU.mult), [eg] + regs_in, [bg])
            DVE(lambda e: e.tensor_tensor(out=egl[0:C, 0:I], in0=pv(p_gcb)[:, :, C - 1], in1=gcs[0:C, 0:I], op=ALU.subtract), [p_gcb, gcs], [egl])
            ACT(lambda e: e.activation(out=glt[:, 0:I], in_=pv(p_gcb, 128)[:, :, C - 1], func=AF.Exp), [p_gcb], [glt])
            ACT(lambda e: e.activation(out=egl[0:C, 0:I], in_=egl[0:C, 0:I], func=AF.Exp), [egl], [egl])
            EGB = GU
            ACT(lambda e: e.activation(out=v3(EGB, 128), in_=pv(p_gcb, 128), func=AF.Exp), [p_gcb], [EGB])
            if q3 is not None:
                DVE(lambda e: e.tensor_tensor(out=v3(QG, 128), in0=q3, in1=v3(EGB, 128), op=ALU.mult), regs_in + [EGB], [QG])
            else:
                for it in range(I):
                    DVE(lambda e, it=it: e.tensor_tensor(out=QG[:, it, 0:C], in0=qa(it), in1=EGB[:, it, 0:C], op=ALU.mult), regs_in + [EGB], [QG])
            p_kt, p_vt = PS[1], PS[2]
            for it in range(I):
                PE(lambda e, it=it: e.matmul(p_kt[0:C, it * 128:(it + 1) * 128], lhsT=ka(it), rhs=cb(C_ID), start=True, stop=True), regs_in + [conb], [p_kt])
                PE(lambda e, it=it: e.matmul(p_vt[0:C, it * 128:(it + 1) * 128], lhsT=va(it), rhs=cb(C_ID), start=True, stop=True), regs_in + [conb], [p_vt])
            for it in range(I):
                DVE(lambda e, it=it: e.tensor_scalar(out=KBG[0:C, it, :], in0=p_kt[0:C, it * 128:(it + 1) * 128], scalar1=bg[0:C, it:it + 1], scalar2=None, op0=ALU.mult), [p_kt, bg], [KBG])
                DVE(lambda e, it=it: e.tensor_scalar(out=KG[0:C, it, :], in0=p_kt[0:C, it * 128:(it + 1) * 128], scalar1=egl[0:C, it:it + 1], scalar2=None, op0=ALU.mult), [p_kt, egl], [KG])
                DVE(lambda e, it=it: e.tensor_scalar(out=VB[0:C, it, :], in0=p_vt[0:C, it * 128:(it + 1) * 128], scalar1=beta_ap(it), scalar2=None, op0=ALU.mult), [p_vt] + regs_in, [VB])
            p_w = PS[3]
            for it in range(I):
                PE(lambda e, it=it: e.matmul(p_w[:, it * C:(it + 1) * C], lhsT=KBG[0:C, it, :], rhs=TTb[0:C, it, 0:C], start=True, stop=True), [KBG, TTb], [p_w])
            ACT(lambda e: e.activation(out=v3(NW, 128), in_=pv(p_w, 128), func=AF.Copy, scale=-1.0), [p_w], [NW])
            if S3 is not None:
                ACT(lambda e: e.activation(out=Sbf[:, 0:I, :], in_=S3, func=AF.Copy), o_regs, [Sbf])
            else:
                for it in range(I):
                    ACT(lambda e, it=it: e.activation(out=Sbf[:, it, :], in_=Sst(it), func=AF.Copy), o_regs, [Sbf])
            p_vn, p_o, p_s = PS[4], PS[5], PS[6]
            for it in range(I):
                def f(e, it=it):
                    e.matmul(p_vn[0:C, it * 128:(it + 1) * 128], lhsT=TTb[0:C, it, 0:C], rhs=VB[0:C, it, :], start=True, stop=False)
                    return e.matmul(p_vn[0:C, it * 128:(it + 1) * 128], lhsT=NW[:, it, 0:C], rhs=Sbf[:, it, :], start=False, stop=True)
                PE(f, [TTb, VB, NW, Sbf], [p_vn])
            ACT(lambda e: e.activation(out=VN[0:C, 0:I, :], in_=pv(p_vn, C, 128), func=AF.Copy), [p_vn], [VN])
            for it in range(I):
                def f2(e, it=it):
                    e.matmul(p_o[:, it * C:(it + 1) * C], lhsT=Sbf[:, it, :], rhs=QG[:, it, 0:C], start=True, stop=False)
                    return e.matmul(p_o[:, it * C:(it + 1) * C], lhsT=VN[0:C, it, :], rhs=AQT[0:C, it, 0:C], start=False, stop=True)
                PE(f2, [Sbf, QG, VN, AQT], [p_o])
                PE(lambda e, it=it: e.matmul(p_s[:, it * 128:(it + 1) * 128], lhsT=KG[0:C, it, :], rhs=VN[0:C, it, :], start=True, stop=True), [KG, VN], [p_s])
            if o3 is not None:
                ACT(lambda e: e.activation(out=o3, in_=pv(p_o, 128), func=AF.Copy), [p_o], o_regs[1:])
            for it in range(I):
                if o3 is None:
                    DVE(lambda e, it=it: e.tensor_copy(out=oa(it), in_=p_o[:, it * C:(it + 1) * C]), [p_o], o_regs[1:])
                DVE(lambda e, it=it: e.scalar_tensor_tensor(out=Sst(it), in0=Sst(it), scalar=glt[:, it:it + 1], in1=p_s[:, it * 128:(it + 1) * 128],
                                                            op0=ALU.mult, op1=ALU.add), [glt, p_s], o_regs[0:1])

        def attn_combine(Ops, Lps, ncols, dst_ap, dst_reg, lcol=None):
            r0 = nscr()
            DVE(lambda e: e.reciprocal(out=r0[:, 0:ncols], in_=Lps[0][:, 0:ncols]), [Lps[0]], [r0])
            t0_ = nscr()
            DVE(lambda e: e.tensor_tensor(out=t0_[:, 0:ncols], in0=Ops[0][:, 0:ncols], in1=r0[:, 0:ncols], op=ALU.mult), [Ops[0], r0], [t0_])
            if lcol is None:
                ACT(lambda e: e.activation(out=dst_ap, in_=t0_[:, 0:ncols], func=AF.Copy), [t0_], [dst_reg])
                return
            r1 = nscr()
            DVE(lambda e: e.reciprocal(out=r1[:, 0:ncols], in_=Lps[1][:, 0:ncols]), [Lps[1]], [r1])
            t1_ = nscr()
            DVE(lambda e: e.tensor_tensor(out=t1_[:, 0:ncols], in0=Ops[1][:, 0:ncols], in1=r1[:, 0:ncols], op=ALU.mult), [Ops[1], r1], [t1_])
            DVE(lambda e: e.scalar_tensor_tensor(out=dst_ap, in0=t1_[:, 0:ncols], scalar=lcol, in1=t0_[:, 0:ncols], op0=ALU.mult, op1=ALU.add),
                [t1_, t0_, lamv], [dst_reg])

        def headnorm(src, dst, l, gcol, T, extra=None, post_scale=1.0):
            for h in range(4):
                for (t0, n) in blocks_of(T):
                    tr = stats([(src[:, h, t0:t0 + n], (src, h))], n, C_O128, nps(4, 6))
                    if extra is None:
                        DVE(lambda e, h=h, t0=t0, n=n, tr=tr: e.scalar_tensor_tensor(out=dst[:, h, t0:t0 + n], in0=src[:, h, t0:t0 + n], scalar=pcol(l, gcol),
                                                                                    in1=tr[:, 0:n], op0=ALU.mult, op1=ALU.mult), [(src, h), tr, par], [(dst, h)])
                        if post_scale != 1.0:
                            DVE(lambda e, h=h, t0=t0, n=n: e.tensor_scalar(out=dst[:, h, t0:t0 + n], in0=dst[:, h, t0:t0 + n], scalar1=post_scale, scalar2=None, op0=ALU.mult),
                                [(dst, h)], [(dst, h)])
                    else:
                        tq = nscr()
                        DVE(lambda e, h=h, t0=t0, n=n, tr=tr, tq=tq: e.scalar_tensor_tensor(out=tq[:, 0:n], in0=src[:, h, t0:t0 + n], scalar=pcol(l, gcol),
                                                                                           in1=tr[:, 0:n], op0=ALU.mult, op1=ALU.mult), [(src, h), tr, par], [tq])
                        DVE(lambda e, h=h, t0=t0, n=n, tq=tq: e.tensor_tensor(out=dst[:, h, t0:t0 + n], in0=tq[:, 0:n], in1=extra[:, h, t0:t0 + n], op=ALU.mult),
                            [tq, (extra, h)], [(dst, h)])

        def layer(seg, l):
            has_s = (seg == NSEG - 1)
            T = TSEG + TS if has_s else TSEG
            TP = TSEG
            a0 = seg * TSEG
            blks = blocks_of(T)
            win = w_in[l]
            S.dma("sp", memx[:, :, :], memT_in.rearrange("(kc p) n -> p kc n", p=128), writes=[memx]) if False else None
            lam_init = 0.8 - 0.6 * math.exp(-0.3 * l)
            hreg = [hT]
            wst["pos"] = 0
            wst["issued"] = {}
            wst["l"] = l

            def hrhs(k, t0, n):
                return hT[:, k, t0:t0 + n]
            def chk(p):
                if DBG["phase"] < p:
                    raise _Stop()
            rmsnorm_x(xT, l, P_NMIX, T, hT)
            chk(2)
            S.dma("sp", cst[:, :, 0:TP], cs_in[:, :, a0:a0 + TP], writes=[cst])
            if has_s:
                for s in range(NS):
                    S.dma("sp", cst[:, :, TP + 4 * s:TP + 4 * s + 4], cs_in[:, :, SEQ:SEQ + SL], writes=[cst])
            if DBG["sub"] < 1:
                raise _Stop()
            kcur = B1 if has_s else kd_st[l]
            vcur = vloc if has_s else v_st[l]
            kdo = 0 if has_s else a0
            vto = 0 if has_s else seg * TT
            for (c0, dst, gcol, isk) in ((C_DQ, B0, P_QND, False), (C_DK, kcur, P_KND, True)):
                sl = wload("w_in", 0, 8, c0, 512)
                if DBG["sub"] < 2:
                    raise _Stop()

                def hqk(cc, t0, n, ps, dst=dst, gcol=gcol, isk=isk):
                    if DBG["sub"] < 3:
                        return
                    X = nscr()
                    ACT(lambda e: e.activation(out=X[:, 0:n], in_=ps[:, 0:n], func=AF.Copy), [ps], [X])
                    tr = stats([(X[:, 0:n], X)], n, C_BLK, nps(4, 6))
                    Y = nscr()
                    DVE(lambda e: e.scalar_tensor_tensor(out=Y[:, 0:n], in0=X[:, 0:n], scalar=pcol(l, gcol), in1=tr[:, 0:n], op0=ALU.mult, op1=ALU.mult),
                        [X, tr, par], [Y])
                    pr = nps(6, 8)
                    PE(lambda e: e.matmul(pr[:, 0:n], lhsT=cf(C_ROT), rhs=Y[:, 0:n], start=True, stop=True), [con, Y], [pr])
                    Z = nscr()
                    DVE(lambda e: e.tensor_tensor(out=Z[:, 0:n], in0=Y[:, 0:n], in1=cst[:, 0, t0:t0 + n], op=ALU.mult), [Y, cst], [Z])
                    Z2 = nscr()
                    DVE(lambda e: e.tensor_tensor(out=Z2[:, 0:n], in0=pr[:, 0:n], in1=cst[:, 1, t0:t0 + n], op=ALU.mult), [pr, cst], [Z2])
                    if not isk:
                        DVE(lambda e: e.tensor_tensor(out=dst[:, cc, t0:t0 + n], in0=Z[:, 0:n], in1=Z2[:, 0:n], op=ALU.add), [Z, Z2], [(dst, cc)])
                    else:
                        KF = nscr()
                        DVE(lambda e: e.tensor_tensor(out=KF[:, 0:n], in0=Z[:, 0:n], in1=Z2[:, 0:n], op=ALU.add), [Z, Z2], [KF])
                        ACT(lambda e: e.activation(out=dst[:, cc, kdo + t0:kdo + t0 + n], in_=KF[:, 0:n], func=AF.Copy), [KF], [(dst, cc)] if dst.n > 1 else [dst])
                        if t0 < TP:
                            S.dma("sp", pkT_o[l, cc * 128:(cc + 1) * 128, a0 + t0:a0 + t0 + n], KF[:, 0:n], reads=[KF], is_output=True)
                        else:
                            S.dma("sp", skT_o[l, cc * 128:(cc + 1) * 128, :], KF[:, 0:n], reads=[KF], is_output=True)
                proj_fm([(sl, 8)], range(4), hrhs, hreg, blks, hqk)
            if DBG["sub"] < 4:
                raise _Stop()
            sl = wload("w_in", 0, 8, C_DV, 512)
            for tt in range(TT):
                ps = nps(0, 4)

                def mmv(e, ps=ps, tt=tt):
                    ins = None
                    for k in range(8):
                        ins = e.matmul(ps[:, 0:512], lhsT=hT[:, k, tt * 128:(tt + 1) * 128], rhs=sl[:, k, 0:512], start=(k == 0), stop=(k == 7))
                    return ins
                PE(mmv, [sl, hT], [ps])
                VF = nscr()
                ACT(lambda e, ps=ps, VF=VF: e.activation(out=VF[:, :], in_=ps[:, :], func=AF.Copy), [ps], [VF])
                if DBG["sub"] >= 5:
                    DVE(lambda e, VF=VF, tt=tt: e.tensor_copy(out=vcur[:, vto + tt, :], in_=VF[:, :]), [VF], [vcur])
                if DBG["sub"] >= 6:
                    S.dma("sp", pv_o[l, a0 + tt * 128:a0 + (tt + 1) * 128, :], VF[:, :], reads=[VF], is_output=True)
            if has_s:
                for s in range(NS):
                    ps = nps(0, 4)

                    def mmvs(e, ps=ps, s=s):
                        ins = None
                        for k in range(8):
                            ins = e.matmul(ps[0:SL, 0:512], lhsT=hT[:, k, TP + 4 * s:TP + 4 * s + 4], rhs=sl[:, k, 0:512], start=(k == 0), stop=(k == 7))
                        return ins
                    PE(mmvs, [sl, hT], [ps])
                    VF = nscr()
                    ACT(lambda e, ps=ps, VF=VF: e.activation(out=VF[0:SL, :], in_=ps[0:SL, :], func=AF.Copy), [ps], [VF])
                    DVE(lambda e, VF=VF, s=s: e.tensor_copy(out=vs_tok[0:SL, s, :], in_=VF[0:SL, :]), [VF], [vs_tok])
                    S.dma("sp", sv_o[l, s, :, :], VF[0:SL, :], reads=[VF], is_output=True)
            chk(3)
            neglam = lamv[:, l:l + 1]
            for qb in range(1):
                q0t = seg * TT
                for h in range(4):
                    Ops = [PS[2], PS[3]]
                    Lps = [PS[4], PS[5]]
                    nkt = q0t + 4
                    steps = [(c, kt) for c in range(2) for kt in range(nkt)]

                    def emitS(i, h=h, nkt=nkt):
                        c, kt = steps[i]
                        off = max(0, kt - q0t) * 128
                        ncol = 512 - off
                        qc0 = qb * 512 + off
                        if kt < seg * TT or not has_s:
                            ksrc, ko, vsrc, vt = kd_st[l], kt * 128, v_st[l], kt
                        else:
                            ksrc, ko, vsrc, vt = B1, (kt - seg * TT) * 128, vloc, kt - seg * TT
                        sp_ = PS[i % 2]
                        PE(lambda e: e.matmul(sp_[:, 0:ncol], lhsT=ksrc[c * 64:(c + 1) * 64, h, ko:ko + 128], rhs=B0[c * 64:(c + 1) * 64, h, qc0:qc0 + ncol], start=True, stop=True),
                           [ksrc, (B0, h)], [sp_])
                        PT = scrb[i % 4]
                        ACT(lambda e: e.activation(out=PT[:, 0:ncol], in_=sp_[:, 0:ncol], func=AF.Exp, scale=0.125), [sp_], [PT])
                        if kt >= q0t:
                            POOL(lambda e: e.tensor_tensor(out=PT[:, 0:128], in0=PT[:, 0:128], in1=cb(C_TRI), op=ALU.mult), [PT, conb], [PT])
                        return (c, kt, off, ncol, PT, vsrc, vt)

                    def emitPV(info, h=h, nkt=nkt):
                        c, kt, off, ncol, PT, vsrc, vt = info
                        PE(lambda e: e.matmul(Ops[c][:, off:off + ncol], lhsT=vsrc[:, vt, h * 128:(h + 1) * 128], rhs=PT[:, 0:ncol], start=(kt == 0), stop=(kt == nkt - 1)),
                           [vsrc, PT], [Ops[c]])
                        PE(lambda e: e.matmul(Lps[c][:, off:off + ncol], lhsT=cb(C_ONE), rhs=PT[:, 0:ncol], start=(kt == 0), stop=(kt == nkt - 1)),
                           [conb, PT], [Lps[c]])
                    prev = emitS(0)
                    for i in range(1, len(steps)):
                        cur = emitS(i)
                        emitPV(prev)
                        prev = cur
                    emitPV(prev)
                    attn_combine(Ops, Lps, 512, OF[:, h, qb * 512:(qb + 1) * 512], (OF, h), lcol=neglam)
            chk(4)
            if has_s:
                for s in range(NS):
                    Os, Ls = PS[6], PS[7]
                    sc0 = TP + 4 * s

                    def gatherK(j):
                        ic = s * NPAGES + j
                        S.dma("pool", kpage[j % 2][:, :], cache_k[:, :], writes=[kpage[j % 2]], reads=[idx_l[l]], indirect=idx_l[l][:, ic:ic + 1])

                    def gatherV(j):
                        ic = s * NPAGES + j
                        S.dma("pool", vpage[j % 2][:, :], cache_v[:, :], writes=[vpage[j % 2]], reads=[idx_l[l]], indirect=idx_l[l][:, ic:ic + 1])

                    def stageT(j):
                        kp = kpage[j % 2]
                        pk_ = PS[2 + j % 2]
                        for h in range(4):
                            PE(lambda e, h=h: e.matmul(pk_[:, h * 128:(h + 1) * 128], lhsT=kp[:, h * 128:(h + 1) * 128], rhs=cb(C_ID), start=True, stop=True),
                               [kp, conb], [pk_])
                        KT = KTp[j % 2]
                        DVE(lambda e: e.tensor_copy(out=KT[:, :, :], in_=pk_[:, :].rearrange("p (h t) -> p h t", h=4)), [pk_], [KT])

                    PTs = {}
                    DVE(lambda e: e.tensor_copy(out=Qpad[0:64, :, 0:4], in_=B0[0:64, :, sc0:sc0 + 4]), [B0], [Qpad])
                    DVE(lambda e: e.tensor_copy(out=Qpad[64:128, :, 4:8], in_=B0[64:128, :, sc0:sc0 + 4]), [B0], [Qpad])

                    def stageS(j):
                        sp_ = PS[j % 2]
                        nk_ = 128 if j < NPAGES else SL
                        for h in range(4):
                            if j < NPAGES:
                                KT = KTp[j % 2]
                                PE(lambda e, h=h: e.matmul(sp_[:, h * 8:(h + 1) * 8], lhsT=KT[:, h, :], rhs=Qpad[:, h, :], start=True, stop=True), [KT, Qpad], [sp_])
                            else:
                                PE(lambda e, h=h: e.matmul(sp_[0:SL, h * 8:(h + 1) * 8], lhsT=B1[:, h, sc0:sc0 + 4], rhs=Qpad[:, h, :], start=True, stop=True),
                                   [(B1, h), Qpad], [sp_])
                        PT = scrb[j % 4]
                        PTs[j] = PT
                        ACT(lambda e: e.activation(out=PT[0:nk_, 0:32], in_=sp_[0:nk_, 0:32], func=AF.Exp, scale=0.125), [sp_], [PT])
                        if j == NPAGES:
                            DVE(lambda e: e.tensor_tensor(out=PT[0:SL, 0:32], in0=PT[0:SL, 0:32], in1=smask[0:SL, 0:32], op=ALU.mult), [PT, smask], [PT])

                    def stagePV(j):
                        PT = PTs.pop(j)
                        nk_ = 128 if j < NPAGES else SL
                        vp = vpage[j % 2]
                        for h in range(4):
                            if j < NPAGES:
                                PE(lambda e, h=h: e.matmul(Os[:, h * 8:(h + 1) * 8], lhsT=vp[:, h * 128:(h + 1) * 128], rhs=PT[:, h * 8:(h + 1) * 8],
                                                           start=(j == 0 and h == 0), stop=False, skip_group_check=True), [vp, PT], [Os])
                            else:
                                PE(lambda e, h=h: e.matmul(Os[:, h * 8:(h + 1) * 8], lhsT=vs_tok[0:SL, s, h * 128:(h + 1) * 128], rhs=PT[0:SL, h * 8:(h + 1) * 8],
                                                           start=False, stop=True, skip_group_check=True), [vs_tok, PT], [Os])
                        PE(lambda e: e.matmul(Ls[:, 0:32], lhsT=cb(C_ONE, nk_, 128), rhs=PT[0:nk_, 0:32], start=(j == 0), stop=(j == NPAGES)),
                           [conb, PT], [Ls])

                    gatherK(0)
                    gatherK(1)
                    gatherV(0)
                    stageT(0)
                    for j in range(0, NPAGES + 2):
                        if j + 1 < NPAGES:
                            stageT(j + 1)
                        if j <= NPAGES:
                            stageS(j)
                        if 0 <= j - 1 <= NPAGES:
                            stagePV(j - 1)
                        if j + 2 < NPAGES:
                            gatherK(j + 2)
                        if j + 1 < NPAGES:
                            gatherV(j + 1)
                    r = nscr()
                    DVE(lambda e, r=r: e.reciprocal(out=r[:, 0:32], in_=Ls[:, 0:32]), [Ls], [r])
                    t_ = nscr()
                    DVE(lambda e, r=r, t_=t_: e.tensor_tensor(out=t_[:, 0:32], in0=Os[:, 0:32], in1=r[:, 0:32], op=ALU.mult), [Os, r], [t_])
                    for h in range(4):
                        DVE(lambda e, h=h, t_=t_: e.scalar_tensor_tensor(out=OF[:, h, sc0:sc0 + 4], in0=t_[:, (2 * h + 1) * 4:(2 * h + 2) * 4], scalar=neglam,
                                                                        in1=t_[:, 2 * h * 4:(2 * h + 1) * 4], op0=ALU.mult, op1=ALU.add), [t_, lamv], [(OF, h)])
            chk(5)
            headnorm(OF, OD, l, P_DIFFN, T, post_scale=(1.0 - lam_init))
            chk(6)
            for bi, (c0, dst) in enumerate(((C_Q, B0), (C_K, B1), (C_V, B2))):
                sl = wload("w_in", 0, 8, c0, 512)
                st = {}

                def hraw(cc, t0, n, ps, st=st, bi=bi):
                    rw = raw[0]
                    sr = sraw[0]
                    ch = bi * 4 + cc
                    if t0 == 0:
                        if seg == 0:
                            DVE(lambda e: e.memset(rw[:, 0:3], 0.0), [], [rw])
                        else:
                            DVE(lambda e: e.tensor_copy(out=rw[:, 0:3], in_=gtail[l][:, ch, :]), [gtail[l]], [rw])
                    if t0 < TP:
                        ACT(lambda e: e.activation(out=rw[:, 3 + t0:3 + t0 + n], in_=ps[:, 0:n], func=AF.Copy), [ps], [rw])
                    else:
                        S.dma("sp", sr[:, :, 0:3], st_gconvT[l, ch * 128:(ch + 1) * 128, :, :], writes=[sr])
                        ACT(lambda e: e.activation(out=sr[:, :, 3:7], in_=ps[:, 0:TS].rearrange("p (s j) -> p s j", s=NS), func=AF.Copy), [ps], [sr])

                def aft(cc, bi=bi, dst=dst):
                    ch = bi * 4 + cc
                    rw = raw[0]
                    sr = sraw[0]
                    ac = acc[0]
                    wc = P_CQKV + ch * 4
                    ACT(lambda e: e.activation(out=ac[:, 0:TP], in_=rw[:, 0:TP], func=AF.Copy, scale=pcol(l, wc)), [rw, par], [ac])
                    for j in range(1, 4):
                        DVE(lambda e, j=j: e.scalar_tensor_tensor(out=ac[:, 0:TP], in0=rw[:, j:j + TP], scalar=pcol(l, wc + j), in1=ac[:, 0:TP], op0=ALU.mult, op1=ALU.add),
                            [rw, par, ac], [ac])
                    if has_s:
                        av = ac[:, TP:TP + TS].rearrange("p (s j) -> p s j", s=NS)
                        ACT(lambda e: e.activation(out=av, in_=sr[:, :, 0:4], func=AF.Copy, scale=pcol(l, wc)), [sr, par], [ac])
                        for j in range(1, 4):
                            DVE(lambda e, j=j: e.scalar_tensor_tensor(out=av, in0=sr[:, :, j:j + 4], scalar=pcol(l, wc + j), in1=av, op0=ALU.mult, op1=ALU.add),
                                [sr, par, ac], [ac])
                        S.dma("sp", sgcT_o[l, ch * 128:(ch + 1) * 128, :, :], sr[:, :, 4:7], reads=[sr], is_output=True)
                        S.dma("sp", pgcT_o[l, ch * 128:(ch + 1) * 128, :], rw[:, TP:TP + 3], reads=[rw], is_output=True)
                    else:
                        DVE(lambda e: e.tensor_copy(out=gtail[l][:, ch, :], in_=rw[:, TP:TP + 3]), [rw], [gtail[l]])
                    if bi == 2:
                        ACT(lambda e: e.activation(out=dst[:, cc, 0:T], in_=ac[:, 0:T], func=AF.Silu), [ac], [(dst, cc)])
                    else:
                        ACT(lambda e: e.activation(out=ac[:, 0:T], in_=ac[:, 0:T], func=AF.Silu), [ac], [ac])
                        for (t0, n) in blks:
                            tr = stats([(ac[:, t0:t0 + n], ac)], n, C_ONE, nps(4, 6))
                            DVE(lambda e, t0=t0, n=n, tr=tr: e.scalar_tensor_tensor(out=dst[:, cc, t0:t0 + n], in0=ac[:, t0:t0 + n], scalar=(ISQ if bi == 0 else 1.0),
                                                                                  in1=tr[:, 0:n], op0=ALU.mult, op1=ALU.mult), [ac, tr], [(dst, cc)])
                proj_fm([(sl, 8)], range(4), hrhs, hreg, blks, hraw, after=aft)
            sl = wload("w_in", 0, 8, C_G, 512)
            proj_fm([(sl, 8)], range(4), hrhs, hreg, blks,
                    lambda cc, t0, n, ps: ACT(lambda e: e.activation(out=B3[:, cc, t0:t0 + n], in_=ps[:, 0:n], func=AF.Silu), [ps], [(B3, cc)]))
            sl = wload("w_in", 0, 8, C_BA, 8)

            def ba_post(ps, m, bdst, gdst):
                ACT(lambda e: e.activation(out=bdst, in_=ps[0:m, 0:4], func=AF.Sigmoid), [ps], [betaT, betaS])
                xs, tt_, ee, ln_ = gsm[0], gsm[1], gsm[2], gsm[3]
                DVE(lambda e: e.tensor_tensor(out=xs[0:m, 0:4], in0=ps[0:m, 4:8], in1=pcol(l, P_DTB, 4)[0:m, :], op=ALU.add), [ps, par], [xs])
                ACT(lambda e: e.activation(out=tt_[0:m, 0:4], in_=xs[0:m, 0:4], func=AF.Abs), [xs], [tt_])
                ACT(lambda e: e.activation(out=ee[0:m, 0:4], in_=tt_[0:m, 0:4], func=AF.Exp, scale=-1.0), [tt_], [ee])
                ACT(lambda e: e.activation(out=ln_[0:m, 0:4], in_=ee[0:m, 0:4], func=AF.Ln, bias=1.0), [ee], [ln_])
                DVE(lambda e: e.scalar_tensor_tensor(out=xs[0:m, 0:4], in0=xs[0:m, 0:4], scalar=0.0, in1=ln_[0:m, 0:4], op0=ALU.max, op1=ALU.add), [xs, ln_], [xs])
                DVE(lambda e: e.tensor_tensor(out=gdst, in0=xs[0:m, 0:4], in1=negA[0:m, 4 * l:4 * l + 4], op=ALU.mult), [xs, negA], [gT, gS])
            for tt in range(TT):
                ps = nps(0, 4)

                def mmb(e, ps=ps, tt=tt):
                    ins = None
                    for k in range(8):
                        ins = e.matmul(ps[:, 0:8], lhsT=hT[:, k, tt * 128:(tt + 1) * 128], rhs=sl[:, k, 0:8], start=(k == 0), stop=(k == 7))
                    return ins
                PE(mmb, [sl, hT], [ps])
                ba_post(ps, 128, betaT[:, tt, :], gT[:, tt, :])
            if has_s:
                for s in range(NS):
                    ps = nps(0, 4)

                    def mmbs(e, ps=ps, s=s):
                        ins = None
                        for k in range(8):
                            ins = e.matmul(ps[0:SL, 0:8], lhsT=hT[:, k, TP + 4 * s:TP + 4 * s + 4], rhs=sl[:, k, 0:8], start=(k == 0), stop=(k == 7))
                        return ins
                    PE(mmbs, [sl, hT], [ps])
                    ba_post(ps, SL, betaS[0:SL, s, :], gS[0:SL, s, :])
            chk(7)
            if seg == 0:
                DVE(lambda e: e.memset(S32[l][:, :, :], 0.0), [], [S32[l]])
            for ci in range(TT):
                cs_ = slice(ci * 128, (ci + 1) * 128)
                gdn_step(128, 4, lambda it: B0[:, it, cs_], lambda it: B1[:, it, cs_], lambda it: B2[:, it, cs_], [B0, B1, B2, betaT, gT],
                         lambda it: betaT[:, ci, it:it + 1], lambda it: gT[:, ci, it:it + 1],
                         lambda it: S32[l][:, it, :], lambda it: OF[:, it, cs_], [S32[l], OF],
                         q3=B0[:, 0:4, cs_], S3=S32[l][:, 0:4, :], o3=OF[:, 0:4, cs_], beta2=betaT[:, ci, 0:4], g2=gT[:, ci, 0:4])
            if has_s:
                for h in range(4):
                    S.dma("sp", pg_o[l, h, :, :], S32[l][:, h, :], reads=[S32[l]], is_output=True)
                for h in range(4):
                    for s in range(NS):
                        S.dma("sp", Ssm[:, s, :], st_gdn[l, s, h, :, :], writes=[Ssm])
                    gdn_step(SL, NS, lambda it, h=h: B0[:, h, TP + 4 * it:TP + 4 * it + 4], lambda it, h=h: B1[:, h, TP + 4 * it:TP + 4 * it + 4],
                             lambda it, h=h: B2[:, h, TP + 4 * it:TP + 4 * it + 4], [B0, B1, B2, betaS, gS],
                             lambda it, h=h: betaS[0:SL, it, h:h + 1], lambda it, h=h: gS[0:SL, it, h:h + 1],
                             lambda it: Ssm[:, it, :], lambda it, h=h: OF[:, h, TP + 4 * it:TP + 4 * it + 4], [Ssm, OF])
                    for s in range(NS):
                        S.dma("sp", sg_o[l, s, h, :, :], Ssm[:, s, :], reads=[Ssm], is_output=True)
            headnorm(OF, OG, l, P_GDNN, T, extra=B3)
            chk(8)
            for cbk in range(2):
                sl = wload("w_out", 0, 8, cbk * 512, 512)

                def orhs(k, t0, n):
                    return OG[:, k, t0:t0 + n] if k < 4 else OD[:, k - 4, t0:t0 + n]

                def hres(cc, t0, n, ps, cbk=cbk):
                    kc = cbk * 4 + cc
                    DVE(lambda e: e.tensor_tensor(out=xT[:, kc, t0:t0 + n], in0=xT[:, kc, t0:t0 + n], in1=ps[:, 0:n], op=ALU.add), [(xT, kc), ps], [(xT, kc)])
                proj_fm([(sl, 8)], range(4), orhs, [OG, OD], blks, hres)
            chk(9)
            rmsnorm_x(xT, l, P_NCROSS, T, hT)
            sl = wload("w_cq", 0, 8, 0, 512)

            def hq(cc, t0, n, ps):
                X = nscr()
                ACT(lambda e: e.activation(out=X[:, 0:n], in_=ps[:, 0:n], func=AF.Copy), [ps], [X])
                tr = stats([(X[:, 0:n], X)], n, C_O128, nps(4, 6))
                DVE(lambda e: e.scalar_tensor_tensor(out=B2[:, cc, t0:t0 + n], in0=X[:, 0:n], scalar=pcol(l, P_QNC), in1=tr[:, 0:n], op0=ALU.mult, op1=ALU.mult),
                    [X, tr, par], [(B2, cc)])
            proj_fm([(sl, 8)], range(4), hrhs, hreg, blks, hq)
            S.dma("sp", memx[:, :, :], memT_in.rearrange("(kc p) n -> p kc n", p=128), writes=[memx])
            rmsnorm_x(memx, l, P_NMEM, 256, mnT)
            sl = wload("w_ck", 0, 8, 0, 512)

            def hmk(cc, t0, n, ps):
                X = nscr()
                ACT(lambda e: e.activation(out=X[:, 0:n], in_=ps[:, 0:n], func=AF.Copy), [ps], [X])
                tr = stats([(X[:, 0:n], X)], n, C_O128, nps(4, 6))
                KF = nscr()
                DVE(lambda e: e.scalar_tensor_tensor(out=KF[:, 0:n], in0=X[:, 0:n], scalar=pcol(l, P_KNC), in1=tr[:, 0:n], op0=ALU.mult, op1=ALU.mult),
                    [X, tr, par], [KF])
                ACT(lambda e: e.activation(out=mkT[:, cc, 0:n], in_=KF[:, 0:n], func=AF.Copy), [KF], [mkT])
                if seg == 0:
                    S.dma("sp", pmkT_o[l, cc * 128:(cc + 1) * 128, :], KF[:, 0:n], reads=[KF], is_output=True)
            proj_fm([(sl, 8)], range(4), lambda k, t0, n: mnT[:, k, t0:t0 + n], [mnT], [(0, 256)], hmk)
            sl = wload("w_cv", 0, 8, 0, 512)
            for mt in range(2):
                ps = nps(0, 4)

                def mmm(e, ps=ps, mt=mt):
                    ins = None
                    for k in range(8):
                        ins = e.matmul(ps[:, 0:512], lhsT=mnT[:, k, mt * 128:(mt + 1) * 128], rhs=sl[:, k, 0:512], start=(k == 0), stop=(k == 7))
                    return ins
                PE(mmm, [sl, mnT], [ps])
                VF = nscr()
                ACT(lambda e, ps=ps, VF=VF: e.activation(out=VF[:, :], in_=ps[:, :], func=AF.Copy), [ps], [VF])
                DVE(lambda e, VF=VF, mt=mt: e.tensor_copy(out=mv[:, mt, :], in_=VF[:, :]), [VF], [mv])
                if seg == 0:
                    S.dma("sp", pmv_o[l, mt * 128:(mt + 1) * 128, :], VF[:, :], reads=[VF], is_output=True)
            qlist = [(0, 512, mkT, mv, None)]
            if has_s:
                qlist += [(TP + 4 * s, 4, mkTs, mvs, s) for s in range(NS)]
            for (q0, nq, mk_, mv_, ss) in qlist:
                if ss is not None:
                    for h in range(4):
                        S.dma("pool", mkTs[:, h, :], cmkT[l, ss, h, :, :], writes=[mkTs])
                    S.dma("pool", mvs[:, :, :], cmv[l, ss].rearrange("(m p) f -> p m f", p=128), writes=[mvs])
                for h in range(4):
                    Ops = [PS[2]]
                    Lps = [PS[4]]
                    for mt in range(2):
                        sp_ = nps(0, 2)
                        PE(lambda e, sp_=sp_, mt=mt, h=h, mk_=mk_, q0=q0, nq=nq: e.matmul(sp_[:, 0:nq], lhsT=mk_[:, h, mt * 128:(mt + 1) * 128], rhs=B2[:, h, q0:q0 + nq], start=True, stop=True),
                           [mk_, (B2, h)], [sp_])
                        PT = nscrb()
                        ACT(lambda e, sp_=sp_, PT=PT, nq=nq: e.activation(out=PT[:, 0:nq], in_=sp_[:, 0:nq], func=AF.Exp, scale=ISQ), [sp_], [PT])
                        PE(lambda e, PT=PT, mt=mt, h=h, mv_=mv_, nq=nq: e.matmul(Ops[0][:, 0:nq], lhsT=mv_[:, mt, h * 128:(h + 1) * 128], rhs=PT[:, 0:nq], start=(mt == 0), stop=(mt == 1)),
                           [mv_, PT], [Ops[0]])
                        PE(lambda e, PT=PT, mt=mt, nq=nq: e.matmul(Lps[0][:, 0:nq], lhsT=cb(C_ONE), rhs=PT[:, 0:nq], start=(mt == 0), stop=(mt == 1)), [conb, PT], [Lps[0]])
                    attn_combine(Ops, Lps, nq, B3[:, h, q0:q0 + nq], (B3, h))
            for cbk in range(2):
                sl = wload("w_co", 0, 4, cbk * 512, 512)

                def hres2(cc, t0, n, ps, cbk=cbk):
                    kc = cbk * 4 + cc
                    DVE(lambda e: e.tensor_tensor(out=xT[:, kc, t0:t0 + n], in0=xT[:, kc, t0:t0 + n], in1=ps[:, 0:n], op=ALU.add), [(xT, kc), ps], [(xT, kc)])
                proj_fm([(sl, 4)], range(4), lambda k, t0, n: B3[:, k, t0:t0 + n], [B3], blks, hres2)
            chk(10)
            rmsnorm_x(xT, l, P_NFFN, T, hT)
            barrier()
            for fb in range(6):
                ncol = 512 if fb < 5 else 256
                ncc = ncol // 128
                slg = wload("w_gate", 0, 8, fb * 512, ncol)
                slu = wload("w_up", 0, 8, fb * 512, ncol)

                def hgr(cc, t0, n, ps, fb=fb):
                    fc = fb * 4 + cc
                    rw = raw[0]
                    sr = sraw[0]
                    if t0 == 0:
                        if seg == 0:
                            DVE(lambda e: e.memset(rw[:, 0:2], 0.0), [], [rw])
                        else:
                            DVE(lambda e: e.tensor_copy(out=rw[:, 0:2], in_=ftail[l][:, fc, :]), [ftail[l]], [rw])
                    if t0 < TP:
                        ACT(lambda e: e.activation(out=rw[:, 2 + t0:2 + t0 + n], in_=ps[:, 0:n], func=AF.Copy), [ps], [rw])
                    else:
                        S.dma("sp", sr[:, :, 0:2], st_fconvT[l, fc * 128:(fc + 1) * 128, :, :], writes=[sr])
                        ACT(lambda e: e.activation(out=sr[:, :, 2:6], in_=ps[:, 0:TS].rearrange("p (s j) -> p s j", s=NS), func=AF.Copy), [ps], [sr])

                def aftg(cc, fb=fb):
                    fc = fb * 4 + cc
                    rw = raw[0]
                    sr = sraw[0]
                    ac = acc[0]
                    wc = P_CFFN + fc * 3
                    ACT(lambda e: e.activation(out=ac[:, 0:TP], in_=rw[:, 0:TP], func=AF.Copy, scale=pcol(l, wc)), [rw, par], [ac])
                    for j in range(1, 3):
                        DVE(lambda e, j=j: e.scalar_tensor_tensor(out=ac[:, 0:TP], in0=rw[:, j:j + TP], scalar=pcol(l, wc + j), in1=ac[:, 0:TP], op0=ALU.mult, op1=ALU.add),
                            [rw, par, ac], [ac])
                    if has_s:
                        av = ac[:, TP:TP + TS].rearrange("p (s j) -> p s j", s=NS)
                        ACT(lambda e: e.activation(out=av, in_=sr[:, :, 0:4], func=AF.Copy, scale=pcol(l, wc)), [sr, par], [ac])
                        for j in range(1, 3):
                            DVE(lambda e, j=j: e.scalar_tensor_tensor(out=av, in0=sr[:, :, j:j + 4], scalar=pcol(l, wc + j), in1=av, op0=ALU.mult, op1=ALU.add),
                                [sr, par, ac], [ac])
                        S.dma("sp", sfcT_o[l, fc * 128:(fc + 1) * 128, :, :], sr[:, :, 4:6], reads=[sr], is_output=True)
                        S.dma("sp", pfcT_o[l, fc * 128:(fc + 1) * 128, :], rw[:, TP:TP + 2], reads=[rw], is_output=True)
                    else:
                        DVE(lambda e: e.tensor_copy(out=ftail[l][:, fc, :], in_=rw[:, TP:TP + 2]), [rw], [ftail[l]])
                    ACT(lambda e: e.activation(out=ac[:, 0:T], in_=ac[:, 0:T], func=AF.Silu), [ac], [ac])

                    def hup(cc2, t0, n, ps, fc=fc, ac=ac):
                        DVE(lambda e: e.tensor_tensor(out=actT[:, fc, t0:t0 + n], in0=ac[:, t0:t0 + n], in1=ps[:, 0:n], op=ALU.mult), [ac, ps], [(actT, fc)])
                    proj_fm([(slu, 8)], [cc], hrhs, hreg, blks, hup)
                proj_fm([(slg, 8)], range(ncc), hrhs, hreg, blks, hgr, after=aftg, delay=0)
            for cbk in range(2):
                sls = [(wload("w_down", 0, 8, cbk * 512, 512), 8), (wload("w_down", 8, 8, cbk * 512, 512), 8), (wload("w_down", 16, 6, cbk * 512, 512, nopf=True), 6)]

                def hres3(cc, t0, n, ps, cbk=cbk):
                    kc = cbk * 4 + cc
                    DVE(lambda e: e.tensor_tensor(out=xT[:, kc, t0:t0 + n], in0=xT[:, kc, t0:t0 + n], in1=ps[:, 0:n], op=ALU.add), [(xT, kc), ps], [(xT, kc)])
                proj_fm(sls, range(4), lambda k, t0, n: actT[:, k, t0:t0 + n], [actT], blks, hres3)
            barrier()


        S.dry = True
        try:
            layer(0, 0)
        except _Stop:
            pass
        S.dry = False
        for seg in range(NSEG):
            TP = TSEG
            a0 = seg * TSEG
            for kc in range(KC):
                S.dma("sp", xT[:, kc, 0:TP], xT_in[kc * 128:(kc + 1) * 128, a0:a0 + TP], writes=[(xT, kc)])
                if seg == NSEG - 1:
                    S.dma("sp", xT[:, kc, TP:TP + TS], xsT_in[kc * 128:(kc + 1) * 128, :], writes=[(xT, kc)])
            for l in range(DBG["layers"]):
                if stage >= 1 and seg < DBG["segs"]:
                    try:
                        layer(seg, l)
                    except _Stop:
                        pass
            for kc in range(KC):
                S.dma("sp", yT_o[kc * 128:(kc + 1) * 128, a0:a0 + TP], xT[:, kc, 0:TP], reads=[(xT, kc)], is_output=True)
                if seg == NSEG - 1:
                    S.dma("sp", ysT_o[kc * 128:(kc + 1) * 128, :], xT[:, kc, TP:TP + TS], reads=[(xT, kc)], is_output=True)

        S.finish()
    return nc, S


def _consts():
    con = np.zeros((128, 11, 128), np.float32)
    p = np.arange(128)
    con[:, 0] = np.eye(128)
    con[:, 1] = 1.0
    con[:, 2] = (p[:, None] <= p[None, :])
    con[:, 3] = np.where(p[:, None] < p[None, :], BIG, 0.0)
    con[:, 4] = (p[None, :] < p[:, None])
    con[:, 5] = (p[:, None] <= p[None, :])
    R = np.zeros((128, 128), np.float32)
    for q in range(128):
        if q % 64 < 32:
            R[q + 32, q] = -1.0
        else:
            R[q - 32, q] = 1.0
    con[:, 6] = R
    con[:, 7] = (p[:, None] // 64 == p[None, :] // 64) / 64.0
    con[:, 8, 0] = p
    con[:, 8, 1] = p + NPOOL * 128
    con[:, 8, 8] = EPS
    con[:, 9] = 1.0 / 128
    con[:, 10] = 1.0 / 1024
    half = 32
    inv = 10000.0 ** (-np.arange(half, dtype=np.float32) / half)
    pos = np.concatenate([np.arange(SEQ), 8192 + np.arange(SL)]).astype(np.float32)
    ang = pos[None, :] * inv[p % 32][:, None]
    cs = np.stack([np.cos(ang), np.sin(ang)], axis=1).astype(np.float32)
    sm = np.zeros((128, 32), np.float32)
    for j in range(4):
        for m in range(8):
            for q in range(4):
                sm[j, m * 4 + q] = 1.0 if j <= q else 0.0
    return con.reshape(128, 11 * 128), cs, sm


_CACHE = {}


def kernel(**inp):
    f = lambda a: np.ascontiguousarray(np.asarray(a, dtype=np.float32))
    if "nc" not in _CACHE:
        _CACHE["nc"] = build_program()[0]
    nc = _CACHE["nc"]
    con, cs, sm = _consts()
    NPS = 161
    par = np.zeros((128, DEPTH, NPS), np.float32)
    for l in range(DEPTH):
        for o, nm in ((0, "norm_mix"), (8, "norm_cross"), (16, "norm_mem"), (24, "norm_ffn")):
            par[:, l, o:o + 8] = np.asarray(inp[nm][l]).reshape(8, 128).T
        par[:, l, 32:80] = np.asarray(inp["conv_qkv"][l]).reshape(4, 12, 128).transpose(2, 1, 0).reshape(128, 48)
        par[:, l, 80:146] = np.asarray(inp["conv_ffn"][l]).reshape(3, 22, 128).transpose(2, 1, 0).reshape(128, 66)
        par[:, l, 146:150] = np.asarray(inp["a_log"][l])[None, :]
        par[:, l, 150:154] = np.asarray(inp["dt_bias"][l])[None, :]
        par[:, l, 154] = np.asarray(inp["gdn_norm"][l])
        par[:, l, 155] = np.tile(np.asarray(inp["qnorm_diff"][l]), 2)
        par[:, l, 156] = np.tile(np.asarray(inp["knorm_diff"][l]), 2)
        par[:, l, 157] = np.asarray(inp["diff_norm"][l])
        par[:, l, 158] = np.asarray(inp["qnorm_cross"][l])
        par[:, l, 159] = np.asarray(inp["knorm_cross"][l])
    par = par.reshape(128, DEPTH * NPS)
    lam = np.stack([np.stack([np.asarray(inp[n][l]) for n in ("lam_q1", "lam_k1", "lam_q2", "lam_k2")]) for l in range(DEPTH)]).astype(np.float32).reshape(1, -1)
    ck = f(inp["cache_k"]).reshape(DEPTH * NPOOL * 128, 512)
    cv = f(inp["cache_v"]).reshape(DEPTH * NPOOL * 128, 512)
    if DBG["small_cache"]:
        ck, cv = ck[:128], cv[:128]
    shared = {k: f(inp[k]) for k in ("w_in", "w_out", "w_cq", "w_ck", "w_cv", "w_co", "w_gate", "w_up", "w_down")}
    xp, xs, mem = f(inp["x_prompt"]), f(inp["x_sample"]), f(inp["mem_prompt"])
    pt = np.asarray(inp["page_table"]).astype(np.int32)
    sg, sgc = f(inp["state_gdn"]), f(inp["state_gdn_conv"])
    cmk, cmv_, sfc = f(inp["cache_mem_k"]), f(inp["cache_mem_v"]), f(inp["state_ffn_conv"])
    in_maps = []
    for c in range(NCORES):
        sl = slice(NS * c, NS * c + NS)
        m = dict(shared)
        m.update({
            "xT_in": np.ascontiguousarray(xp[c].T), "xsT_in": np.ascontiguousarray(xs[sl].reshape(TS, D).T),
            "memT_in": np.ascontiguousarray(mem[c].T), "cache_k": ck, "cache_v": cv,
            "ptab": np.ascontiguousarray(pt[sl].reshape(1, NS * NPAGES)),
            "st_gdn": np.ascontiguousarray(sg[:, sl]),
            "st_gconvT": np.ascontiguousarray(sgc[:, sl].transpose(0, 3, 1, 2)),
            "cmkT": np.ascontiguousarray(cmk[:, sl].transpose(0, 1, 3, 4, 2)),
            "cmv": np.ascontiguousarray(cmv_[:, sl].reshape(DEPTH, NS, 256, 512)),
            "st_fconvT": np.ascontiguousarray(sfc[:, sl].transpose(0, 3, 1, 2)),
            "par_in": par, "lam_in": lam, "con_in": con, "cs_in": cs, "smask_in": sm,
        })
        in_maps.append(m)
    res = run_bass_kernel_spmd(nc, in_maps, core_ids=list(range(NCORES))).results
    B, SB = NCORES, NCORES * NS
    y_p = np.stack([res[c]["yT_o"].T for c in range(B)])
    y_s = np.concatenate([res[c]["ysT_o"].T.reshape(NS, SL, D) for c in range(B)])
    p_k = np.stack([np.stack([res[c]["pkT_o"][l].T.reshape(SEQ, 8, 64) for c in range(B)]) for l in range(DEPTH)])
    p_v = np.stack([np.stack([res[c]["pv_o"][l].reshape(SEQ, 4, 128) for c in range(B)]) for l in range(DEPTH)])
    p_g = np.stack([np.stack([res[c]["pg_o"][l] for c in range(B)]) for l in range(DEPTH)])
    p_gc = np.stack([np.stack([res[c]["pgcT_o"][l].T for c in range(B)]) for l in range(DEPTH)])
    p_mk = np.stack([np.stack([res[c]["pmkT_o"][l].T.reshape(256, 4, 128) for c in range(B)]) for l in range(DEPTH)])
    p_mv = np.stack([np.stack([res[c]["pmv_o"][l].reshape(256, 4, 128) for c in range(B)]) for l in range(DEPTH)])
    p_fc = np.stack([np.stack([res[c]["pfcT_o"][l].T for c in range(B)]) for l in range(DEPTH)])
    s_k = np.stack([np.concatenate([res[c]["skT_o"][l].T.reshape(NS, SL, 8, 64) for c in range(B)]) for l in range(DEPTH)])
    s_v = np.stack([np.concatenate([res[c]["sv_o"][l].reshape(NS, SL, 4, 128) for c in range(B)]) for l in range(DEPTH)])
    s_g = np.stack([np.concatenate([res[c]["sg_o"][l] for c in range(B)]) for l in range(DEPTH)])
    s_gc = np.stack([np.concatenate([res[c]["sgcT_o"][l].transpose(1, 2, 0) for c in range(B)]) for l in range(DEPTH)])
    s_fc = np.stack([np.concatenate([res[c]["sfcT_o"][l].transpose(1, 2, 0) for c in range(B)]) for l in range(DEPTH)])
    outs = (y_p, y_s, p_k, p_v, p_g, p_gc, p_mk, p_mv, p_fc, s_k, s_v, s_g, s_gc, s_fc)
    return tuple(np.ascontiguousarray(o, dtype=np.float32) for o in outs)
```

```python
import math
import numpy as np
import concourse.bass as bass
import concourse.mybir as mybir
from concourse.bass_utils import run_bass_kernel_spmd

F32 = mybir.dt.float32
BF16 = mybir.dt.bfloat16
I32 = mybir.dt.int32
AF = mybir.ActivationFunctionType
ALU = mybir.AluOpType

NCORES = 8
D = 1024
KC = 8
SEQ = 2048
TSEG = 512
NSEG = 4
TT = 4
NS = 4
SL = 4
TS = NS * SL
DEPTH = 2
NPAGES = 64
NPOOL = 2560
DFF = 2816
FC = 22
INW = 3592
EPS = 1e-6
BIG = 1.0e30
C_Q, C_K, C_V, C_G, C_BA, C_DQ, C_DK, C_DV = 0, 512, 1024, 1536, 2048, 2056, 2568, 3080


class Tl:
    def __init__(self, t, n=1, excl=False):
        self.t = t
        self.n = n
        self.excl = excl
        self.lw = [None] * n
        self.rd = [[] for _ in range(n)]

    def __getitem__(self, k):
        return self.t[k]


class Al:
    def __init__(self, t, parents):
        self.t = t
        self.parents = parents
        self.n = 1

    def __getitem__(self, k):
        return self.t[k]


class Sched:
    ENG = ("pe", "act", "dve", "pool", "sp")

    def __init__(self, nc, ndma=40):
        self.nc = nc
        self.eng = {"pe": nc.tensor, "act": nc.scalar, "dve": nc.vector, "pool": nc.gpsimd, "sp": nc.sync}
        self.sem = {}
        self.cnt = {e: 0 for e in self.ENG}
        self.known = {e: {} for e in self.ENG}
        self.prog = {e: [] for e in self.ENG}
        self.dsem = []
        self.dcnt = []
        self.dlast = []
        self.dnext = 0
        self.ndma = ndma
        self.outdeps = []
        self.ninstr = 0
        self.dry = False

    def open(self, stack):
        for e in self.ENG:
            self.sem[e] = stack.enter_context(self.nc.semaphore("s_" + e))
        for i in range(self.ndma):
            self.dsem.append(stack.enter_context(self.nc.semaphore("d%d" % i)))
            self.dcnt.append(0)
            self.dlast.append(None)

    @staticmethod
    def _regs(lst):
        out = []
        for r in lst:
            if isinstance(r, Al):
                for p in r.parents:
                    out.extend((p, i) for i in range(p.n))
            elif isinstance(r, Tl):
                out.extend((r, i) for i in range(r.n))
            else:
                t, i = r
                if isinstance(t, Al):
                    for p in t.parents:
                        out.extend((p, i2) for i2 in range(p.n))
                elif isinstance(i, (list, tuple, range)):
                    out.extend((t, j) for j in i)
                else:
                    out.append((t, i))
        return out

    def _deps(self, reads, writes):
        deps = []
        for t, i in reads:
            if t.lw[i] is not None:
                deps.append(t.lw[i])
        for t, i in writes:
            if t.lw[i] is not None:
                deps.append(t.lw[i])
            deps.extend(t.rd[i])
        return deps

    def _waits(self, e, deps):
        need = {}
        for s, v in deps:
            if self.known[e].get(id(s), (None, 0))[1] < v:
                if need.get(id(s), (None, 0))[1] < v:
                    need[id(s)] = (s, v)
        for k, (s, v) in need.items():
            self.known[e][k] = (s, v)
        return list(need.values())

    def _mark(self, reads, writes, dep):
        for t, i in writes:
            t.lw[i] = dep
            t.rd[i] = []
        for t, i in reads:
            t.rd[i].append(dep)

    def op(self, e, fn, reads=(), writes=()):
        if self.dry:
            return
        reads = self._regs(reads)
        writes = self._regs(writes)
        writes = writes + [r for r in reads if r[0].excl]
        reads = [r for r in reads if not r[0].excl]
        waits = self._waits(e, self._deps(reads, writes))
        self.cnt[e] += 1
        c = self.cnt[e]
        sem = self.sem[e]
        engobj = self.eng[e]

        def thunk():
            for s, v in waits:
                engobj.wait_ge(s, v)
            ins = fn(engobj)
            ins.then_inc(sem, 1)
        thunk()
        self._mark(reads, writes, (sem, c))
        self.ninstr += 1

    def dma(self, q, out_ap, in_ap, reads=(), writes=(), is_output=False, indirect=None):
        if self.dry:
            return
        reads = self._regs(reads)
        writes = self._regs(writes)
        deps = self._deps(reads, writes)
        k = self.dnext
        self.dnext = (self.dnext + 1) % self.ndma
        if self.dlast[k] is not None:
            deps.append(self.dlast[k])
        waits = self._waits(q, deps)
        self.dcnt[k] += 16
        s = self.dsem[k]
        v = self.dcnt[k]
        engobj = self.eng[q]

        def thunk():
            for ws, wv in waits:
                engobj.wait_ge(ws, wv)
            if indirect is None:
                ins = engobj.dma_start(out=out_ap, in_=in_ap)
            else:
                ins = engobj.indirect_dma_start(out=out_ap, out_offset=None, in_=in_ap,
                                                in_offset=bass.IndirectOffsetOnAxis(indirect, 0))
            ins.then_inc(s, 16)
        thunk()
        dep = (s, v)
        self.dlast[k] = dep
        self._mark(reads, writes, dep)
        if is_output:
            self.outdeps.append(dep)
        self.ninstr += 1

    def finish(self):
        alld = list(self.outdeps) + [d for d in self.dlast if d is not None] + [(self.sem[e], self.cnt[e]) for e in self.ENG if self.cnt[e] > 0]
        waits = self._waits("sp", alld)
        engobj = self.eng["sp"]

        def thunk():
            for s, v in waits:
                engobj.wait_ge(s, v)
        thunk()

    def emit(self, block):
        progs = self.prog

        @block.tensor
        def _(e):
            for th in progs["pe"]:
                th()

        @block.scalar
        def _(e):
            for th in progs["act"]:
                th()

        @block.vector
        def _(e):
            for th in progs["dve"]:
                th()

        @block.gpsimd
        def _(e):
            for th in progs["pool"]:
                th()

        @block.sync
        def _(e):
            for th in progs["sp"]:
                th()


DBG = {"segs": NSEG, "layers": DEPTH, "phase": 99, "small_cache": False, "sub": 99}


class _Stop(Exception):
    pass


def build_program(stage=99):
    from contextlib import ExitStack
    nc = bass.Bass("TRN2", target_bir_lowering=False)
    S = Sched(nc)

    def din(name, shape, dt=F32):
        return nc.dram_tensor(name, list(shape), dt, kind="ExternalInput").ap()

    def dout(name, shape, dt=F32):
        return nc.dram_tensor(name, list(shape), dt, kind="ExternalOutput").ap()

    xT_in = din("xT_in", [D, SEQ])
    xsT_in = din("xsT_in", [D, TS])
    memT_in = din("memT_in", [D, 256])
    CR = 128 if DBG["small_cache"] else DEPTH * NPOOL * 128
    cache_k = din("cache_k", [CR, 512])
    cache_v = din("cache_v", [CR, 512])
    ptab = din("ptab", [1, NS * NPAGES], I32)
    st_gdn = din("st_gdn", [DEPTH, NS, 4, 128, 128])
    st_gconvT = din("st_gconvT", [DEPTH, 1536, NS, 3])
    cmkT = din("cmkT", [DEPTH, NS, 4, 128, 256])
    cmv = din("cmv", [DEPTH, NS, 256, 512])
    st_fconvT = din("st_fconvT", [DEPTH, DFF, NS, 2])
    w_in = din("w_in", [DEPTH, D, INW])
    w_out = din("w_out", [DEPTH, D, D])
    w_cq = din("w_cq", [DEPTH, D, 512])
    w_ck = din("w_ck", [DEPTH, D, 512])
    w_cv = din("w_cv", [DEPTH, D, 512])
    w_co = din("w_co", [DEPTH, 512, D])
    w_gate = din("w_gate", [DEPTH, D, DFF])
    w_up = din("w_up", [DEPTH, D, DFF])
    w_down = din("w_down", [DEPTH, DFF, D])
    NPS = 8 * 4 + 12 * 4 + 22 * 3 + 4 + 4 + 7
    par_in = din("par_in", [128, DEPTH * NPS])
    lam_in = din("lam_in", [1, DEPTH * 4 * 64])
    NCON = 128 * 11
    con_in = din("con_in", [128, NCON])
    cs_in = din("cs_in", [128, 2, SEQ + SL])
    smask_in = din("smask_in", [128, 32])

    yT_o = dout("yT_o", [D, SEQ])
    ysT_o = dout("ysT_o", [D, TS])
    pkT_o = dout("pkT_o", [DEPTH, 512, SEQ])
    pv_o = dout("pv_o", [DEPTH, SEQ, 512])
    pg_o = dout("pg_o", [DEPTH, 4, 128, 128])
    pgcT_o = dout("pgcT_o", [DEPTH, 1536, 3])
    pmkT_o = dout("pmkT_o", [DEPTH, 512, 256])
    pmv_o = dout("pmv_o", [DEPTH, 256, 512])
    pfcT_o = dout("pfcT_o", [DEPTH, DFF, 2])
    skT_o = dout("skT_o", [DEPTH, 512, TS])
    sv_o = dout("sv_o", [DEPTH, NS, SL, 512])
    sg_o = dout("sg_o", [DEPTH, NS, 4, 128, 128])
    sgcT_o = dout("sgcT_o", [DEPTH, 1536, NS, 3])
    sfcT_o = dout("sfcT_o", [DEPTH, DFF, NS, 2])

    es = ExitStack()
    with es:
        S.open(es)
        _uid = [0]

        def sb(shape, dt=F32, n=1, name=None):
            _uid[0] += 1
            return Tl(nc.alloc_sbuf_tensor(name or ("t%d" % _uid[0]), list(shape), dt), n)

        TM = TSEG + TS
        xT = sb([128, KC, TM], F32, n=KC, name="xT")
        hT = sb([128, KC, TM], BF16, n=KC, name="hT")
        NSLOT = 3
        wsl = [sb([128, 8, 512], BF16, name="wsl%d" % i) for i in range(NSLOT)]
        par = sb([128, DEPTH * NPS], F32, name="par")
        con = sb([128, NCON], F32, name="con")
        conb = sb([128, NCON], BF16, name="conb")
        lam_t = sb([1, DEPTH * 4 * 64], F32, name="lam_t")
        lamv = sb([128, 2 * DEPTH], F32, name="lamv")
        smask = sb([128, 32], F32, name="smask")
        idx_t = sb([128, NS * NPAGES], I32, name="idx_t")
        kd_st = [sb([128, 4, (NSEG - 1) * TSEG], BF16, name="kdst%d" % l) for l in range(DEPTH)]
        v_st = [sb([128, (NSEG - 1) * TT, 512], BF16, name="vst%d" % l) for l in range(DEPTH)]
        S32 = [sb([128, 4, 128], F32, name="S32_%d" % l) for l in range(DEPTH)]
        gtail = [sb([128, 12, 3], F32, name="gtail%d" % l) for l in range(DEPTH)]
        ftail = [sb([128, FC, 2], F32, name="ftail%d" % l) for l in range(DEPTH)]
        PS = [Tl(nc.alloc_psum_tensor("ps%d" % i, [128, 512], F32), excl=True) for i in range(8)]

        def cf(k, rows=128, cols=128):
            return con[0:rows, k * 128:k * 128 + cols]

        def cb(k, rows=128, cols=128):
            return conb[0:rows, k * 128:k * 128 + cols]
        C_ID, C_ONE, C_UTRI, C_MPOS, C_STRICT, C_TRI, C_ROT, C_BLK, C_MISC, C_O128, C_O1024 = range(11)

        S.dma("sp", con[:, :], con_in[:, :], writes=[con])
        S.dma("sp", par[:, :], par_in[:, :], writes=[par])
        S.dma("sp", lam_t[:, :], lam_in[:, :], writes=[lam_t])
        S.dma("sp", smask[:, :], smask_in[:, :], writes=[smask])
        S.op("dve", lambda e: e.tensor_copy(out=conb[:, :], in_=con[:, :]), reads=[con], writes=[conb])

        def pcol(l, off, n=1):
            return par[:, l * NPS + off:l * NPS + off + n]
        P_NMIX, P_NCROSS, P_NMEM, P_NFFN = 0, 8, 16, 24
        P_CQKV = 32
        P_CFFN = 32 + 48
        P_ALOG = P_CFFN + 66
        P_DTB = P_ALOG + 4
        P_GDNN, P_QND, P_KND, P_DIFFN, P_QNC, P_KNC = [P_DTB + 4 + i for i in range(6)]
        P_SP = P_DTB + 4 + 6

        negA = sb([128, DEPTH * 4], F32, name="negA")
        for l in range(DEPTH):
            S.op("act", lambda e, l=l: e.activation(out=negA[:, 4 * l:4 * l + 4], in_=pcol(l, P_ALOG, 4), func=AF.Exp),
                 reads=[par], writes=[negA])
        S.op("dve", lambda e: e.tensor_scalar(out=negA[:, :], in0=negA[:, :], scalar1=-1.0, scalar2=None, op0=ALU.mult),
             reads=[negA], writes=[negA])
        lprod = sb([1, DEPTH * 2 * 64], F32, name="lprod")
        lsum = sb([1, DEPTH * 2], F32, name="lsum")
        lam1 = sb([1, DEPTH], F32, name="lam1")
        for l in range(DEPTH):
            for j in range(2):
                o = (l * 4 + 2 * j) * 64
                S.op("dve", lambda e, o=o, l=l, j=j: e.tensor_tensor(
                    out=lprod[0:1, (l * 2 + j) * 64:(l * 2 + j + 1) * 64], in0=lam_t[0:1, o:o + 64],
                    in1=lam_t[0:1, o + 64:o + 128], op=ALU.mult), reads=[lam_t], writes=[lprod])
                S.op("dve", lambda e, l=l, j=j: e.reduce_sum(
                    out=lsum[0:1, l * 2 + j:l * 2 + j + 1], in_=lprod[0:1, (l * 2 + j) * 64:(l * 2 + j + 1) * 64],
                    axis=mybir.AxisListType.X), reads=[lprod], writes=[lsum])
        S.op("act", lambda e: e.activation(out=lsum[0:1, :], in_=lsum[0:1, :], func=AF.Exp), reads=[lsum], writes=[lsum])
        for l in range(DEPTH):
            lam_init = 0.8 - 0.6 * math.exp(-0.3 * l)
            S.op("dve", lambda e, l=l, li=lam_init: e.scalar_tensor_tensor(
                out=lam1[0:1, l:l + 1], in0=lsum[0:1, 2 * l + 1:2 * l + 2], scalar=-li, in1=lsum[0:1, 2 * l:2 * l + 1],
                op0=ALU.add, op1=ALU.subtract), reads=[lsum], writes=[lam1])
        S.op("pe", lambda e: e.matmul(PS[7][0:128, 0:DEPTH], lhsT=con[0:1, 128:256], rhs=lam1[0:1, 0:DEPTH], start=True, stop=True),
             reads=[con, lam1], writes=[PS[7]])
        S.op("dve", lambda e: e.tensor_copy(out=lamv[:, 0:DEPTH], in_=PS[7][0:128, 0:DEPTH]), reads=[PS[7]], writes=[lamv])

        idx_f = sb([128, NS * NPAGES], F32, name="idx_f")
        idx_l = [sb([128, NS * NPAGES], I32, name="idxl%d" % l) for l in range(DEPTH)]
        S.dma("sp", idx_t[:, :], ptab[0:1, :].partition_broadcast(128), writes=[idx_t])
        S.op("dve", lambda e: e.tensor_copy(out=idx_f[:, :], in_=idx_t[:, :]), reads=[idx_t], writes=[idx_f])
        for l in range(DEPTH):
            S.op("dve", lambda e, l=l: e.tensor_scalar(
                out=idx_f[:, :] if False else idx_l[l][:, :], in0=idx_f[:, :], scalar1=128.0, scalar2=con[:, C_MISC * 128 + l:C_MISC * 128 + l + 1],
                op0=ALU.mult, op1=ALU.add), reads=[idx_f, con], writes=[idx_l[l]])

        AX = mybir.AxisListType.X
        wrot = [0]

        WD = {"w_in": w_in, "w_out": w_out, "w_cq": w_cq, "w_ck": w_ck, "w_cv": w_cv, "w_co": w_co,
              "w_gate": w_gate, "w_up": w_up, "w_down": w_down}
        wreq = []
        wst = {"pos": 0, "issued": {}, "l": 0}

        def w_issue(i):
            key, k0, nk, c0, ncol, _ = wreq[i]
            src2d = WD[key][wst["l"]]
            sl = wsl[wrot[0] % NSLOT]
            wrot[0] += 1
            S.dma("pool", sl[:, 0:nk, 0:ncol],
                  src2d[k0 * 128:(k0 + nk) * 128, c0:c0 + ncol].rearrange("(kc p) n -> p kc n", p=128),
                  writes=[sl])
            wst["issued"][i] = sl

        def wload(key, k0, nk, c0, ncol, nopf=False):
            if S.dry:
                wreq.append((key, k0, nk, c0, ncol, nopf))
                return wsl[0]
            i = wst["pos"]
            wst["pos"] += 1
            assert wreq[i][:5] == (key, k0, nk, c0, ncol), (wreq[i], key, k0, nk, c0, ncol)
            if i not in wst["issued"]:
                w_issue(i)
            sl = wst["issued"].pop(i)
            if not nopf and i + 1 < len(wreq):
                w_issue(i + 1)
            return sl

        def blocks_of(T):
            b = []
            t = 0
            while t < T:
                n = min(512, T - t)
                b.append((t, n))
                t += n
            return b

        def barrier():
            if S.dry:
                return
            deps = [(S.sem[e], S.cnt[e]) for e in S.ENG if S.cnt[e] > 0]
            deps += [d for d in S.dlast if d is not None]
            for e in S.ENG:
                waits = S._waits(e, deps)
                eo = S.eng[e]

                def th(waits=waits, eo=eo):
                    for s_, v_ in waits:
                        eo.wait_ge(s_, v_)
                th()

        scr = [sb([128, 512], F32, name="scr%d" % i) for i in range(6)]
        scrb = [sb([128, 512], BF16, name="scrb%d" % i) for i in range(4)]
        srot = [0]
        sbrot = [0]

        def nscr():
            srot[0] += 1
            return scr[srot[0] % len(scr)]

        def nscrb():
            sbrot[0] += 1
            return scrb[sbrot[0] % len(scrb)]
        psrot = [0]

        def nps(lo=0, hi=8):
            psrot[0] += 1
            return PS[lo + psrot[0] % (hi - lo)]

        def ACT(fn, reads, writes):
            S.op("act", fn, reads=reads, writes=writes)

        def DVE(fn, reads, writes):
            S.op("dve", fn, reads=reads, writes=writes)

        def POOL(fn, reads, writes):
            S.op("pool", fn, reads=reads, writes=writes)

        def PE(fn, reads, writes):
            S.op("pe", fn, reads=reads, writes=writes)

        def stats(srcs, n, ones_blk, psb):
            m = len(srcs)
            for i, (a, reg) in enumerate(srcs):
                sq = nscrb()
                ACT(lambda e, sq=sq, a=a: e.activation(out=sq[:, 0:n], in_=a, func=AF.Square), [reg], [sq])
                PE(lambda e, sq=sq, i=i: e.matmul(psb[:, 0:n], lhsT=cb(ones_blk), rhs=sq[:, 0:n], start=(i == 0), stop=(i == m - 1)),
                   [sq, conb], [psb])
            ta = nscr()
            tr = nscr()
            ACT(lambda e: e.activation(out=ta[:, 0:n], in_=psb[:, 0:n], func=AF.Sqrt, bias=con[:, C_MISC * 128 + 8:C_MISC * 128 + 9], scale=1.0),
                [psb, con], [ta])
            DVE(lambda e: e.reciprocal(out=tr[:, 0:n], in_=ta[:, 0:n]), [ta], [tr])
            return tr

        def rmsnorm_x(src, l, gcol, T, dst, nreg=True):
            for (t0, n) in blocks_of(T):
                tr = stats([(src[:, kc, t0:t0 + n], (src, kc)) for kc in range(KC)], n, C_O1024, nps(4, 6))
                for kc in range(KC):
                    DVE(lambda e, kc=kc, t0=t0, n=n, tr=tr: e.scalar_tensor_tensor(
                        out=dst[:, kc, t0:t0 + n], in0=src[:, kc, t0:t0 + n], scalar=pcol(l, gcol + kc),
                        in1=tr[:, 0:n], op0=ALU.mult, op1=ALU.mult),
                        [(src, kc), tr, par], [(dst, kc)])

        pjrot = [0]

        def proj_fm(slots, ccs, rhs_fn, rhs_regs, blks, handler, after=None, delay=2):
            nktot = sum(nk for _, nk in slots)
            pend = []

            def run(item):
                cc, t0, n, ps, last = item
                handler(cc, t0, n, ps)
                if last and after is not None:
                    after(cc)
            for cc in ccs:
                for bi_, (t0, n) in enumerate(blks):
                    pjrot[0] += 1
                    ps = PS[pjrot[0] % 4]

                    def mm(e, ps=ps, cc=cc, t0=t0, n=n):
                        ins = None
                        kk = 0
                        for sl, nk in slots:
                            for k in range(nk):
                                ins = e.matmul(ps[:, 0:n], lhsT=sl[:, k, cc * 128:(cc + 1) * 128], rhs=rhs_fn(kk, t0, n),
                                               start=(kk == 0), stop=(kk == nktot - 1))
                                kk += 1
                        return ins
                    PE(mm, [sl for sl, _ in slots] + rhs_regs, [ps])
                    pend.append((cc, t0, n, ps, bi_ == len(blks) - 1))
                    if len(pend) > delay:
                        run(pend.pop(0))
            while pend:
                run(pend.pop(0))

        BE = 4 * TM
        arena = nc.alloc_sbuf_tensor("arena", [128, 7 * BE], BF16)

        def bview(o):
            return arena[:, o:o + BE].rearrange("p (h t) -> p h t", h=4)
        B0 = Tl(bview(0), 4)
        B1 = Tl(bview(BE), 4)
        B2 = Tl(bview(2 * BE), 4)
        B3 = Tl(bview(3 * BE), 4)
        OF = Tl(arena[:, 4 * BE:6 * BE].bitcast(F32).rearrange("p (h t) -> p h t", h=4), 4)
        OD = Tl(bview(6 * BE), 4)
        OG = B0
        actT = Al(arena[:, 0:FC * TM].rearrange("p (f t) -> p f t", f=FC), [B0, B1, B2, B3, OF, OD])
        cst = Al(arena[:, 3 * BE:4 * BE].bitcast(F32).rearrange("p (c t) -> p c t", c=2), [B3])
        vloc = Al(arena[:, 2 * BE:2 * BE + 2048].rearrange("p (a f) -> p a f", a=4), [B2])
        memx = Al(arena[:, 0:4096].bitcast(F32).rearrange("p (k n) -> p k n", k=8), [B0, B1])
        mnT = Al(arena[:, 4 * BE:4 * BE + 2048].rearrange("p (k n) -> p k n", k=8), [OF])
        mkT = Al(arena[:, 4 * BE + 2048:4 * BE + 3072].rearrange("p (k n) -> p k n", k=4), [OF])
        mv = Al(arena[:, 4 * BE + 3072:4 * BE + 4096].rearrange("p (k n) -> p k n", k=2), [OF])
        vs_tok = sb([128, NS, 512], BF16, name="vs_tok")
        mkTs = sb([128, 4, 256], BF16, name="mkTs")
        mvs = sb([128, 2, 512], BF16, name="mvs")
        raw = [sb([128, 3 + TSEG], F32, name="raw%d" % i) for i in range(1)]
        sraw = [sb([128, NS, 8], F32, name="sraw%d" % i) for i in range(1)]
        acc = [sb([128, TM], F32, name="acc%d" % i) for i in range(1)]
        betaT = sb([128, TT + 1, 4], F32, name="betaT")
        gT = sb([128, TT + 1, 4], F32, name="gT")
        betaS = sb([128, NS, 4], F32, name="betaS")
        gS = sb([128, NS, 4], F32, name="gS")
        Ssm = sb([128, 4, 128], F32, name="Ssm")
        Sbf = sb([128, 4, 128], BF16, name="Sbf")
        kpage = [sb([128, 512], BF16, name="kpage%d" % i) for i in range(2)]
        vpage = [sb([128, 512], BF16, name="vpage%d" % i) for i in range(2)]
        KTp = [sb([128, 4, 128], BF16, name="KTp%d" % i) for i in range(2)]
        gA = [sb([128, 4, 128], F32, name="gA%d" % i) for i in range(6)]
        gB = [sb([128, 4, 128], BF16, name="gB%d" % i) for i in range(10)]
        gsm = [sb([128, 16], F32, name="gsm%d" % i) for i in range(8)]
        Qpad = sb([128, 4, 8], BF16, name="Qpad")
        S.op("dve", lambda e: e.memset(Qpad[:, :, :], 0.0), reads=[], writes=[Qpad])

        ISQ = 128.0 ** -0.5

        def gdn_step(C, I, qa, ka, va, regs_in, beta_ap, g_ap, Sst, oa, o_regs, q3=None, S3=None, o3=None, beta2=None, g2=None):
            IC = I * C
            gsc = gsm[0]; negb = gsm[1]; gcs = gsm[2]; eg = gsm[3]; bg = gsm[4]; egl = gsm[5]; glt = gsm[6]
            GU, TMPD, Dm, N_, NT, TT = gA[0:6]
            P2, PT2 = GU, TMPD
            AQ, AQT, QG, KBG, KG, VB, NW, VN, DS, TTb = gB[0:10]

            def v3(t, rows=C, w=C):
                return t[0:rows, 0:I, 0:w]

            def pv(ps, rows=C, w=C):
                return ps[0:rows, 0:I * w].rearrange("p (i c) -> p i c", i=I)
            if g2 is not None:
                DVE(lambda e: e.tensor_copy(out=gsc[0:C, 0:I], in_=g2), regs_in, [gsc])
                DVE(lambda e: e.tensor_scalar(out=negb[0:C, 0:I], in0=beta2, scalar1=-1.0, scalar2=None, op0=ALU.mult), regs_in, [negb])
            else:
                for it in range(I):
                    DVE(lambda e, it=it: e.tensor_copy(out=gsc[0:C, it:it + 1], in_=g_ap(it)), regs_in, [gsc])
                    DVE(lambda e, it=it: e.tensor_scalar(out=negb[0:C, it:it + 1], in0=beta_ap(it), scalar1=-1.0, scalar2=None, op0=ALU.mult),
                        regs_in, [negb])
            p_gc, p_gcb, p_kk, p_qk = PS[7], PS[0], PS[1], PS[2]
            PE(lambda e: e.matmul(p_gc[0:C, 0:I], lhsT=cf(C_UTRI, C, C), rhs=gsc[0:C, 0:I], start=True, stop=True), [con, gsc], [p_gc])
            DVE(lambda e: e.tensor_copy(out=gcs[0:C, 0:I], in_=p_gc[0:C, 0:I]), [p_gc], [gcs])
            for it in range(I):
                DVE(lambda e, it=it: e.tensor_scalar(out=GU[0:C, it, 0:C], in0=cf(C_UTRI, C, C), scalar1=gsc[0:C, it:it + 1], scalar2=None, op0=ALU.mult),
                    [con, gsc], [GU])
            for it in range(I):
                PE(lambda e, it=it: e.matmul(p_gcb[:, it * C:(it + 1) * C], lhsT=cf(C_ONE, C, 128), rhs=GU[0:C, it, 0:C], start=True, stop=True),
                   [con, GU], [p_gcb])
            for it in range(I):
                DVE(lambda e, it=it: e.scalar_tensor_tensor(out=TMPD[0:C, it, 0:C], in0=p_gcb[0:C, it * C:(it + 1) * C], scalar=gcs[0:C, it:it + 1],
                                                            in1=cf(C_MPOS, C, C), op0=ALU.subtract, op1=ALU.add), [p_gcb, gcs, con], [TMPD])
            ACT(lambda e: e.activation(out=v3(Dm), in_=v3(TMPD), func=AF.Exp, scale=-1.0), [TMPD], [Dm])
            for it in range(I):
                POOL(lambda e, it=it: e.tensor_tensor(out=DS[0:C, it, 0:C], in0=Dm[0:C, it, 0:C], in1=cf(C_STRICT, C, C), op=ALU.mult), [Dm, con], [DS])
            for it in range(I):
                PE(lambda e, it=it: e.matmul(p_kk[0:C, it * C:(it + 1) * C], lhsT=ka(it), rhs=ka(it), start=True, stop=True), regs_in, [p_kk])
                PE(lambda e, it=it: e.matmul(p_qk[0:C, it * C:(it + 1) * C], lhsT=qa(it), rhs=ka(it), start=True, stop=True), regs_in, [p_qk])
            for it in range(I):
                DVE(lambda e, it=it: e.scalar_tensor_tensor(out=N_[0:C, it, 0:C], in0=p_kk[0:C, it * C:(it + 1) * C], scalar=negb[0:C, it:it + 1],
                                                            in1=DS[0:C, it, 0:C], op0=ALU.mult, op1=ALU.mult), [p_kk, negb, DS], [N_])
            DVE(lambda e: e.tensor_tensor(out=v3(AQ), in0=pv(p_qk), in1=v3(Dm), op=ALU.mult), [p_qk, Dm], [AQ])
            p_nt, p_aqt = PS[3], PS[4]
            for it in range(I):
                PE(lambda e, it=it: e.matmul(p_nt[0:C, it * C:(it + 1) * C], lhsT=N_[0:C, it, 0:C], rhs=cf(C_ID, C, C), start=True, stop=True), [N_, con], [p_nt])
                PE(lambda e, it=it: e.matmul(p_aqt[0:C, it * C:(it + 1) * C], lhsT=AQ[0:C, it, 0:C], rhs=cb(C_ID, C, C), start=True, stop=True), [AQ, conb], [p_aqt])
            ACT(lambda e: e.activation(out=v3(NT), in_=pv(p_nt), func=AF.Copy), [p_nt], [NT])
            ACT(lambda e: e.activation(out=v3(AQT), in_=pv(p_aqt), func=AF.Copy), [p_aqt], [AQT])
            for it in range(I):
                DVE(lambda e, it=it: e.tensor_tensor(out=TT[0:C, it, 0:C], in0=p_nt[0:C, it * C:(it + 1) * C], in1=cf(C_ID, C, C), op=ALU.add), [p_nt, con], [TT])
            nlev = int(round(math.log2(C)))
            Pk, Ptk = N_, NT
            Pn, Ptn = P2, PT2
            for k in range(nlev):
                if k >= 1:
                    p_d = PS[5]
                    for it in range(I):
                        PE(lambda e, it=it, Pk=Pk: e.matmul(p_d[0:C, it * C:(it + 1) * C], lhsT=Pk[0:C, it, 0:C], rhs=TT[0:C, it, 0:C], start=True, stop=True),
                           [Pk, TT], [p_d])
                    DVE(lambda e: e.tensor_tensor(out=v3(TT), in0=pv(p_d), in1=v3(TT), op=ALU.add), [p_d, TT], [TT])
                if k <= nlev - 2:
                    p_a, p_b = PS[6], PS[7]
                    for it in range(I):
                        PE(lambda e, it=it, Pk=Pk, Ptk=Ptk: e.matmul(p_a[0:C, it * C:(it + 1) * C], lhsT=Ptk[0:C, it, 0:C], rhs=Pk[0:C, it, 0:C], start=True, stop=True),
                           [Pk, Ptk], [p_a])
                        PE(lambda e, it=it, Pk=Pk, Ptk=Ptk: e.matmul(p_b[0:C, it * C:(it + 1) * C], lhsT=Pk[0:C, it, 0:C], rhs=Ptk[0:C, it, 0:C], start=True, stop=True),
                           [Pk, Ptk], [p_b])
                    ACT(lambda e, Pn=Pn: e.activation(out=v3(Pn), in_=pv(p_a), func=AF.Copy), [p_a], [Pn])
                    DVE(lambda e, Ptn=Ptn: e.tensor_copy(out=v3(Ptn), in_=pv(p_b)), [p_b], [Ptn])
                    Pk, Ptk, Pn, Ptn = Pn, Ptn, Pk, Ptk
                    if Pn is N_:
                        Pn, Ptn = N_, NT
            ACT(lambda e: e.activation(out=v3(TTb), in_=v3(TT), func=AF.Copy), [TT], [TTb])
            ACT(lambda e: e.activation(out=eg[0:C, 0:I], in_=gcs[0:C, 0:I], func=AF.Exp), [gcs], [eg])
            if beta2 is not None:
                DVE(lambda e: e.tensor_tensor(out=bg[0:C, 0:I], in0=eg[0:C, 0:I], in1=beta2, op=ALU.mult), [eg] + regs_in, [bg])
            else:
                for it in range(I):
                    DVE(lambda e, it=it: e.tensor_tensor(out=bg[0:C, it:it + 1], in0=eg[0:C, it:it + 1], in1=beta_ap(it), op=ALU.mult), [eg] + regs_in, [bg])
            DVE(lambda e: e.tensor_tensor(out=egl[0:C, 0:I], in0=pv(p_gcb)[:, :, C - 1], in1=gcs[0:C, 0:I], op=ALU.subtract), [p_gcb, gcs], [egl])
            ACT(lambda e: e.activation(out=glt[:, 0:I], in_=pv(p_gcb, 128)[:, :, C - 1], func=AF.Exp), [p_gcb], [glt])
            ACT(lambda e: e.activation(out=egl[0:C, 0:I], in_=egl[0:C, 0:I], func=AF.Exp), [egl], [egl])
            EGB = GU
            ACT(lambda e: e.activation(out=v3(EGB, 128), in_=pv(p_gcb, 128), func=AF.Exp), [p_gcb], [EGB])
            if q3 is not None:
                DVE(lambda e: e.tensor_tensor(out=v3(QG, 128), in0=q3, in1=v3(EGB, 128), op=ALU.mult), regs_in + [EGB], [QG])
            else:
                for it in range(I):
                    DVE(lambda e, it=it: e.tensor_tensor(out=QG[:, it, 0:C], in0=qa(it), in1=EGB[:, it, 0:C], op=ALU.mult), regs_in + [EGB], [QG])
            p_kt, p_vt = PS[1], PS[2]
            for it in range(I):
                PE(lambda e, it=it: e.matmul(p_kt[0:C, it * 128:(it + 1) * 128], lhsT=ka(it), rhs=cb(C_ID), start=True, stop=True), regs_in + [conb], [p_kt])
                PE(lambda e, it=it: e.matmul(p_vt[0:C, it * 128:(it + 1) * 128], lhsT=va(it), rhs=cb(C_ID), start=True, stop=True), regs_in + [conb], [p_vt])
            for it in range(I):
                DVE(lambda e, it=it: e.tensor_scalar(out=KBG[0:C, it, :], in0=p_kt[0:C, it * 128:(it + 1) * 128], scalar1=bg[0:C, it:it + 1], scalar2=None, op0=ALU.mult), [p_kt, bg], [KBG])
                DVE(lambda e, it=it: e.tensor_scalar(out=KG[0:C, it, :], in0=p_kt[0:C, it * 128:(it + 1) * 128], scalar1=egl[0:C, it:it + 1], scalar2=None, op0=ALU.mult), [p_kt, egl], [KG])
                DVE(lambda e, it=it: e.tensor_scalar(out=VB[0:C, it, :], in0=p_vt[0:C, it * 128:(it + 1) * 128], scalar1=beta_ap(it), scalar2=None, op0=ALU.mult), [p_vt] + regs_in, [VB])
            p_w = PS[3]
            for it in range(I):
                PE(lambda e, it=it: e.matmul(p_w[:, it * C:(it + 1) * C], lhsT=KBG[0:C, it, :], rhs=TTb[0:C, it, 0:C], start=True, stop=True), [KBG, TTb], [p_w])
            ACT(lambda e: e.activation(out=v3(NW, 128), in_=pv(p_w, 128), func=AF.Copy, scale=-1.0), [p_w], [NW])
            if S3 is not None:
                ACT(lambda e: e.activation(out=Sbf[:, 0:I, :], in_=S3, func=AF.Copy), o_regs, [Sbf])
            else:
                for it in range(I):
                    ACT(lambda e, it=it: e.activation(out=Sbf[:, it, :], in_=Sst(it), func=AF.Copy), o_regs, [Sbf])
            p_vn, p_o, p_s = PS[4], PS[5], PS[6]
            for it in range(I):
                def f(e, it=it):
                    e.matmul(p_vn[0:C, it * 128:(it + 1) * 128], lhsT=TTb[0:C, it, 0:C], rhs=VB[0:C, it, :], start=True, stop=False)
                    return e.matmul(p_vn[0:C, it * 128:(it + 1) * 128], lhsT=NW[:, it, 0:C], rhs=Sbf[:, it, :], start=False, stop=True)
                PE(f, [TTb, VB, NW, Sbf], [p_vn])
            ACT(lambda e: e.activation(out=VN[0:C, 0:I, :], in_=pv(p_vn, C, 128), func=AF.Copy), [p_vn], [VN])
            for it in range(I):
                def f2(e, it=it):
                    e.matmul(p_o[:, it * C:(it + 1) * C], lhsT=Sbf[:, it, :], rhs=QG[:, it, 0:C], start=True, stop=False)
                    return e.matmul(p_o[:, it * C:(it + 1) * C], lhsT=VN[0:C, it, :], rhs=AQT[0:C, it, 0:C], start=False, stop=True)
                PE(f2, [Sbf, QG, VN, AQT], [p_o])
                PE(lambda e, it=it: e.matmul(p_s[:, it * 128:(it + 1) * 128], lhsT=KG[0:C, it, :], rhs=VN[0:C, it, :], start=True, stop=True), [KG, VN], [p_s])
            if o3 is not None:
                ACT(lambda e: e.activation(out=o3, in_=pv(p_o, 128), func=AF.Copy), [p_o], o_regs[1:])
            for it in range(I):
                if o3 is None:
                    DVE(lambda e, it=it: e.tensor_copy(out=oa(it), in_=p_o[:, it * C:(it + 1) * C]), [p_o], o_regs[1:])
                DVE(lambda e, it=it: e.scalar_tensor_tensor(out=Sst(it), in0=Sst(it), scalar=glt[:, it:it + 1], in1=p_s[:, it * 128:(it + 1) * 128],
                                                            op0=ALU.mult, op1=ALU.add), [glt, p_s], o_regs[0:1])

        def attn_combine(Ops, Lps, ncols, dst_ap, dst_reg, lcol=None):
            r0 = nscr()
            DVE(lambda e: e.reciprocal(out=r0[:, 0:ncols], in_=Lps[0][:, 0:ncols]), [Lps[0]], [r0])
            t0_ = nscr()
            DVE(lambda e: e.tensor_tensor(out=t0_[:, 0:ncols], in0=Ops[0][:, 0:ncols], in1=r0[:, 0:ncols], op=ALU.mult), [Ops[0], r0], [t0_])
            if lcol is None:
                ACT(lambda e: e.activation(out=dst_ap, in_=t0_[:, 0:ncols], func=AF.Copy), [t0_], [dst_reg])
                return
            r1 = nscr()
            DVE(lambda e: e.reciprocal(out=r1[:, 0:ncols], in_=Lps[1][:, 0:ncols]), [Lps[1]], [r1])
            t1_ = nscr()
            DVE(lambda e: e.tensor_tensor(out=t1_[:, 0:ncols], in0=Ops[1][:, 0:ncols], in1=r1[:, 0:ncols], op=ALU.mult), [Ops[1], r1], [t1_])
            DVE(lambda e: e.scalar_tensor_tensor(out=dst_ap, in0=t1_[:, 0:ncols], scalar=lcol, in1=t0_[:, 0:ncols], op0=ALU.mult, op1=ALU.add),
                [t1_, t0_, lamv], [dst_reg])

        def headnorm(src, dst, l, gcol, T, extra=None, post_scale=1.0):
            for h in range(4):
                for (t0, n) in blocks_of(T):
                    tr = stats([(src[:, h, t0:t0 + n], (src, h))], n, C_O128, nps(4, 6))
                    if extra is None:
                        DVE(lambda e, h=h, t0=t0, n=n, tr=tr: e.scalar_tensor_tensor(out=dst[:, h, t0:t0 + n], in0=src[:, h, t0:t0 + n], scalar=pcol(l, gcol),
                                                                                    in1=tr[:, 0:n], op0=ALU.mult, op1=ALU.mult), [(src, h), tr, par], [(dst, h)])
                        if post_scale != 1.0:
                            DVE(lambda e, h=h, t0=t0, n=n: e.tensor_scalar(out=dst[:, h, t0:t0 + n], in0=dst[:, h, t0:t0 + n], scalar1=post_scale, scalar2=None, op0=ALU.mult),
                                [(dst, h)], [(dst, h)])
                    else:
                        tq = nscr()
                        DVE(lambda e, h=h, t0=t0, n=n, tr=tr, tq=tq: e.scalar_tensor_tensor(out=tq[:, 0:n], in0=src[:, h, t0:t0 + n], scalar=pcol(l, gcol),
                                                                                           in1=tr[:, 0:n], op0=ALU.mult, op1=ALU.mult), [(src, h), tr, par], [tq])
                        DVE(lambda e, h=h, t0=t0, n=n, tq=tq: e.tensor_tensor(out=dst[:, h, t0:t0 + n], in0=tq[:, 0:n], in1=extra[:, h, t0:t0 + n], op=ALU.mult),
                            [tq, (extra, h)], [(dst, h)])

        def layer(seg, l):
            has_s = (seg == NSEG - 1)
            T = TSEG + TS if has_s else TSEG
            TP = TSEG
            a0 = seg * TSEG
            blks = blocks_of(T)
            win = w_in[l]
            S.dma("sp", memx[:, :, :], memT_in.rearrange("(kc p) n -> p kc n", p=128), writes=[memx]) if False else None
            lam_init = 0.8 - 0.6 * math.exp(-0.3 * l)
            hreg = [hT]
            wst["pos"] = 0
            wst["issued"] = {}
            wst["l"] = l

            def hrhs(k, t0, n):
                return hT[:, k, t0:t0 + n]
            def chk(p):
                if DBG["phase"] < p:
                    raise _Stop()
            rmsnorm_x(xT, l, P_NMIX, T, hT)
            chk(2)
            S.dma("sp", cst[:, :, 0:TP], cs_in[:, :, a0:a0 + TP], writes=[cst])
            if has_s:
                for s in range(NS):
                    S.dma("sp", cst[:, :, TP + 4 * s:TP + 4 * s + 4], cs_in[:, :, SEQ:SEQ + SL], writes=[cst])
            if DBG["sub"] < 1:
                raise _Stop()
            kcur = B1 if has_s else kd_st[l]
            vcur = vloc if has_s else v_st[l]
            kdo = 0 if has_s else a0
            vto = 0 if has_s else seg * TT
            for (c0, dst, gcol, isk) in ((C_DQ, B0, P_QND, False), (C_DK, kcur, P_KND, True)):
                sl = wload("w_in", 0, 8, c0, 512)
                if DBG["sub"] < 2:
                    raise _Stop()

                def hqk(cc, t0, n, ps, dst=dst, gcol=gcol, isk=isk):
                    if DBG["sub"] < 3:
                        return
                    tr = stats([(ps[:, 0:n], ps)], n, C_BLK, nps(4, 6))
                    Y = nscr()
                    DVE(lambda e: e.scalar_tensor_tensor(out=Y[:, 0:n], in0=ps[:, 0:n], scalar=pcol(l, gcol), in1=tr[:, 0:n], op0=ALU.mult, op1=ALU.mult),
                        [ps, tr, par], [Y])
                    pr = nps(6, 8)
                    PE(lambda e: e.matmul(pr[:, 0:n], lhsT=cf(C_ROT), rhs=Y[:, 0:n], start=True, stop=True), [con, Y], [pr])
                    Z = nscr()
                    DVE(lambda e: e.tensor_tensor(out=Z[:, 0:n], in0=Y[:, 0:n], in1=cst[:, 0, t0:t0 + n], op=ALU.mult), [Y, cst], [Z])
                    Z2 = nscr()
                    DVE(lambda e: e.tensor_tensor(out=Z2[:, 0:n], in0=pr[:, 0:n], in1=cst[:, 1, t0:t0 + n], op=ALU.mult), [pr, cst], [Z2])
                    if not isk:
                        DVE(lambda e: e.tensor_tensor(out=dst[:, cc, t0:t0 + n], in0=Z[:, 0:n], in1=Z2[:, 0:n], op=ALU.add), [Z, Z2], [(dst, cc)])
                    else:
                        KF = nscr()
                        DVE(lambda e: e.tensor_tensor(out=KF[:, 0:n], in0=Z[:, 0:n], in1=Z2[:, 0:n], op=ALU.add), [Z, Z2], [KF])
                        ACT(lambda e: e.activation(out=dst[:, cc, kdo + t0:kdo + t0 + n], in_=KF[:, 0:n], func=AF.Copy), [KF], [(dst, cc)] if dst.n > 1 else [dst])
                        if t0 < TP:
                            S.dma("sp", pkT_o[l, cc * 128:(cc + 1) * 128, a0 + t0:a0 + t0 + n], KF[:, 0:n], reads=[KF], is_output=True)
                        else:
                            S.dma("sp", skT_o[l, cc * 128:(cc + 1) * 128, :], KF[:, 0:n], reads=[KF], is_output=True)
                proj_fm([(sl, 8)], range(4), hrhs, hreg, blks, hqk)
            if DBG["sub"] < 4:
                raise _Stop()
            sl = wload("w_in", 0, 8, C_DV, 512)
            for tt in range(TT):
                ps = nps(0, 4)

                def mmv(e, ps=ps, tt=tt):
                    ins = None
                    for k in range(8):
                        ins = e.matmul(ps[:, 0:512], lhsT=hT[:, k, tt * 128:(tt + 1) * 128], rhs=sl[:, k, 0:512], start=(k == 0), stop=(k == 7))
                    return ins
                PE(mmv, [sl, hT], [ps])
                VF = nscr()
                ACT(lambda e, ps=ps, VF=VF: e.activation(out=VF[:, :], in_=ps[:, :], func=AF.Copy), [ps], [VF])
                if DBG["sub"] >= 5:
                    DVE(lambda e, VF=VF, tt=tt: e.tensor_copy(out=vcur[:, vto + tt, :], in_=VF[:, :]), [VF], [vcur])
                if DBG["sub"] >= 6:
                    S.dma("sp", pv_o[l, a0 + tt * 128:a0 + (tt + 1) * 128, :], VF[:, :], reads=[VF], is_output=True)
            if has_s:
                for s in range(NS):
                    ps = nps(0, 4)

                    def mmvs(e, ps=ps, s=s):
                        ins = None
                        for k in range(8):
                            ins = e.matmul(ps[0:SL, 0:512], lhsT=hT[:, k, TP + 4 * s:TP + 4 * s + 4], rhs=sl[:, k, 0:512], start=(k == 0), stop=(k == 7))
                        return ins
                    PE(mmvs, [sl, hT], [ps])
                    VF = nscr()
                    ACT(lambda e, ps=ps, VF=VF: e.activation(out=VF[0:SL, :], in_=ps[0:SL, :], func=AF.Copy), [ps], [VF])
                    DVE(lambda e, VF=VF, s=s: e.tensor_copy(out=vs_tok[0:SL, s, :], in_=VF[0:SL, :]), [VF], [vs_tok])
                    S.dma("sp", sv_o[l, s, :, :], VF[0:SL, :], reads=[VF], is_output=True)
            chk(3)
            neglam = lamv[:, l:l + 1]
            for qb in range(1):
                q0t = seg * TT
                for h in range(4):
                    Ops = [PS[2], PS[3]]
                    Lps = [PS[4], PS[5]]
                    nkt = q0t + 4
                    steps = [(c, kt) for c in range(2) for kt in range(nkt)]

                    def emitS(i, h=h, nkt=nkt):
                        c, kt = steps[i]
                        off = max(0, kt - q0t) * 128
                        ncol = 512 - off
                        qc0 = qb * 512 + off
                        if kt < seg * TT or not has_s:
                            ksrc, ko, vsrc, vt = kd_st[l], kt * 128, v_st[l], kt
                        else:
                            ksrc, ko, vsrc, vt = B1, (kt - seg * TT) * 128, vloc, kt - seg * TT
                        sp_ = PS[i % 2]
                        PE(lambda e: e.matmul(sp_[:, 0:ncol], lhsT=ksrc[c * 64:(c + 1) * 64, h, ko:ko + 128], rhs=B0[c * 64:(c + 1) * 64, h, qc0:qc0 + ncol], start=True, stop=True),
                           [ksrc, (B0, h)], [sp_])
                        PT = scrb[i % 4]
                        ACT(lambda e: e.activation(out=PT[:, 0:ncol], in_=sp_[:, 0:ncol], func=AF.Exp, scale=0.125), [sp_], [PT])
                        if kt >= q0t:
                            POOL(lambda e: e.tensor_tensor(out=PT[:, 0:128], in0=PT[:, 0:128], in1=cb(C_TRI), op=ALU.mult), [PT, conb], [PT])
                        return (c, kt, off, ncol, PT, vsrc, vt)

                    def emitPV(info, h=h, nkt=nkt):
                        c, kt, off, ncol, PT, vsrc, vt = info
                        PE(lambda e: e.matmul(Ops[c][:, off:off + ncol], lhsT=vsrc[:, vt, h * 128:(h + 1) * 128], rhs=PT[:, 0:ncol], start=(kt == 0), stop=(kt == nkt - 1)),
                           [vsrc, PT], [Ops[c]])
                        PE(lambda e: e.matmul(Lps[c][:, off:off + ncol], lhsT=cb(C_ONE), rhs=PT[:, 0:ncol], start=(kt == 0), stop=(kt == nkt - 1)),
                           [conb, PT], [Lps[c]])
                    prev = emitS(0)
                    for i in range(1, len(steps)):
                        cur = emitS(i)
                        emitPV(prev)
                        prev = cur
                    emitPV(prev)
                    attn_combine(Ops, Lps, 512, OF[:, h, qb * 512:(qb + 1) * 512], (OF, h), lcol=neglam)
            chk(4)
            if has_s:
                for s in range(NS):
                    Os, Ls = PS[6], PS[7]
                    sc0 = TP + 4 * s

                    def gatherK(j):
                        ic = s * NPAGES + j
                        S.dma("pool", kpage[j % 2][:, :], cache_k[:, :], writes=[kpage[j % 2]], reads=[idx_l[l]], indirect=idx_l[l][:, ic:ic + 1])

                    def gatherV(j):
                        ic = s * NPAGES + j
                        S.dma("pool", vpage[j % 2][:, :], cache_v[:, :], writes=[vpage[j % 2]], reads=[idx_l[l]], indirect=idx_l[l][:, ic:ic + 1])

                    def stageT(j):
                        kp = kpage[j % 2]
                        pk_ = PS[2 + j % 2]
                        for h in range(4):
                            PE(lambda e, h=h: e.matmul(pk_[:, h * 128:(h + 1) * 128], lhsT=kp[:, h * 128:(h + 1) * 128], rhs=cb(C_ID), start=True, stop=True),
                               [kp, conb], [pk_])
                        KT = KTp[j % 2]
                        DVE(lambda e: e.tensor_copy(out=KT[:, :, :], in_=pk_[:, :].rearrange("p (h t) -> p h t", h=4)), [pk_], [KT])

                    PTs = {}
                    DVE(lambda e: e.tensor_copy(out=Qpad[0:64, :, 0:4], in_=B0[0:64, :, sc0:sc0 + 4]), [B0], [Qpad])
                    DVE(lambda e: e.tensor_copy(out=Qpad[64:128, :, 4:8], in_=B0[64:128, :, sc0:sc0 + 4]), [B0], [Qpad])

                    def stageS(j):
                        sp_ = PS[j % 2]
                        nk_ = 128 if j < NPAGES else SL
                        for h in range(4):
                            if j < NPAGES:
                                KT = KTp[j % 2]
                                PE(lambda e, h=h: e.matmul(sp_[:, h * 8:(h + 1) * 8], lhsT=KT[:, h, :], rhs=Qpad[:, h, :], start=True, stop=True), [KT, Qpad], [sp_])
                            else:
                                PE(lambda e, h=h: e.matmul(sp_[0:SL, h * 8:(h + 1) * 8], lhsT=B1[:, h, sc0:sc0 + 4], rhs=Qpad[:, h, :], start=True, stop=True),
                                   [(B1, h), Qpad], [sp_])
                        PT = scrb[j % 4]
                        PTs[j] = PT
                        ACT(lambda e: e.activation(out=PT[0:nk_, 0:32], in_=sp_[0:nk_, 0:32], func=AF.Exp, scale=0.125), [sp_], [PT])
                        if j == NPAGES:
                            DVE(lambda e: e.tensor_tensor(out=PT[0:SL, 0:32], in0=PT[0:SL, 0:32], in1=smask[0:SL, 0:32], op=ALU.mult), [PT, smask], [PT])

                    def stagePV(j):
                        PT = PTs.pop(j)
                        nk_ = 128 if j < NPAGES else SL
                        vp = vpage[j % 2]
                        for h in range(4):
                            if j < NPAGES:
                                PE(lambda e, h=h: e.matmul(Os[:, h * 8:(h + 1) * 8], lhsT=vp[:, h * 128:(h + 1) * 128], rhs=PT[:, h * 8:(h + 1) * 8],
                                                           start=(j == 0 and h == 0), stop=False, skip_group_check=True), [vp, PT], [Os])
                            else:
                                PE(lambda e, h=h: e.matmul(Os[:, h * 8:(h + 1) * 8], lhsT=vs_tok[0:SL, s, h * 128:(h + 1) * 128], rhs=PT[0:SL, h * 8:(h + 1) * 8],
                                                           start=False, stop=True, skip_group_check=True), [vs_tok, PT], [Os])
                        PE(lambda e: e.matmul(Ls[:, 0:32], lhsT=cb(C_ONE, nk_, 128), rhs=PT[0:nk_, 0:32], start=(j == 0), stop=(j == NPAGES)),
                           [conb, PT], [Ls])

                    gatherK(0)
                    gatherK(1)
                    gatherV(0)
                    stageT(0)
                    for j in range(0, NPAGES + 2):
                        if j + 1 < NPAGES:
                            stageT(j + 1)
                        if j <= NPAGES:
                            stageS(j)
                        if 0 <= j - 1 <= NPAGES:
                            stagePV(j - 1)
                        if j + 2 < NPAGES:
                            gatherK(j + 2)
                        if j + 1 < NPAGES:
                            gatherV(j + 1)
                    r = nscr()
                    DVE(lambda e, r=r: e.reciprocal(out=r[:, 0:32], in_=Ls[:, 0:32]), [Ls], [r])
                    t_ = nscr()
                    DVE(lambda e, r=r, t_=t_: e.tensor_tensor(out=t_[:, 0:32], in0=Os[:, 0:32], in1=r[:, 0:32], op=ALU.mult), [Os, r], [t_])
                    for h in range(4):
                        DVE(lambda e, h=h, t_=t_: e.scalar_tensor_tensor(out=OF[:, h, sc0:sc0 + 4], in0=t_[:, (2 * h + 1) * 4:(2 * h + 2) * 4], scalar=neglam,
                                                                        in1=t_[:, 2 * h * 4:(2 * h + 1) * 4], op0=ALU.mult, op1=ALU.add), [t_, lamv], [(OF, h)])
            chk(5)
            headnorm(OF, OD, l, P_DIFFN, T, post_scale=(1.0 - lam_init))
            chk(6)
            for bi, (c0, dst) in enumerate(((C_Q, B0), (C_K, B1), (C_V, B2))):
                sl = wload("w_in", 0, 8, c0, 512)
                st = {}

                def hraw(cc, t0, n, ps, st=st, bi=bi):
                    rw = raw[0]
                    sr = sraw[0]
                    ch = bi * 4 + cc
                    if t0 == 0:
                        if seg == 0:
                            DVE(lambda e: e.memset(rw[:, 0:3], 0.0), [], [rw])
                        else:
                            DVE(lambda e: e.tensor_copy(out=rw[:, 0:3], in_=gtail[l][:, ch, :]), [gtail[l]], [rw])
                    if t0 < TP:
                        ACT(lambda e: e.activation(out=rw[:, 3 + t0:3 + t0 + n], in_=ps[:, 0:n], func=AF.Copy), [ps], [rw])
                    else:
                        S.dma("sp", sr[:, :, 0:3], st_gconvT[l, ch * 128:(ch + 1) * 128, :, :], writes=[sr])
                        ACT(lambda e: e.activation(out=sr[:, :, 3:7], in_=ps[:, 0:TS].rearrange("p (s j) -> p s j", s=NS), func=AF.Copy), [ps], [sr])

                def aft(cc, bi=bi, dst=dst):
                    ch = bi * 4 + cc
                    rw = raw[0]
                    sr = sraw[0]
                    ac = acc[0]
                    wc = P_CQKV + ch * 4
                    ACT(lambda e: e.activation(out=ac[:, 0:TP], in_=rw[:, 0:TP], func=AF.Copy, scale=pcol(l, wc)), [rw, par], [ac])
                    for j in range(1, 4):
                        DVE(lambda e, j=j: e.scalar_tensor_tensor(out=ac[:, 0:TP], in0=rw[:, j:j + TP], scalar=pcol(l, wc + j), in1=ac[:, 0:TP], op0=ALU.mult, op1=ALU.add),
                            [rw, par, ac], [ac])
                    if has_s:
                        av = ac[:, TP:TP + TS].rearrange("p (s j) -> p s j", s=NS)
                        ACT(lambda e: e.activation(out=av, in_=sr[:, :, 0:4], func=AF.Copy, scale=pcol(l, wc)), [sr, par], [ac])
                        for j in range(1, 4):
                            DVE(lambda e, j=j: e.scalar_tensor_tensor(out=av, in0=sr[:, :, j:j + 4], scalar=pcol(l, wc + j), in1=av, op0=ALU.mult, op1=ALU.add),
                                [sr, par, ac], [ac])
                        S.dma("sp", sgcT_o[l, ch * 128:(ch + 1) * 128, :, :], sr[:, :, 4:7], reads=[sr], is_output=True)
                        S.dma("sp", pgcT_o[l, ch * 128:(ch + 1) * 128, :], rw[:, TP:TP + 3], reads=[rw], is_output=True)
                    else:
                        DVE(lambda e: e.tensor_copy(out=gtail[l][:, ch, :], in_=rw[:, TP:TP + 3]), [rw], [gtail[l]])
                    if bi == 2:
                        ACT(lambda e: e.activation(out=dst[:, cc, 0:T], in_=ac[:, 0:T], func=AF.Silu), [ac], [(dst, cc)])
                    else:
                        ACT(lambda e: e.activation(out=ac[:, 0:T], in_=ac[:, 0:T], func=AF.Silu), [ac], [ac])
                        for (t0, n) in blks:
                            tr = stats([(ac[:, t0:t0 + n], ac)], n, C_ONE, nps(4, 6))
                            DVE(lambda e, t0=t0, n=n, tr=tr: e.scalar_tensor_tensor(out=dst[:, cc, t0:t0 + n], in0=ac[:, t0:t0 + n], scalar=(ISQ if bi == 0 else 1.0),
                                                                                  in1=tr[:, 0:n], op0=ALU.mult, op1=ALU.mult), [ac, tr], [(dst, cc)])
                proj_fm([(sl, 8)], range(4), hrhs, hreg, blks, hraw, after=aft)
            sl = wload("w_in", 0, 8, C_G, 512)
            proj_fm([(sl, 8)], range(4), hrhs, hreg, blks,
                    lambda cc, t0, n, ps: ACT(lambda e: e.activation(out=B3[:, cc, t0:t0 + n], in_=ps[:, 0:n], func=AF.Silu), [ps], [(B3, cc)]))
            sl = wload("w_in", 0, 8, C_BA, 8)

            def ba_post(ps, m, bdst, gdst):
                ACT(lambda e: e.activation(out=bdst, in_=ps[0:m, 0:4], func=AF.Sigmoid), [ps], [betaT, betaS])
                xs, tt_, ee, ln_ = gsm[0], gsm[1], gsm[2], gsm[3]
                DVE(lambda e: e.tensor_tensor(out=xs[0:m, 0:4], in0=ps[0:m, 4:8], in1=pcol(l, P_DTB, 4)[0:m, :], op=ALU.add), [ps, par], [xs])
                ACT(lambda e: e.activation(out=tt_[0:m, 0:4], in_=xs[0:m, 0:4], func=AF.Abs), [xs], [tt_])
                ACT(lambda e: e.activation(out=ee[0:m, 0:4], in_=tt_[0:m, 0:4], func=AF.Exp, scale=-1.0), [tt_], [ee])
                ACT(lambda e: e.activation(out=ln_[0:m, 0:4], in_=ee[0:m, 0:4], func=AF.Ln, bias=1.0), [ee], [ln_])
                DVE(lambda e: e.scalar_tensor_tensor(out=xs[0:m, 0:4], in0=xs[0:m, 0:4], scalar=0.0, in1=ln_[0:m, 0:4], op0=ALU.max, op1=ALU.add), [xs, ln_], [xs])
                DVE(lambda e: e.tensor_tensor(out=gdst, in0=xs[0:m, 0:4], in1=negA[0:m, 4 * l:4 * l + 4], op=ALU.mult), [xs, negA], [gT, gS])
            for tt in range(TT):
                ps = nps(0, 4)

                def mmb(e, ps=ps, tt=tt):
                    ins = None
                    for k in range(8):
                        ins = e.matmul(ps[:, 0:8], lhsT=hT[:, k, tt * 128:(tt + 1) * 128], rhs=sl[:, k, 0:8], start=(k == 0), stop=(k == 7))
                    return ins
                PE(mmb, [sl, hT], [ps])
                ba_post(ps, 128, betaT[:, tt, :], gT[:, tt, :])
            if has_s:
                for s in range(NS):
                    ps = nps(0, 4)

                    def mmbs(e, ps=ps, s=s):
                        ins = None
                        for k in range(8):
                            ins = e.matmul(ps[0:SL, 0:8], lhsT=hT[:, k, TP + 4 * s:TP + 4 * s + 4], rhs=sl[:, k, 0:8], start=(k == 0), stop=(k == 7))
                        return ins
                    PE(mmbs, [sl, hT], [ps])
                    ba_post(ps, SL, betaS[0:SL, s, :], gS[0:SL, s, :])
            chk(7)
            if seg == 0:
                DVE(lambda e: e.memset(S32[l][:, :, :], 0.0), [], [S32[l]])
            for ci in range(TT):
                cs_ = slice(ci * 128, (ci + 1) * 128)
                gdn_step(128, 4, lambda it: B0[:, it, cs_], lambda it: B1[:, it, cs_], lambda it: B2[:, it, cs_], [B0, B1, B2, betaT, gT],
                         lambda it: betaT[:, ci, it:it + 1], lambda it: gT[:, ci, it:it + 1],
                         lambda it: S32[l][:, it, :], lambda it: OF[:, it, cs_], [S32[l], OF],
                         q3=B0[:, 0:4, cs_], S3=S32[l][:, 0:4, :], o3=OF[:, 0:4, cs_], beta2=betaT[:, ci, 0:4], g2=gT[:, ci, 0:4])
            if has_s:
                for h in range(4):
                    S.dma("sp", pg_o[l, h, :, :], S32[l][:, h, :], reads=[S32[l]], is_output=True)
                for h in range(4):
                    for s in range(NS):
                        S.dma("sp", Ssm[:, s, :], st_gdn[l, s, h, :, :], writes=[Ssm])
                    gdn_step(SL, NS, lambda it, h=h: B0[:, h, TP + 4 * it:TP + 4 * it + 4], lambda it, h=h: B1[:, h, TP + 4 * it:TP + 4 * it + 4],
                             lambda it, h=h: B2[:, h, TP + 4 * it:TP + 4 * it + 4], [B0, B1, B2, betaS, gS],
                             lambda it, h=h: betaS[0:SL, it, h:h + 1], lambda it, h=h: gS[0:SL, it, h:h + 1],
                             lambda it: Ssm[:, it, :], lambda it, h=h: OF[:, h, TP + 4 * it:TP + 4 * it + 4], [Ssm, OF])
                    for s in range(NS):
                        S.dma("sp", sg_o[l, s, h, :, :], Ssm[:, s, :], reads=[Ssm], is_output=True)
            headnorm(OF, OG, l, P_GDNN, T, extra=B3)
            chk(8)
            for cbk in range(2):
                sl = wload("w_out", 0, 8, cbk * 512, 512)

                def orhs(k, t0, n):
                    return OG[:, k, t0:t0 + n] if k < 4 else OD[:, k - 4, t0:t0 + n]

                def hres(cc, t0, n, ps, cbk=cbk):
                    kc = cbk * 4 + cc
                    DVE(lambda e: e.tensor_tensor(out=xT[:, kc, t0:t0 + n], in0=xT[:, kc, t0:t0 + n], in1=ps[:, 0:n], op=ALU.add), [(xT, kc), ps], [(xT, kc)])
                proj_fm([(sl, 8)], range(4), orhs, [OG, OD], blks, hres)
            chk(9)
            rmsnorm_x(xT, l, P_NCROSS, T, hT)
            sl = wload("w_cq", 0, 8, 0, 512)

            def hq(cc, t0, n, ps):
                tr = stats([(ps[:, 0:n], ps)], n, C_O128, nps(4, 6))
                DVE(lambda e: e.scalar_tensor_tensor(out=B2[:, cc, t0:t0 + n], in0=ps[:, 0:n], scalar=pcol(l, P_QNC), in1=tr[:, 0:n], op0=ALU.mult, op1=ALU.mult),
                    [ps, tr, par], [(B2, cc)])
            proj_fm([(sl, 8)], range(4), hrhs, hreg, blks, hq)
            S.dma("sp", memx[:, :, :], memT_in.rearrange("(kc p) n -> p kc n", p=128), writes=[memx])
            rmsnorm_x(memx, l, P_NMEM, 256, mnT)
            sl = wload("w_ck", 0, 8, 0, 512)

            def hmk(cc, t0, n, ps):
                tr = stats([(ps[:, 0:n], ps)], n, C_O128, nps(4, 6))
                KF = nscr()
                DVE(lambda e: e.scalar_tensor_tensor(out=KF[:, 0:n], in0=ps[:, 0:n], scalar=pcol(l, P_KNC), in1=tr[:, 0:n], op0=ALU.mult, op1=ALU.mult),
                    [ps, tr, par], [KF])
                ACT(lambda e: e.activation(out=mkT[:, cc, 0:n], in_=KF[:, 0:n], func=AF.Copy), [KF], [mkT])
                if seg == 0:
                    S.dma("sp", pmkT_o[l, cc * 128:(cc + 1) * 128, :], KF[:, 0:n], reads=[KF], is_output=True)
            proj_fm([(sl, 8)], range(4), lambda k, t0, n: mnT[:, k, t0:t0 + n], [mnT], [(0, 256)], hmk)
            sl = wload("w_cv", 0, 8, 0, 512)
            for mt in range(2):
                ps = nps(0, 4)

                def mmm(e, ps=ps, mt=mt):
                    ins = None
                    for k in range(8):
                        ins = e.matmul(ps[:, 0:512], lhsT=mnT[:, k, mt * 128:(mt + 1) * 128], rhs=sl[:, k, 0:512], start=(k == 0), stop=(k == 7))
                    return ins
                PE(mmm, [sl, mnT], [ps])
                VF = nscr()
                ACT(lambda e, ps=ps, VF=VF: e.activation(out=VF[:, :], in_=ps[:, :], func=AF.Copy), [ps], [VF])
                DVE(lambda e, VF=VF, mt=mt: e.tensor_copy(out=mv[:, mt, :], in_=VF[:, :]), [VF], [mv])
                if seg == 0:
                    S.dma("sp", pmv_o[l, mt * 128:(mt + 1) * 128, :], VF[:, :], reads=[VF], is_output=True)
            qlist = [(0, 512, mkT, mv, None)]
            if has_s:
                qlist += [(TP + 4 * s, 4, mkTs, mvs, s) for s in range(NS)]
            for (q0, nq, mk_, mv_, ss) in qlist:
                if ss is not None:
                    for h in range(4):
                        S.dma("pool", mkTs[:, h, :], cmkT[l, ss, h, :, :], writes=[mkTs])
                    S.dma("pool", mvs[:, :, :], cmv[l, ss].rearrange("(m p) f -> p m f", p=128), writes=[mvs])
                for h in range(4):
                    Ops = [PS[2]]
                    Lps = [PS[4]]
                    for mt in range(2):
                        sp_ = nps(0, 2)
                        PE(lambda e, sp_=sp_, mt=mt, h=h, mk_=mk_, q0=q0, nq=nq: e.matmul(sp_[:, 0:nq], lhsT=mk_[:, h, mt * 128:(mt + 1) * 128], rhs=B2[:, h, q0:q0 + nq], start=True, stop=True),
                           [mk_, (B2, h)], [sp_])
                        PT = nscrb()
                        ACT(lambda e, sp_=sp_, PT=PT, nq=nq: e.activation(out=PT[:, 0:nq], in_=sp_[:, 0:nq], func=AF.Exp, scale=ISQ), [sp_], [PT])
                        PE(lambda e, PT=PT, mt=mt, h=h, mv_=mv_, nq=nq: e.matmul(Ops[0][:, 0:nq], lhsT=mv_[:, mt, h * 128:(h + 1) * 128], rhs=PT[:, 0:nq], start=(mt == 0), stop=(mt == 1)),
                           [mv_, PT], [Ops[0]])
                        PE(lambda e, PT=PT, mt=mt, nq=nq: e.matmul(Lps[0][:, 0:nq], lhsT=cb(C_ONE), rhs=PT[:, 0:nq], start=(mt == 0), stop=(mt == 1)), [conb, PT], [Lps[0]])
                    attn_combine(Ops, Lps, nq, B3[:, h, q0:q0 + nq], (B3, h))
            for cbk in range(2):
                sl = wload("w_co", 0, 4, cbk * 512, 512)

                def hres2(cc, t0, n, ps, cbk=cbk):
                    kc = cbk * 4 + cc
                    DVE(lambda e: e.tensor_tensor(out=xT[:, kc, t0:t0 + n], in0=xT[:, kc, t0:t0 + n], in1=ps[:, 0:n], op=ALU.add), [(xT, kc), ps], [(xT, kc)])
                proj_fm([(sl, 4)], range(4), lambda k, t0, n: B3[:, k, t0:t0 + n], [B3], blks, hres2)
            chk(10)
            rmsnorm_x(xT, l, P_NFFN, T, hT)
            barrier()
            for fb in range(6):
                ncol = 512 if fb < 5 else 256
                ncc = ncol // 128
                slg = wload("w_gate", 0, 8, fb * 512, ncol)
                slu = wload("w_up", 0, 8, fb * 512, ncol)

                def hgr(cc, t0, n, ps, fb=fb):
                    fc = fb * 4 + cc
                    rw = raw[0]
                    sr = sraw[0]
                    if t0 == 0:
                        if seg == 0:
                            DVE(lambda e: e.memset(rw[:, 0:2], 0.0), [], [rw])
                        else:
                            DVE(lambda e: e.tensor_copy(out=rw[:, 0:2], in_=ftail[l][:, fc, :]), [ftail[l]], [rw])
                    if t0 < TP:
                        ACT(lambda e: e.activation(out=rw[:, 2 + t0:2 + t0 + n], in_=ps[:, 0:n], func=AF.Copy), [ps], [rw])
                    else:
                        S.dma("sp", sr[:, :, 0:2], st_fconvT[l, fc * 128:(fc + 1) * 128, :, :], writes=[sr])
                        ACT(lambda e: e.activation(out=sr[:, :, 2:6], in_=ps[:, 0:TS].rearrange("p (s j) -> p s j", s=NS), func=AF.Copy), [ps], [sr])

                def aftg(cc, fb=fb):
                    fc = fb * 4 + cc
                    rw = raw[0]
                    sr = sraw[0]
                    ac = acc[0]
                    wc = P_CFFN + fc * 3
                    ACT(lambda e: e.activation(out=ac[:, 0:TP], in_=rw[:, 0:TP], func=AF.Copy, scale=pcol(l, wc)), [rw, par], [ac])
                    for j in range(1, 3):
                        DVE(lambda e, j=j: e.scalar_tensor_tensor(out=ac[:, 0:TP], in0=rw[:, j:j + TP], scalar=pcol(l, wc + j), in1=ac[:, 0:TP], op0=ALU.mult, op1=ALU.add),
                            [rw, par, ac], [ac])
                    if has_s:
                        av = ac[:, TP:TP + TS].rearrange("p (s j) -> p s j", s=NS)
                        ACT(lambda e: e.activation(out=av, in_=sr[:, :, 0:4], func=AF.Copy, scale=pcol(l, wc)), [sr, par], [ac])
                        for j in range(1, 3):
                            DVE(lambda e, j=j: e.scalar_tensor_tensor(out=av, in0=sr[:, :, j:j + 4], scalar=pcol(l, wc + j), in1=av, op0=ALU.mult, op1=ALU.add),
                                [sr, par, ac], [ac])
                        S.dma("sp", sfcT_o[l, fc * 128:(fc + 1) * 128, :, :], sr[:, :, 4:6], reads=[sr], is_output=True)
                        S.dma("sp", pfcT_o[l, fc * 128:(fc + 1) * 128, :], rw[:, TP:TP + 2], reads=[rw], is_output=True)
                    else:
                        DVE(lambda e: e.tensor_copy(out=ftail[l][:, fc, :], in_=rw[:, TP:TP + 2]), [rw], [ftail[l]])
                    ACT(lambda e: e.activation(out=ac[:, 0:T], in_=ac[:, 0:T], func=AF.Silu), [ac], [ac])

                    def hup(cc2, t0, n, ps, fc=fc, ac=ac):
                        DVE(lambda e: e.tensor_tensor(out=actT[:, fc, t0:t0 + n], in0=ac[:, t0:t0 + n], in1=ps[:, 0:n], op=ALU.mult), [ac, ps], [(actT, fc)])
                    proj_fm([(slu, 8)], [cc], hrhs, hreg, blks, hup)
                proj_fm([(slg, 8)], range(ncc), hrhs, hreg, blks, hgr, after=aftg, delay=0)
            for cbk in range(2):
                sls = [(wload("w_down", 0, 8, cbk * 512, 512), 8), (wload("w_down", 8, 8, cbk * 512, 512), 8), (wload("w_down", 16, 6, cbk * 512, 512, nopf=True), 6)]

                def hres3(cc, t0, n, ps, cbk=cbk):
                    kc = cbk * 4 + cc
                    DVE(lambda e: e.tensor_tensor(out=xT[:, kc, t0:t0 + n], in0=xT[:, kc, t0:t0 + n], in1=ps[:, 0:n], op=ALU.add), [(xT, kc), ps], [(xT, kc)])
                proj_fm(sls, range(4), lambda k, t0, n: actT[:, k, t0:t0 + n], [actT], blks, hres3)
            barrier()


        S.dry = True
        try:
            layer(0, 0)
        except _Stop:
            pass
        S.dry = False
        for seg in range(NSEG):
            TP = TSEG
            a0 = seg * TSEG
            for kc in range(KC):
                S.dma("sp", xT[:, kc, 0:TP], xT_in[kc * 128:(kc + 1) * 128, a0:a0 + TP], writes=[(xT, kc)])
                if seg == NSEG - 1:
                    S.dma("sp", xT[:, kc, TP:TP + TS], xsT_in[kc * 128:(kc + 1) * 128, :], writes=[(xT, kc)])
            for l in range(DBG["layers"]):
                if stage >= 1 and seg < DBG["segs"]:
                    try:
                        layer(seg, l)
                    except _Stop:
                        pass
            for kc in range(KC):
                S.dma("sp", yT_o[kc * 128:(kc + 1) * 128, a0:a0 + TP], xT[:, kc, 0:TP], reads=[(xT, kc)], is_output=True)
                if seg == NSEG - 1:
                    S.dma("sp", ysT_o[kc * 128:(kc + 1) * 128, :], xT[:, kc, TP:TP + TS], reads=[(xT, kc)], is_output=True)

        S.finish()
    return nc, S


def _consts():
    con = np.zeros((128, 11, 128), np.float32)
    p = np.arange(128)
    con[:, 0] = np.eye(128)
    con[:, 1] = 1.0
    con[:, 2] = (p[:, None] <= p[None, :])
    con[:, 3] = np.where(p[:, None] < p[None, :], BIG, 0.0)
    con[:, 4] = (p[None, :] < p[:, None])
    con[:, 5] = (p[:, None] <= p[None, :])
    R = np.zeros((128, 128), np.float32)
    for q in range(128):
        if q % 64 < 32:
            R[q + 32, q] = -1.0
        else:
            R[q - 32, q] = 1.0
    con[:, 6] = R
    con[:, 7] = (p[:, None] // 64 == p[None, :] // 64) / 64.0
    con[:, 8, 0] = p
    con[:, 8, 1] = p + NPOOL * 128
    con[:, 8, 8] = EPS
    con[:, 9] = 1.0 / 128
    con[:, 10] = 1.0 / 1024
    half = 32
    inv = 10000.0 ** (-np.arange(half, dtype=np.float32) / half)
    pos = np.concatenate([np.arange(SEQ), 8192 + np.arange(SL)]).astype(np.float32)
    ang = pos[None, :] * inv[p % 32][:, None]
    cs = np.stack([np.cos(ang), np.sin(ang)], axis=1).astype(np.float32)
    sm = np.zeros((128, 32), np.float32)
    for j in range(4):
        for m in range(8):
            for q in range(4):
                sm[j, m * 4 + q] = 1.0 if j <= q else 0.0
    return con.reshape(128, 11 * 128), cs, sm


_CACHE = {}


def kernel(**inp):
    f = lambda a: np.ascontiguousarray(np.asarray(a, dtype=np.float32))
    if "nc" not in _CACHE:
        _CACHE["nc"] = build_program()[0]
    nc = _CACHE["nc"]
    con, cs, sm = _consts()
    NPS = 161
    par = np.zeros((128, DEPTH, NPS), np.float32)
    for l in range(DEPTH):
        for o, nm in ((0, "norm_mix"), (8, "norm_cross"), (16, "norm_mem"), (24, "norm_ffn")):
            par[:, l, o:o + 8] = np.asarray(inp[nm][l]).reshape(8, 128).T
        par[:, l, 32:80] = np.asarray(inp["conv_qkv"][l]).reshape(4, 12, 128).transpose(2, 1, 0).reshape(128, 48)
        par[:, l, 80:146] = np.asarray(inp["conv_ffn"][l]).reshape(3, 22, 128).transpose(2, 1, 0).reshape(128, 66)
        par[:, l, 146:150] = np.asarray(inp["a_log"][l])[None, :]
        par[:, l, 150:154] = np.asarray(inp["dt_bias"][l])[None, :]
        par[:, l, 154] = np.asarray(inp["gdn_norm"][l])
        par[:, l, 155] = np.tile(np.asarray(inp["qnorm_diff"][l]), 2)
        par[:, l, 156] = np.tile(np.asarray(inp["knorm_diff"][l]), 2)
        par[:, l, 157] = np.asarray(inp["diff_norm"][l])
        par[:, l, 158] = np.asarray(inp["qnorm_cross"][l])
        par[:, l, 159] = np.asarray(inp["knorm_cross"][l])
    par = par.reshape(128, DEPTH * NPS)
    lam = np.stack([np.stack([np.asarray(inp[n][l]) for n in ("lam_q1", "lam_k1", "lam_q2", "lam_k2")]) for l in range(DEPTH)]).astype(np.float32).reshape(1, -1)
    ck = f(inp["cache_k"]).reshape(DEPTH * NPOOL * 128, 512)
    cv = f(inp["cache_v"]).reshape(DEPTH * NPOOL * 128, 512)
    if DBG["small_cache"]:
        ck, cv = ck[:128], cv[:128]
    shared = {k: f(inp[k]) for k in ("w_in", "w_out", "w_cq", "w_ck", "w_cv", "w_co", "w_gate", "w_up", "w_down")}
    xp, xs, mem = f(inp["x_prompt"]), f(inp["x_sample"]), f(inp["mem_prompt"])
    pt = np.asarray(inp["page_table"]).astype(np.int32)
    sg, sgc = f(inp["state_gdn"]), f(inp["state_gdn_conv"])
    cmk, cmv_, sfc = f(inp["cache_mem_k"]), f(inp["cache_mem_v"]), f(inp["state_ffn_conv"])
    in_maps = []
    for c in range(NCORES):
        sl = slice(NS * c, NS * c + NS)
        m = dict(shared)
        m.update({
            "xT_in": np.ascontiguousarray(xp[c].T), "xsT_in": np.ascontiguousarray(xs[sl].reshape(TS, D).T),
            "memT_in": np.ascontiguousarray(mem[c].T), "cache_k": ck, "cache_v": cv,
            "ptab": np.ascontiguousarray(pt[sl].reshape(1, NS * NPAGES)),
            "st_gdn": np.ascontiguousarray(sg[:, sl]),
            "st_gconvT": np.ascontiguousarray(sgc[:, sl].transpose(0, 3, 1, 2)),
            "cmkT": np.ascontiguousarray(cmk[:, sl].transpose(0, 1, 3, 4, 2)),
            "cmv": np.ascontiguousarray(cmv_[:, sl].reshape(DEPTH, NS, 256, 512)),
            "st_fconvT": np.ascontiguousarray(sfc[:, sl].transpose(0, 3, 1, 2)),
            "par_in": par, "lam_in": lam, "con_in": con, "cs_in": cs, "smask_in": sm,
        })
        in_maps.append(m)
    res = run_bass_kernel_spmd(nc, in_maps, core_ids=list(range(NCORES))).results
    B, SB = NCORES, NCORES * NS
    y_p = np.stack([res[c]["yT_o"].T for c in range(B)])
    y_s = np.concatenate([res[c]["ysT_o"].T.reshape(NS, SL, D) for c in range(B)])
    p_k = np.stack([np.stack([res[c]["pkT_o"][l].T.reshape(SEQ, 8, 64) for c in range(B)]) for l in range(DEPTH)])
    p_v = np.stack([np.stack([res[c]["pv_o"][l].reshape(SEQ, 4, 128) for c in range(B)]) for l in range(DEPTH)])
    p_g = np.stack([np.stack([res[c]["pg_o"][l] for c in range(B)]) for l in range(DEPTH)])
    p_gc = np.stack([np.stack([res[c]["pgcT_o"][l].T for c in range(B)]) for l in range(DEPTH)])
    p_mk = np.stack([np.stack([res[c]["pmkT_o"][l].T.reshape(256, 4, 128) for c in range(B)]) for l in range(DEPTH)])
    p_mv = np.stack([np.stack([res[c]["pmv_o"][l].reshape(256, 4, 128) for c in range(B)]) for l in range(DEPTH)])
    p_fc = np.stack([np.stack([res[c]["pfcT_o"][l].T for c in range(B)]) for l in range(DEPTH)])
    s_k = np.stack([np.concatenate([res[c]["skT_o"][l].T.reshape(NS, SL, 8, 64) for c in range(B)]) for l in range(DEPTH)])
    s_v = np.stack([np.concatenate([res[c]["sv_o"][l].reshape(NS, SL, 4, 128) for c in range(B)]) for l in range(DEPTH)])
    s_g = np.stack([np.concatenate([res[c]["sg_o"][l] for c in range(B)]) for l in range(DEPTH)])
    s_gc = np.stack([np.concatenate([res[c]["sgcT_o"][l].transpose(1, 2, 0) for c in range(B)]) for l in range(DEPTH)])
    s_fc = np.stack([np.concatenate([res[c]["sfcT_o"][l].transpose(1, 2, 0) for c in range(B)]) for l in range(DEPTH)])
    outs = (y_p, y_s, p_k, p_v, p_g, p_gc, p_mk, p_mv, p_fc, s_k, s_v, s_g, s_gc, s_fc)
    return tuple(np.ascontiguousarray(o, dtype=np.float32) for o in outs)
```
